# Optimizing a Trainium2 kernel written in Bass

```python
import jax, jax.numpy as jnp
from jax import lax
import numpy as np

D_MODEL = 1024
BATCH = 16
SEQ = 2048
DEPTH = 2
DEC_BATCH = 8
DEC_SEQ = 32
PAST_LEN = 2048

CHUNK = 64
D_CONV = 1024
CONV_W = 3
N_HEADS = 16
QK_NOPE = 64
QK_ROPE = 32
V_HEAD = 64
Q_LORA = 384
KV_LORA = 256
D_FF = 2816
ROPE_THETA = 10000.0
EPS = 1e-6
Q_BLOCK = 128
ATTN_SCALE = (QK_NOPE + QK_ROPE) ** -0.5
NEG_INF = -1e30
D_IN = 3 * D_CONV + Q_LORA + KV_LORA + QK_ROPE + 2 * D_MODEL

kernel_name = "hybrid_shortconv_mla_stream_step"


def rmsnorm(x, g):
    xf = x.astype(jnp.float32)
    y = xf * lax.rsqrt(jnp.mean(xf * xf, axis=-1, keepdims=True) + EPS)
    return (y * g.astype(jnp.float32)).astype(x.dtype)


def apply_rope(x, pos):
    half = QK_ROPE // 2
    inv = ROPE_THETA ** (-jnp.arange(half, dtype=jnp.float32) / half)
    ang = pos.astype(jnp.float32)[:, None] * inv[None, :]
    cos = jnp.cos(ang)[None, :, None, :]
    sin = jnp.sin(ang)[None, :, None, :]
    xf = x.astype(jnp.float32)
    x1, x2 = xf[..., :half], xf[..., half:]
    return jnp.concatenate([x1 * cos - x2 * sin, x1 * sin + x2 * cos], axis=-1).astype(x.dtype)


def attend_block(q_nope, q_rope, q_pos, k_nope, k_rope, v, k_pos):
    s = (jnp.einsum('bqhn,bkhn->bhqk', q_nope, k_nope)
         + jnp.einsum('bqhr,bkr->bhqk', q_rope, k_rope))
    s = s.astype(jnp.float32) * ATTN_SCALE
    mask = (k_pos // CHUNK)[None, :] <= (q_pos // CHUNK)[:, None]
    s = jnp.where(mask[None, None], s, jnp.float32(NEG_INF))
    p = jax.nn.softmax(s, axis=-1).astype(v.dtype)
    return jnp.einsum('bhqk,bkhv->bqhv', p, v)


def mixer(h, pos, conv_hist, ckv_past, krope_past, k_pos, w_in, norm_q, norm_kv,
          w_uq, w_ukv, conv_w, w_conv_out, w_attn_out, w_merge):
    bsz, s_len, _ = h.shape
    proj = h @ w_in
    o1, o2, o3 = D_CONV, 2 * D_CONV, 3 * D_CONV
    o4 = o3 + Q_LORA
    o5 = o4 + KV_LORA
    o6 = o5 + QK_ROPE
    o7 = o6 + D_MODEL
    b_g, c_g, xin, q_lat, ckv, k_r, g_a, g_b = jnp.split(proj, [o1, o2, o3, o4, o5, o6, o7], axis=-1)

    u = c_g * xin
    u_ext = jnp.concatenate([conv_hist, u], axis=1)
    conv = (conv_w[0] * u_ext[:, 0:s_len] + conv_w[1] * u_ext[:, 1:s_len + 1]
            + conv_w[2] * u_ext[:, 2:s_len + 2])
    y_a = (b_g * conv) @ w_conv_out
    new_conv = u_ext[:, -(CONV_W - 1):]

    q = (rmsnorm(q_lat, norm_q) @ w_uq).reshape(bsz, s_len, N_HEADS, QK_NOPE + QK_ROPE)
    q_nope, q_rope = q[..., :QK_NOPE], apply_rope(q[..., QK_NOPE:], pos)
    c_new = rmsnorm(ckv, norm_kv)
    kr_new = apply_rope(k_r[:, :, None, :], pos)[:, :, 0]
    if ckv_past is None:
        c_all, kr_all = c_new, kr_new
    else:
        c_all = jnp.concatenate([ckv_past, c_new], axis=1)
        kr_all = jnp.concatenate([krope_past, kr_new], axis=1)
    n_keys = c_all.shape[1]
    kv = (c_all @ w_ukv).reshape(bsz, n_keys, N_HEADS, QK_NOPE + V_HEAD)
    k_nope, v = kv[..., :QK_NOPE], kv[..., QK_NOPE:]
    if ckv_past is None:
        n_blk = s_len // Q_BLOCK
        qn_b = jnp.moveaxis(q_nope.reshape(bsz, n_blk, Q_BLOCK, N_HEADS, QK_NOPE), 1, 0)
        qr_b = jnp.moveaxis(q_rope.reshape(bsz, n_blk, Q_BLOCK, N_HEADS, QK_ROPE), 1, 0)
        pos_b = pos.reshape(n_blk, Q_BLOCK)
        out = lax.map(lambda a: attend_block(a[0], a[1], a[2], k_nope, kr_all, v, k_pos),
                      (qn_b, qr_b, pos_b))
        attn = jnp.moveaxis(out, 0, 1).reshape(bsz, s_len, N_HEADS * V_HEAD)
    else:
        attn = attend_block(q_nope, q_rope, pos, k_nope, kr_all, v, k_pos).reshape(bsz, s_len, N_HEADS * V_HEAD)
    y_b = attn @ w_attn_out

    mixed = (jax.nn.sigmoid(g_a) * y_a + jax.nn.sigmoid(g_b) * y_b) @ w_merge
    return mixed, new_conv, c_new, kr_new


def layer(x, pos, k_pos, conv_hist, ckv_past, krope_past, w_in, norm_attn_pre, norm_attn_post,
          norm_q, norm_kv, w_uq, w_ukv, conv_w, w_conv_out, w_attn_out, w_merge,
          norm_ffn_pre, norm_ffn_post, w_gate_up, w_down):
    m, new_conv, c_new, kr_new = mixer(rmsnorm(x, norm_attn_pre), pos, conv_hist, ckv_past, krope_past,
                                       k_pos, w_in, norm_q, norm_kv, w_uq, w_ukv, conv_w,
                                       w_conv_out, w_attn_out, w_merge)
    x = x + rmsnorm(m, norm_attn_post)
    gu = rmsnorm(x, norm_ffn_pre) @ w_gate_up
    f = (jax.nn.silu(gu[..., :D_FF]) * gu[..., D_FF:]) @ w_down
    x = x + rmsnorm(f, norm_ffn_post)
    return x, new_conv, c_new, kr_new


def setup_inputs(seed: int = 0) -> dict:
    key = jax.random.key(seed)
    ks = jax.random.split(key, 24)
    f32 = jnp.float32

    def nrm(k, shape, scale):
        return jax.random.normal(k, shape, f32) * scale

    def gain(k, n):
        return 1.0 + 0.05 * jax.random.normal(k, (DEPTH, n), f32)

    return {
        "x_prompt": nrm(ks[0], (BATCH, SEQ, D_MODEL), 1.0),
        "x_sample": nrm(ks[1], (DEC_BATCH, DEC_SEQ, D_MODEL), 1.0),
        "state_conv": nrm(ks[2], (DEPTH, DEC_BATCH, CONV_W - 1, D_CONV), 1.0),
        "cache_ckv": nrm(ks[3], (DEPTH, DEC_BATCH, PAST_LEN, KV_LORA), 1.0),
        "cache_krope": nrm(ks[4], (DEPTH, DEC_BATCH, PAST_LEN, QK_ROPE), 1.0),
        "w_in": nrm(ks[5], (DEPTH, D_MODEL, D_IN), D_MODEL ** -0.5),
        "norm_attn_pre": gain(ks[6], D_MODEL),
        "norm_attn_post": gain(ks[7], D_MODEL),
        "norm_q": gain(ks[8], Q_LORA),
        "norm_kv": gain(ks[9], KV_LORA),
        "w_uq": nrm(ks[10], (DEPTH, Q_LORA, N_HEADS * (QK_NOPE + QK_ROPE)), Q_LORA ** -0.5),
        "w_ukv": nrm(ks[11], (DEPTH, KV_LORA, N_HEADS * (QK_NOPE + V_HEAD)), KV_LORA ** -0.5),
        "conv_w": nrm(ks[12], (DEPTH, CONV_W, D_CONV), CONV_W ** -0.5),
        "w_conv_out": nrm(ks[13], (DEPTH, D_CONV, D_MODEL), D_CONV ** -0.5),
        "w_attn_out": nrm(ks[14], (DEPTH, N_HEADS * V_HEAD, D_MODEL), (N_HEADS * V_HEAD) ** -0.5),
        "w_merge": nrm(ks[15], (DEPTH, D_MODEL, D_MODEL), D_MODEL ** -0.5),
        "norm_ffn_pre": gain(ks[16], D_MODEL),
        "norm_ffn_post": gain(ks[17], D_MODEL),
        "w_gate_up": nrm(ks[18], (DEPTH, D_MODEL, 2 * D_FF), D_MODEL ** -0.5),
        "w_down": nrm(ks[19], (DEPTH, D_FF, D_MODEL), D_FF ** -0.5),
    }


def reference(x_prompt, x_sample, state_conv, cache_ckv, cache_krope, w_in, norm_attn_pre,
              norm_attn_post, norm_q, norm_kv, w_uq, w_ukv, conv_w, w_conv_out, w_attn_out,
              w_merge, norm_ffn_pre, norm_ffn_post, w_gate_up, w_down):
    s_p = x_prompt.shape[1]
    s_s = x_sample.shape[1]
    past = cache_ckv.shape[2]
    pos_p = jnp.arange(s_p, dtype=jnp.int32)
    pos_s = past + jnp.arange(s_s, dtype=jnp.int32)
    kpos_s = jnp.arange(past + s_s, dtype=jnp.int32)
    hist_p = jnp.zeros((x_prompt.shape[0], CONV_W - 1, D_CONV), x_prompt.dtype)

    xp, xs = x_prompt, x_sample
    conv_p, ckv_p, kr_p, conv_s, ckv_s, kr_s = [], [], [], [], [], []
    for l in range(DEPTH):
        w = (w_in[l], norm_attn_pre[l], norm_attn_post[l], norm_q[l], norm_kv[l], w_uq[l], w_ukv[l],
             conv_w[l], w_conv_out[l], w_attn_out[l], w_merge[l], norm_ffn_pre[l], norm_ffn_post[l],
             w_gate_up[l], w_down[l])
        xp, nc, cn, kn = layer(xp, pos_p, pos_p, hist_p, None, None, *w)
        conv_p.append(nc); ckv_p.append(cn); kr_p.append(kn)
        xs, nc, cn, kn = layer(xs, pos_s, kpos_s, state_conv[l], cache_ckv[l], cache_krope[l], *w)
        conv_s.append(nc); ckv_s.append(cn); kr_s.append(kn)

    return (xp, xs, jnp.stack(conv_p), jnp.stack(ckv_p), jnp.stack(kr_p),
            jnp.stack(conv_s), jnp.stack(ckv_s), jnp.stack(kr_s))
```

```python
import numpy as np
from contextlib import ExitStack
import concourse.bass as bass
import concourse.mybir as mybir
from concourse.bass_utils import run_bass_kernel_spmd

F32 = mybir.dt.float32
BF16 = mybir.dt.bfloat16
ALU = mybir.AluOpType
AF = mybir.ActivationFunctionType

NCORES = 8
D = 1024
KC = 8
L = 2
SEQ = 2048
SEG = 512
DEC_SEQ = 32
PAST = 2048
NKEYMAX = PAST + DEC_SEQ
H = 16
QK_NOPE = 64
QK_ROPE = 32
QH = QK_NOPE + QK_ROPE
V_HEAD = 64
Q_LORA = 384
KV_LORA = 256
DFF = 2816
FC = DFF // 128
D_IN = 3 * D + Q_LORA + KV_LORA + QK_ROPE + 2 * D
O1, O2, O3 = D, 2 * D, 3 * D
O4 = O3 + Q_LORA
O5 = O4 + KV_LORA
O6 = O5 + QK_ROPE
O7 = O6 + D
EPS = 1e-6
ATTN_SCALE = float(QH ** -0.5)
ROPE_THETA = 10000.0
NSLOT = 6
NSP = 61
C_GPRE, C_GPOST, C_FPRE, C_FPOST, C_GQ, C_GKV, C_CW = 0, 8, 16, 24, 32, 35, 37


class Rec:
    ENG = ("pe", "act", "dve", "pool", "sp")

    def __init__(self):
        self.streams = {e: [] for e in self.ENG}
        self.cnt = {}
        self.waited = {e: {} for e in self.ENG}
        self.res = {}

    def _deps(self, eng, reads, writes):
        deps = {}

        def add(sk, v):
            if eng == "pe" and sk == "E_pe":
                return
            if deps.get(sk, 0) < v:
                deps[sk] = v

        for r in reads:
            st = self.res.get(r)
            if st and st[0] is not None:
                add(*st[0])
        for w in writes:
            st = self.res.get(w)
            if st:
                if st[0] is not None:
                    add(*st[0])
                for sk, v in st[1].items():
                    add(sk, v)
        wd = self.waited[eng]
        for sk, v in deps.items():
            if wd.get(sk, 0) < v:
                self.streams[eng].append(("w", sk, v))
                wd[sk] = v

    def _commit(self, t, reads, writes):
        for r in reads:
            st = self.res.setdefault(r, [None, {}])
            if st[1].get(t[0], 0) < t[1]:
                st[1][t[0]] = t[1]
        for w in writes:
            self.res[w] = [t, {}]

    def op(self, eng, fn, reads=(), writes=()):
        self._deps(eng, reads, writes)
        sk = "E_" + eng
        self.cnt[sk] = self.cnt.get(sk, 0) + 1
        t = (sk, self.cnt[sk])
        self.streams[eng].append(("i", fn, sk, 1))
        self._commit(t, reads, writes)
        return t

    def dma(self, eng, fn, semkey, reads=(), writes=()):
        self._deps(eng, reads, writes)
        self.cnt[semkey] = self.cnt.get(semkey, 0) + 16
        t = (semkey, self.cnt[semkey])
        self.streams[eng].append(("i", fn, semkey, 16))
        self._commit(t, reads, writes)
        return t

    def mmgroup(self, bank_key, mms):
        sk = "E_pe"
        final = (sk, self.cnt.get(sk, 0) + 1)
        n = len(mms)
        for i, (fn, reads) in enumerate(mms):
            self._deps("pe", reads, (bank_key,) if i == 0 else ())
            last = i == n - 1
            self.streams["pe"].append(
                ("i", (lambda e, fn=fn, st=(i == 0), sp=last: fn(e, st, sp)), sk if last else None, 1))
            for r in reads:
                st_ = self.res.setdefault(r, [None, {}])
                if st_[1].get(sk, 0) < final[1]:
                    st_[1][sk] = final[1]
        self.cnt[sk] = final[1]
        self.res[bank_key] = [final, {}]
        return final

    def mm1(self, bank_key, fn, reads, first):
        sk = "E_pe"
        self._deps("pe", reads, (bank_key,) if first else ())
        self.cnt[sk] = self.cnt.get(sk, 0) + 1
        t = (sk, self.cnt[sk])
        self.streams["pe"].append(("i", fn, sk, 1))
        for r in reads:
            st_ = self.res.setdefault(r, [None, {}])
            st_[1][sk] = t[1]
        old = self.res.get(bank_key)
        self.res[bank_key] = [t, {} if (first or not old) else old[1]]
        return t


def _replay(e, stream, semh):
    for it in stream:
        if it[0] == "w":
            e.wait_ge(semh[it[1]], it[2])
        else:
            inst = it[1](e)
            if it[2] is not None:
                inst.then_inc(semh[it[2]], it[3])


class _Stop(Exception):
    pass


def build_program(n_prompt_seq=2, n_quarters=4, with_sample=True, n_layers=L, debug=False, stop=None):
    nc = bass.Bass("TRN2", target_bir_lowering=False)
    R = Rec()

    def din(name, shape):
        return nc.dram_tensor(name, list(shape), F32, kind="ExternalInput").ap()

    def dout(name, shape):
        return nc.dram_tensor(name, list(shape), F32, kind="ExternalOutput").ap()

    x_p = din("x_p", (2, SEQ, D))
    x_s = din("x_s", (DEC_SEQ, D))
    c_ckv = din("c_ckv", (L, PAST, KV_LORA))
    c_kr = din("c_kr", (L, PAST, QK_ROPE))
    hist0 = din("hist0", (128, L * KC * 2))
    smallp_d = din("smallp", (128, L * NSP))
    rope_d = din("rope", (2, 128, NKEYMAX))
    ident_d = din("ident", (128, 128))
    w_in = din("w_in", (L, D, D_IN))
    w_uq = din("w_uq", (L, Q_LORA, H * QH))
    w_ukv = din("w_ukv", (L, KV_LORA, H * 128))
    w_conv_out = din("w_conv_out", (L, D, D))
    w_attn_out = din("w_attn_out", (L, D, D))
    w_merge = din("w_merge", (L, D, D))
    w_gate_up = din("w_gate_up", (L, D, 2 * DFF))
    w_down = din("w_down", (L, DFF, D))

    y_p = dout("y_p", (2, SEQ, D))
    y_s = dout("y_s", (DEC_SEQ, D))
    nconv_p = dout("nconv_p", (L, 2, 2, D))
    nckv_p = dout("nckv_p", (L, 2, SEQ, KV_LORA))
    nkr_p = dout("nkr_p", (L, 2, SEQ, QK_ROPE))
    nconv_s = dout("nconv_s", (L, 2, D))
    nckv_s = dout("nckv_s", (L, DEC_SEQ, KV_LORA))
    nkr_s = dout("nkr_s", (L, DEC_SEQ, QK_ROPE))
    dbg = {}
    if debug:
        for nm, shp in (("d_xn", (128, KC * SEG)), ("d_bc", (128, KC * SEG)), ("d_qn", (128, 3 * SEG)),
                        ("d_attn", (128, KC * SEG)), ("d_z", (128, KC * SEG)), ("d_xres", (128, KC * SEG)),
                        ("d_xres2", (128, KC * SEG)), ("d_qt", (128, SEG)), ("d_kt", (128, SEG))):
            dbg[nm] = dout(nm, shp)

    wv = {
        "w_in": w_in.rearrange("l (kc p) m -> l p kc m", p=128),
        "w_uq": w_uq.rearrange("l (kc p) m -> l p kc m", p=128),
        "w_ukv": w_ukv.rearrange("l (kc p) m -> l p kc m", p=128),
        "w_conv_out": w_conv_out.rearrange("l (kc p) m -> l p kc m", p=128),
        "w_attn_out": w_attn_out.rearrange("l (kc p) m -> l p kc m", p=128),
        "w_merge": w_merge.rearrange("l (kc p) m -> l p kc m", p=128),
        "w_gate_up": w_gate_up.rearrange("l (kc p) m -> l p kc m", p=128),
        "w_down": w_down.rearrange("l (kc p) m -> l p kc m", p=128),
    }

    with ExitStack() as es:
        def sb(name, shape, dt):
            return es.enter_context(nc.sbuf_tensor(name, list(shape), dt))

        xresT = sb("xresT", (128, KC, SEG), F32)
        xnT = sb("xnT", (128, KC, SEG), BF16)
        big = sb("big", (128, 24, SEG), BF16)
        qnT = sb("qnT", (128, 3, SEG), BF16)
        cT = sb("cT", (128, L, 2, NKEYMAX), BF16)
        krT = sb("krT", (128, L, NKEYMAX), BF16)
        scr = sb("scr", (128, KC, SEG), F32)
        wbuf = sb("wbuf", (128, NSLOT, KC, 128), BF16)
        wuq = sb("wuq", (128, 3, H * QH), BF16)
        wuqB = sb("wuqB", (128, 3, H, 32), BF16)
        wuqA = sb("wuqA", (128, 3, H, 32), BF16)
        wukv = sb("wukv", (128, 2, H * 128), BF16)
        wkrB = sb("wkrB", (128, KC, 32), BF16)
        KT = sb("KT", (128, 2, NKEYMAX), BF16)
        QT = sb("QT", (128, 2, SEG), BF16)
        VB = sb("VB", (128, 2, 17, 128), BF16)
        PT = sb("PT", (128, 3, SEG), BF16)
        Rr = sb("Rr", (128, SEG), F32)
        u_t = sb("u_t", (128, SEG + 2), F32)
        cv_t = sb("cv_t", (128, SEG), F32)
        cg_t = sb("cg_t", (128, SEG), F32)
        sa_t = sb("sa_t", (128, SEG), F32)
        sb_t = sb("sb_t", (128, SEG), F32)
        sg_t = sb("sg_t", (128, 2, SEG), F32)
        sq_t = sb("sq_t", (128, 2, SEG), BF16)
        rt_t = sb("rt_t", (128, SEG), F32)
        rstd_t = sb("rstd_t", (128, SEG), F32)
        kro_t = sb("kro_t", (128, SEG), F32)
        t1_t = sb("t1_t", (128, SEG), F32)
        t2_t = sb("t2_t", (128, SEG), F32)
        cos_t = sb("cos_t", (128, SEG), F32)
        sin_t = sb("sin_t", (128, SEG), F32)
        xtok = sb("xtok", (128, D), F32)
        ost = sb("ost", (128, 2, 288), F32)
        hist = sb("hist", (128, L, KC, 2), F32)
        smallp = sb("smallp_sb", (128, L, NSP), F32)
        ident = sb("ident_sb", (128, 128), F32)
        ones = sb("ones_sb", (128, 128), BF16)
        ps = [es.enter_context(nc.psum_tensor(f"ps{i}", [128, 512], F32)) for i in range(8)]

        bcT = lambda m: big[:, m, :]
        attnT_i, zT_i = 8, 16
        qlat_i, ckvs_i, cnew_i = 0, 3, 5

        rr = {"mm": 0, "aux": 0, "acc": 0}
        MM_BANKS, AUX_BANKS, ACC_BANKS, SS_BANK = (0, 1, 2), (4, 7), (5, 6), 3

        def bank(cls):
            lst = {"mm": MM_BANKS, "aux": AUX_BANKS, "acc": ACC_BANKS}[cls]
            b = lst[rr[cls] % len(lst)]
            rr[cls] += 1
            return b

        def act(out, in_, func, reads, writes, scale=None, bias=None):
            kw = {}
            if scale is not None:
                kw["scale"] = scale
            if bias is not None:
                kw["bias"] = bias
            return R.op("act", lambda e: e.activation(out=out, in_=in_, func=func, **kw), reads, writes)

        def tt(out, in0, in1, op, reads, writes):
            return R.op("dve", lambda e: e.tensor_tensor(out=out, in0=in0, in1=in1, op=op), reads, writes)

        def stt(out, in0, scalar, in1, op0, op1, reads, writes):
            return R.op("dve", lambda e: e.scalar_tensor_tensor(out=out, in0=in0, scalar=scalar, in1=in1,
                                                               op0=op0, op1=op1), reads, writes)

        def dcopy(out, in_, reads, writes):
            return R.op("dve", lambda e: e.tensor_copy(out=out, in_=in_), reads, writes)

        def acopy(out, in_, reads, writes, scale=None):
            if scale is None:
                return R.op("act", lambda e: e.copy(out=out, in_=in_), reads, writes)
            return R.op("act", lambda e: e.mul(out=out, in_=in_, mul=scale), reads, writes)

        def mm(out, lhsT, rhs):
            return lambda e, st, sp: e.matmul(out, lhsT, rhs, start=st, stop=sp)

        def tr(out, in_, idn):
            return lambda e, st, sp: e.transpose(out, in_, idn)

        out_tickets = []

        def chk(name):
            if stop == name:
                raise _Stop()

        class WS:
            def __init__(self):
                self.sched = []
                self.emitted = 0

            def advance(self, upto):
                upto = min(upto, len(self.sched) - 1)
                while self.emitted <= upto:
                    i = self.emitted
                    (l, name, kc0, nkc, c0, ncols) = self.sched[i]
                    slot = i % NSLOT
                    src = wv[name][l, :, kc0:kc0 + nkc, c0:c0 + ncols]
                    dst = wbuf[:, slot, 0:nkc, 0:ncols]
                    R.dma("pool", lambda e, dst=dst, src=src: e.dma_start(out=dst, in_=src), f"W{slot}",
                          reads=(), writes=(("w", slot),))
                    self.emitted += 1

            def use(self, i, desc):
                assert self.sched[i] == desc, (i, self.sched[i], desc)
                assert i < self.emitted, (i, self.emitted)
                return i % NSLOT

        ws = WS()
        wpos = [0]

        def wtile(desc):
            i = wpos[0]
            wpos[0] += 1
            slot = ws.use(i, desc)
            return slot, ("w", slot)

        def wdone():
            ws.advance(wpos[0] - 1 + NSLOT)

        segs = []
        for s in range(n_prompt_seq):
            for q in range(n_quarters):
                segs.append(dict(kind="p", seq=s, q=q, N=SEG, pos0=q * SEG, key0=q * SEG))
        if with_sample:
            segs.append(dict(kind="s", seq=0, q=4, N=DEC_SEQ, pos0=PAST, key0=PAST))

        def sl_sched(l):
            out = []
            for m in range(KC):
                out.append((l, "w_in", 0, KC, O1 + 128 * m, 128))
                out.append((l, "w_in", 0, KC, O2 + 128 * m, 128))
                out.append((l, "w_in", 0, KC, 128 * m, 128))
            for i in range(3):
                out.append((l, "w_in", 0, KC, O3 + 128 * i, 128))
            for i in range(2):
                out.append((l, "w_in", 0, KC, O4 + 128 * i, 128))
            out.append((l, "w_in", 0, KC, O5, 32))
            for m in range(KC):
                out.append((l, "w_conv_out", 0, KC, 128 * m, 128))
                out.append((l, "w_in", 0, KC, O6 + 128 * m, 128))
                out.append((l, "w_attn_out", 0, KC, 128 * m, 128))
                out.append((l, "w_in", 0, KC, O7 + 128 * m, 128))
            for m in range(KC):
                out.append((l, "w_merge", 0, KC, 128 * m, 128))
            for f in range(FC):
                out.append((l, "w_gate_up", 0, KC, 128 * f, 128))
                out.append((l, "w_gate_up", 0, KC, DFF + 128 * f, 128))
            for m in range(KC):
                out.append((l, "w_down", 0, 8, 128 * m, 128))
                out.append((l, "w_down", 8, 8, 128 * m, 128))
                out.append((l, "w_down", 16, 6, 128 * m, 128))
            return out

        for sg in segs:
            for l in range(n_layers):
                ws.sched.extend(sl_sched(l))

        R.dma("sp", lambda e: e.dma_start(out=smallp[:, :, :].rearrange("p l c -> p (l c)"), in_=smallp_d[:, :]),
              "S_small", writes=("smallp",))
        R.dma("sp", lambda e: e.dma_start(out=ident[:, :], in_=ident_d[:, :]), "S_ident", writes=("ident",))
        R.op("dve", lambda e: e.memset(ones[:, :], 1.0), writes=("ones",))
        R.op("dve", lambda e: e.memset(VB[:, 0, :, 64:128], 1.0), writes=(("V", 0),))
        R.op("dve", lambda e: e.memset(VB[:, 1, :, 0:64], 1.0), writes=(("V", 1),))
        R.op("dve", lambda e: e.memset(KT[:, :, :], 0.0), writes=(("KT", 0), ("KT", 1)))
        R.op("dve", lambda e: e.memset(QT[:, :, :], 0.0), writes=(("QT", 0), ("QT", 1)))
        ws.advance(NSLOT - 1)

        def sp_col(l, c0, n=1):
            return smallp[:, l, c0:c0 + n]

        def load_wuq_wukv(l):
            for kc in range(3):
                R.dma("pool", lambda e, kc=kc: e.dma_start(out=wuq[:, kc, :], in_=wv["w_uq"][l, :, kc, :]),
                      "S_wuq", writes=("wuq",))
            for kc in range(2):
                for hf in range(2):
                    R.dma("pool", lambda e, kc=kc, hf=hf: e.dma_start(
                        out=wukv[:, kc, hf * 1024:(hf + 1) * 1024], in_=wv["w_ukv"][l, :, kc, hf * 1024:(hf + 1) * 1024]),
                        "S_wukv", writes=("wukv",))
            wv4 = wuq[:, :, :].rearrange("p k (h d) -> p k h d", d=QH)
            acopy(wuqB[:, :, :, 0:16], wv4[:, :, :, 80:96], reads=("wuq",), writes=("wuqB",), scale=-1.0)
            acopy(wuqB[:, :, :, 16:32], wv4[:, :, :, 64:80], reads=("wuq",), writes=("wuqB",))
            acopy(wuqA[:, :, :, :], wv4[:, :, :, 64:96], reads=("wuq",), writes=("wuqA",))

        def norm_finish(ssb, nfeat, N):
            act(rt_t[:, :N], ps[ssb][:, :N], AF.Ln, reads=(("ps", ssb),), writes=("rt",), scale=1.0 / nfeat,
                bias=EPS)
            act(rstd_t[:, :N], rt_t[:, :N], AF.Exp, reads=("rt",), writes=("rstd",), scale=-0.5)

        def prenorm(l, gcol, N):
            ssb = SS_BANK
            for kc in range(KC):
                b = kc % 2
                act(sq_t[:, b, :N], xresT[:, kc, :N], AF.Square, reads=(("xres", kc),), writes=(("sq", b),))
                R.mm1(("ps", ssb), (lambda e, b=b, kc=kc: e.matmul(ps[ssb][:, :N], ones[:, :], sq_t[:, b, :N],
                                                                 start=(kc == 0), stop=(kc == KC - 1))),
                      reads=(("sq", b), "ones"), first=(kc == 0))
            norm_finish(ssb, D, N)
            for kc in range(KC):
                stt(xnT[:, kc, :N], xresT[:, kc, :N], sp_col(l, gcol + kc), rstd_t[:, :N], ALU.mult, ALU.mult,
                    reads=(("xres", kc), "rstd", "smallp"), writes=(("xn", kc),))

        def dense_group(l, name, c0, ncols, rhs_fn, rhs_keys, nkc_total=KC, pieces=None):
            b = bank("mm")
            mms = []
            pieces = pieces or [(0, nkc_total)]
            for (k0, nk) in pieces:
                slot, wkey = wtile((l, name, k0, nk, c0, ncols))
                for k in range(nk):
                    kc = k0 + k
                    mms.append((mm(ps[b][0:ncols, :rhs_fn.N], wbuf[:, slot, k, 0:ncols], rhs_fn(kc)),
                                (wkey, rhs_keys(kc))))
            R.mmgroup(("ps", b), mms)
            wdone()
            return b

        class RhsFn:
            def __init__(self, fn, N):
                self.fn = fn
                self.N = N

            def __call__(self, kc):
                return self.fn(kc)

        def segment_layer(sg, l, first_layer, last_layer):
            N = sg["N"]
            kind = sg["kind"]
            key0 = sg["key0"]
            pos0 = sg["pos0"]
            q = sg["q"]
            seq = sg["seq"]
            nblk = (N + 127) // 128
            xn_rhs = RhsFn(lambda kc: xnT[:, kc, :N], N)
            xn_keys = lambda kc: ("xn", kc)

            if first_layer:
                for blk in range(nblk):
                    nb = min(128, N - blk * 128)
                    if kind == "p":
                        src = x_p[seq, pos0 + blk * 128: pos0 + blk * 128 + nb, :]
                    else:
                        src = x_s[blk * 128: blk * 128 + nb, :]
                    R.dma("sp", lambda e, src=src, nb=nb: e.dma_start(out=xtok[0:nb, :], in_=src), "S_xtok",
                          writes=("xtok",))
                    for half in range(2):
                        b = bank("mm")
                        for j in range(4):
                            kc = half * 4 + j
                            R.mmgroup(("ps", b) if j == 0 else ("psx", b), [
                                (tr(ps[b][:, j * 128: j * 128 + nb], xtok[0:nb, kc * 128:(kc + 1) * 128],
                                    ident[0:nb, 0:nb]), ("xtok", "ident"))])
                        R.res[("ps", b)] = [("E_pe", R.cnt["E_pe"]), {}]
                        src_ps = ps[b][:, :].rearrange("p (j t) -> p j t", t=128)[:, :, 0:nb]
                        acopy(xresT[:, half * 4:half * 4 + 4, blk * 128: blk * 128 + nb], src_ps,
                              reads=(("ps", b),), writes=tuple(("xres", half * 4 + j) for j in range(4)))
                R.dma("sp", lambda e: e.dma_start(out=cos_t[:, :N], in_=rope_d[0, :, pos0:pos0 + N]), "S_cos",
                      writes=("cos",))
                R.dma("sp", lambda e: e.dma_start(out=sin_t[:, :N], in_=rope_d[1, :, pos0:pos0 + N]), "S_sin",
                      writes=("sin",))

            chk("input")
            if kind == "p" and q == 0:
                R.op("dve", lambda e: e.memset(hist[:, l, :, :], 0.0), writes=(("hist", l),))
            if kind == "s":
                R.dma("sp", lambda e: e.dma_start(
                    out=hist[:, l, :, :].rearrange("p k j -> p (k j)"), in_=hist0[:, l * 16:(l + 1) * 16]),
                    "S_hist", writes=(("hist", l),))
                for blk in range(PAST // 128):
                    ob = blk % 2
                    R.dma("sp", lambda e, blk=blk, ob=ob: e.dma_start(
                        out=ost[:, ob, 0:256], in_=c_ckv[l, blk * 128:(blk + 1) * 128, :]), f"S_ost{ob}",
                        writes=(("ost", ob),))
                    R.dma("sp", lambda e, blk=blk, ob=ob: e.dma_start(
                        out=ost[:, ob, 256:288], in_=c_kr[l, blk * 128:(blk + 1) * 128, :]), f"S_ost{ob}",
                        writes=())
                    R.res[("ost", ob)] = [(f"S_ost{ob}", R.cnt[f"S_ost{ob}"]), {}]
                    b = bank("aux")
                    R.mmgroup(("ps", b), [(tr(ps[b][:, 0:128], ost[:, ob, 0:128], ident[:, :]), (("ost", ob), "ident"))])
                    R.mmgroup(("psx", b), [(tr(ps[b][:, 128:256], ost[:, ob, 128:256], ident[:, :]), (("ost", ob), "ident"))])
                    R.mmgroup(("psx", b), [(tr(ps[b][0:32, 256:384], ost[:, ob, 256:288], ident[:, :]), (("ost", ob), "ident"))])
                    R.res[("ps", b)] = [("E_pe", R.cnt["E_pe"]), {}]
                    jt = blk // 4
                    acopy(cT[:, l, :, blk * 128:(blk + 1) * 128],
                          ps[b][:, 0:256].rearrange("p (k t) -> p k t", t=128),
                          reads=(("ps", b),), writes=(("cT", l, jt),))
                    acopy(krT[64:96, l, blk * 128:(blk + 1) * 128], ps[b][0:32, 256:384],
                          reads=(("ps", b),), writes=(("krT", l, jt),))

            chk("hist")
            load_wuq_wukv(l)

            chk("wuq")
            prenorm(l, C_GPRE, N)

            chk("prenorm")
            for m in range(KC):
                b_cg = dense_group(l, "w_in", O1 + 128 * m, 128, xn_rhs, xn_keys)
                b_xi = dense_group(l, "w_in", O2 + 128 * m, 128, xn_rhs, xn_keys)
                acopy(cg_t[:, :N], ps[b_cg][:, :N], reads=(("ps", b_cg),), writes=("cg",))
                dcopy(u_t[:, 0:2], hist[:, l, m, :], reads=(("hist", l),), writes=("u",))
                tt(u_t[:, 2:2 + N], ps[b_xi][:, :N], cg_t[:, :N], ALU.mult, reads=(("ps", b_xi), "cg", "u"),
                   writes=("u",))
                cws = [sp_col(l, C_CW + 3 * m + j) for j in range(3)]
                cw = lambda j, cws=cws: cws[j]
                R.op("dve", lambda e, c2=cws[2]: e.tensor_scalar(out=cv_t[:, :N], in0=u_t[:, 2:2 + N], scalar1=c2,
                                                                 scalar2=None, op0=ALU.mult),
                     reads=("u", "smallp"), writes=("cv",))
                stt(cv_t[:, :N], u_t[:, 1:1 + N], cw(1), cv_t[:, :N], ALU.mult, ALU.add, reads=("u", "cv", "smallp"),
                    writes=("cv",))
                stt(cv_t[:, :N], u_t[:, 0:N], cw(0), cv_t[:, :N], ALU.mult, ALU.add, reads=("u", "cv", "smallp"),
                    writes=("cv",))
                dcopy(hist[:, l, m, :], u_t[:, N:N + 2], reads=("u",), writes=(("hist", l),))
                b_bg = dense_group(l, "w_in", 128 * m, 128, xn_rhs, xn_keys)
                tt(big[:, m, :N], ps[b_bg][:, :N], cv_t[:, :N], ALU.mult, reads=(("ps", b_bg), "cv"),
                   writes=(("big", m),))
            if (kind == "p" and q == n_quarters - 1) or kind == "s":
                if kind == "p":
                    dst = nconv_p[l, seq, :, :].rearrange("j (k p) -> p k j", p=128)
                else:
                    dst = nconv_s[l, :, :].rearrange("j (k p) -> p k j", p=128)
                for kc in range(KC):
                    out_tickets.append(R.dma("sp", lambda e, dst=dst, kc=kc: e.dma_start(
                        out=dst[:, kc, :], in_=hist[:, l, kc, :], allow_slow_non_contiguous=True), "S_histout",
                        reads=(("hist", l),)))

            chk("conv")
            for i in range(3):
                bq = dense_group(l, "w_in", O3 + 128 * i, 128, xn_rhs, xn_keys)
                acopy(scr[:, qlat_i + i, :N], ps[bq][:, :N], reads=(("ps", bq),), writes=(("scr", qlat_i + i),))
            for i in range(3):
                b = i % 2
                act(sq_t[:, b, :N], scr[:, qlat_i + i, :N], AF.Square, reads=(("scr", qlat_i + i),),
                    writes=(("sq", b),))
                R.mm1(("ps", SS_BANK), (lambda e, b=b, i=i: e.matmul(ps[SS_BANK][:, :N], ones[:, :], sq_t[:, b, :N],
                                                                    start=(i == 0), stop=(i == 2))),
                      reads=(("sq", b), "ones"), first=(i == 0))
            norm_finish(SS_BANK, Q_LORA, N)
            for i in range(3):
                stt(qnT[:, i, :N], scr[:, qlat_i + i, :N], sp_col(l, C_GQ + i), rstd_t[:, :N], ALU.mult, ALU.mult,
                    reads=(("scr", qlat_i + i), "rstd", "smallp"), writes=(("qn", i),))
            for i in range(2):
                bq = dense_group(l, "w_in", O4 + 128 * i, 128, xn_rhs, xn_keys)
                acopy(scr[:, ckvs_i + i, :N], ps[bq][:, :N], reads=(("ps", bq),), writes=(("scr", ckvs_i + i),))
            for i in range(2):
                b = i % 2
                act(sq_t[:, b, :N], scr[:, ckvs_i + i, :N], AF.Square, reads=(("scr", ckvs_i + i),),
                    writes=(("sq", b),))
                R.mm1(("ps", SS_BANK), (lambda e, b=b, i=i: e.matmul(ps[SS_BANK][:, :N], ones[:, :], sq_t[:, b, :N],
                                                                    start=(i == 0), stop=(i == 1))),
                      reads=(("sq", b), "ones"), first=(i == 0))
            norm_finish(SS_BANK, KV_LORA, N)
            jt_new = q
            for i in range(2):
                stt(scr[:, cnew_i + i, :N], scr[:, ckvs_i + i, :N], sp_col(l, C_GKV + i), rstd_t[:, :N], ALU.mult,
                    ALU.mult, reads=(("scr", ckvs_i + i), "rstd", "smallp"), writes=(("scr", cnew_i + i),))
                acopy(cT[:, l, i, key0:key0 + N], scr[:, cnew_i + i, :N], reads=(("scr", cnew_i + i),),
                      writes=(("cT", l, jt_new),))
            slot, wkey = wtile((l, "w_in", 0, KC, O5, 32))
            acopy(wkrB[:, :, 0:16], wbuf[:, slot, :, 16:32], reads=(wkey,), writes=("wkrB",), scale=-1.0)
            acopy(wkrB[:, :, 16:32], wbuf[:, slot, :, 0:16], reads=(wkey,), writes=("wkrB",))
            bA = bank("aux")
            R.mmgroup(("ps", bA), [(mm(ps[bA][0:32, :N], wbuf[:, slot, kc, 0:32], xnT[:, kc, :N]),
                                    (wkey, ("xn", kc))) for kc in range(KC)])
            wdone()
            bB = bank("aux")
            R.mmgroup(("ps", bB), [(mm(ps[bB][0:32, :N], wkrB[:, kc, :], xnT[:, kc, :N]), ("wkrB", ("xn", kc)))
                                   for kc in range(KC)])
            tt(t1_t[0:32, :N], ps[bA][0:32, :N], cos_t[0:32, :N], ALU.mult, reads=(("ps", bA), "cos"), writes=("t1",))
            tt(t2_t[0:32, :N], ps[bB][0:32, :N], sin_t[0:32, :N], ALU.mult, reads=(("ps", bB), "sin"), writes=("t2",))
            tt(kro_t[0:32, :N], t1_t[0:32, :N], t2_t[0:32, :N], ALU.add, reads=("t1", "t2"), writes=("kro",))
            acopy(krT[64:96, l, key0:key0 + N], kro_t[0:32, :N], reads=("kro",), writes=(("krT", l, jt_new),))
            chk("lowrank")
            for blk in range(nblk):
                nb = min(128, N - blk * 128)
                ob = blk % 2
                b = bank("aux")
                R.mmgroup(("ps", b), [(tr(ps[b][0:nb, 0:128], scr[:, cnew_i, blk * 128:blk * 128 + nb], ident[:, :]),
                                       (("scr", cnew_i), "ident"))])
                R.mmgroup(("psx", b), [(tr(ps[b][0:nb, 128:256], scr[:, cnew_i + 1, blk * 128:blk * 128 + nb],
                                          ident[:, :]), (("scr", cnew_i + 1), "ident"))])
                R.mmgroup(("psx", b), [(tr(ps[b][0:nb, 256:288], kro_t[0:32, blk * 128:blk * 128 + nb],
                                          ident[0:32, 0:32]), ("kro", "ident"))])
                R.res[("ps", b)] = [("E_pe", R.cnt["E_pe"]), {}]
                dcopy(ost[0:nb, ob, :], ps[b][0:nb, 0:288], reads=(("ps", b),), writes=(("ost", ob),))
                if kind == "p":
                    r0 = pos0 + blk * 128
                    d1 = nckv_p[l, seq, r0:r0 + nb, :]
                    d2 = nkr_p[l, seq, r0:r0 + nb, :]
                else:
                    d1 = nckv_s[l, 0:nb, :]
                    d2 = nkr_s[l, 0:nb, :]
                out_tickets.append(R.dma("sp", lambda e, d1=d1, ob=ob, nb=nb: e.dma_start(
                    out=d1, in_=ost[0:nb, ob, 0:256]), f"S_ost{ob}", reads=(("ost", ob),)))
                out_tickets.append(R.dma("sp", lambda e, d2=d2, ob=ob, nb=nb: e.dma_start(
                    out=d2, in_=ost[0:nb, ob, 256:288]), f"S_ost{ob}", reads=(("ost", ob),)))

            chk("ctxout")
            nkeys = key0 + N
            ktiles = [(j * 512, min(512, nkeys - j * 512)) for j in range((nkeys + 511) // 512)]
            kblocks = [(j * 128, min(128, nkeys - j * 128)) for j in range((nkeys + 127) // 128)]
            for hb in range(2):
                for jt, (k0, nk) in enumerate(ktiles):
                    dcopy(KT[64:96, hb, k0:k0 + nk], krT[64:96, l, k0:k0 + nk], reads=(("krT", l, jt),),
                          writes=(("KT", hb),))
            b4 = SS_BANK

            def prep_pieces(h):
                hb = h % 2
                eo = h % 2
                pieces = []
                ev = acopy if len(kblocks) <= 8 else dcopy
                for jt, (k0, nk) in enumerate(ktiles):
                    def p_kt(jt=jt, k0=k0, nk=nk):
                        b = bank("aux")
                        R.mmgroup(("ps", b), [(mm(ps[b][0:64, :nk], wukv[:, kc, h * 128:h * 128 + 64],
                                                  cT[:, l, kc, k0:k0 + nk]), ("wukv", ("cT", l, jt)))
                                              for kc in range(2)])
                        ev(KT[0:64, hb, k0:k0 + nk], ps[b][0:64, :nk], reads=(("ps", b),), writes=(("KT", hb),))
                    pieces.append(p_kt)
                for g0 in range(0, len(kblocks), 8):
                    def p_v(g0=g0):
                        grp = kblocks[g0:g0 + 8]
                        b = bank("aux")
                        for j, (k0, nk) in enumerate(grp):
                            R.mmgroup(("ps", b) if j == 0 else ("psx", b), [
                                (mm(ps[b][0:nk, j * 64:(j + 1) * 64], cT[:, l, kc, k0:k0 + nk],
                                    wukv[:, kc, h * 128 + 64:h * 128 + 128]), ("wukv", ("cT", l, k0 // 512)))
                                for kc in range(2)])
                        R.res[("ps", b)] = [("E_pe", R.cnt["E_pe"]), {}]
                        vc0 = 0 if eo == 0 else 64
                        full = [g for g in grp if g[1] == 128]
                        if full:
                            nf = len(full)
                            ev(VB[:, eo, g0:g0 + nf, vc0:vc0 + 64],
                                  ps[b][:, 0:nf * 64].rearrange("p (j v) -> p j v", v=64),
                                  reads=(("ps", b),), writes=(("V", eo),))
                        if len(full) < len(grp):
                            j = len(full)
                            nk = grp[j][1]
                            ev(VB[0:nk, eo, g0 + j, vc0:vc0 + 64], ps[b][0:nk, j * 64:(j + 1) * 64],
                                  reads=(("ps", b),), writes=(("V", eo),))
                    pieces.append(p_v)

                def p_rot4():
                    g = h // 4
                    bA4 = bank("aux")
                    R.mmgroup(("ps", bA4), [(mm(ps[bA4][:, :N], wuqA[:, kc, 4 * g:4 * g + 4, :], qnT[:, kc, :N]),
                                             ("wuqA", ("qn", kc))) for kc in range(3)])
                    R.mmgroup(("ps", b4), [(mm(ps[b4][:, :N], wuqB[:, kc, 4 * g:4 * g + 4, :], qnT[:, kc, :N]),
                                            ("wuqB", ("qn", kc))) for kc in range(3)])
                    tt(t1_t[:, :N], ps[bA4][:, :N], cos_t[:, :N], ALU.mult, reads=(("ps", bA4), "cos"), writes=("t1",))
                    tt(t2_t[:, :N], ps[b4][:, :N], sin_t[:, :N], ALU.mult, reads=(("ps", b4), "sin"), writes=("t2",))
                    tt(kro_t[:, :N], t1_t[:, :N], t2_t[:, :N], ALU.add, reads=("t1", "t2"), writes=("kro",))

                def p_q():
                    bA = bank("aux")
                    R.mmgroup(("ps", bA), [(mm(ps[bA][0:64, :N], wuq[:, kc, h * QH:h * QH + 64], qnT[:, kc, :N]),
                                            ("wuq", ("qn", kc))) for kc in range(3)])
                    ev(QT[0:64, hb, :N], ps[bA][0:64, :N], reads=(("ps", bA),), writes=(("QT", hb),))
                    j4 = h % 4
                    R.op("pool", lambda e: e.tensor_copy(out=QT[64:96, hb, :N], in_=kro_t[32 * j4:32 * j4 + 32, :N]),
                         reads=("kro",), writes=(("QT", hb),))
                    if debug and h == 0 and l == 0 and sg is segs[0]:
                        R.dma("pool", lambda e: e.dma_start(out=dbg["d_qt"][:, :], in_=QT[:, 0, :]), "S_dbg",
                              reads=(("QT", 0),))
                        R.dma("pool", lambda e: e.dma_start(out=dbg["d_kt"][:, :], in_=KT[:, 0, 0:SEG]), "S_dbg",
                              reads=(("KT", 0),))
                if h % 4 == 0:
                    pieces.insert(0, p_rot4)
                    pieces.insert(1, p_q)
                else:
                    pieces.insert(0, p_q)
                return pieces

            def head_loop(h, pieces):
                hb = h % 2
                eo = h % 2
                accb = bank("acc")
                nkb = len(kblocks)
                info = []
                for kb, (k0, nk) in enumerate(kblocks):
                    if kind == "p" and k0 >= key0:
                        bd = (k0 - key0) // 128
                        info.append((k0, nk, bd, 128 * bd))
                    else:
                        info.append((k0, nk, None, 0))
                sbanks = {}

                def issue_s(kb):
                    k0, nk, bd, qlo = info[kb]
                    sbk = bank("mm")
                    R.mmgroup(("ps", sbk), [(mm(ps[sbk][0:nk, qlo:N], KT[0:QH, hb, k0:k0 + nk], QT[0:QH, hb, qlo:N]),
                                             (("KT", hb), ("QT", hb)))])
                    sbanks[kb] = sbk

                for kb in range(min(2, nkb)):
                    issue_s(kb)
                for kb in range(nkb):
                    k0, nk, bd, qlo = info[kb]
                    sbk = sbanks[kb]
                    pb = kb % 3
                    act(PT[0:nk, pb, qlo:N], ps[sbk][0:nk, qlo:N], AF.Exp, reads=(("ps", sbk),),
                        writes=(("PT", pb),), scale=ATTN_SCALE)
                    def pv(c0, c1, r1, first, last, kb=kb, pb=pb):
                        o_ap, l_ap, r_ap = ps[accb][:, c0:c1], VB[0:r1, eo, kb, :], PT[0:r1, pb, c0:c1]
                        R.mm1(("ps", accb), (lambda e: e.matmul(o_ap, l_ap, r_ap, start=first, stop=last)),
                              reads=(("V", eo), ("PT", pb)), first=first)
                    if bd is None:
                        pv(qlo, N, nk, kb == 0, kb == nkb - 1)
                    else:
                        pv(qlo + 64, N, nk, kb == 0, False)
                        pv(qlo, qlo + 64, 64, False, kb == nkb - 1)
                    if kb + 2 < nkb:
                        issue_s(kb + 2)
                    if pieces:
                        pieces.pop(0)()
                while pieces:
                    pieces.pop(0)()
                dlo, slo = (0, 64) if eo == 0 else (64, 0)
                act(Rr[dlo:dlo + 64, :N], ps[accb][slo:slo + 64, :N], AF.Ln, reads=(("ps", accb),), writes=("R",))
                act(Rr[dlo:dlo + 64, :N], Rr[dlo:dlo + 64, :N], AF.Exp, reads=("R",), writes=("R",), scale=-1.0)
                tt(big[dlo:dlo + 64, attnT_i + h // 2, :N], ps[accb][dlo:dlo + 64, :N], Rr[dlo:dlo + 64, :N], ALU.mult,
                   reads=(("ps", accb), "R"), writes=(("big", attnT_i + h // 2),))

            for p in prep_pieces(0):
                p()
            for h in range(H):
                nxt = prep_pieces(h + 1) if h + 1 < H else []
                head_loop(h, nxt)

            chk("attn")
            bc_rhs = RhsFn(lambda kc: big[:, kc, :N], N)
            bc_keys = lambda kc: ("big", kc)
            at_rhs = RhsFn(lambda kc: big[:, attnT_i + kc, :N], N)
            at_keys = lambda kc: ("big", attnT_i + kc)
            for m in range(KC):
                b_ya = dense_group(l, "w_conv_out", 128 * m, 128, bc_rhs, bc_keys)
                b_ga = dense_group(l, "w_in", O6 + 128 * m, 128, xn_rhs, xn_keys)
                act(sa_t[:, :N], ps[b_ga][:, :N], AF.Sigmoid, reads=(("ps", b_ga),), writes=("sa",))
                tt(sa_t[:, :N], ps[b_ya][:, :N], sa_t[:, :N], ALU.mult, reads=(("ps", b_ya), "sa"), writes=("sa",))
                b_yb = dense_group(l, "w_attn_out", 128 * m, 128, at_rhs, at_keys)
                b_gb = dense_group(l, "w_in", O7 + 128 * m, 128, xn_rhs, xn_keys)
                act(sb_t[:, :N], ps[b_gb][:, :N], AF.Sigmoid, reads=(("ps", b_gb),), writes=("sb",))
                tt(sb_t[:, :N], ps[b_yb][:, :N], sb_t[:, :N], ALU.mult, reads=(("ps", b_yb), "sb"), writes=("sb",))
                tt(big[:, zT_i + m, :N], sa_t[:, :N], sb_t[:, :N], ALU.add, reads=("sa", "sb"),
                   writes=(("big", zT_i + m),))

            chk("z")
            def postnorm_residual(groups_fn, gcol):
                def ss_mm(m):
                    b = m % 2
                    R.mm1(("ps", SS_BANK), (lambda e, b=b, m=m: e.matmul(ps[SS_BANK][:, :N], ones[:, :],
                                                                        sq_t[:, b, :N], start=(m == 0),
                                                                        stop=(m == KC - 1))),
                          reads=(("sq", b), "ones"), first=(m == 0))
                for m in range(KC):
                    bm = groups_fn(m)
                    if m > 0:
                        ss_mm(m - 1)
                    b = m % 2
                    dcopy(scr[:, m, :N], ps[bm][:, :N], reads=(("ps", bm),), writes=(("scr", m),))
                    act(sq_t[:, b, :N], scr[:, m, :N], AF.Square, reads=(("scr", m),), writes=(("sq", b),))
                ss_mm(KC - 1)
                chk("pn_a")
                norm_finish(SS_BANK, D, N)
                chk("pn_b")
                for m in range(KC):
                    stt(scr[:, m, :N], scr[:, m, :N], sp_col(l, gcol + m), rstd_t[:, :N], ALU.mult, ALU.mult,
                        reads=(("scr", m), "rstd", "smallp"), writes=(("scr", m),))
                    tt(xresT[:, m, :N], xresT[:, m, :N], scr[:, m, :N], ALU.add, reads=(("xres", m), ("scr", m)),
                       writes=(("xres", m),))

            z_rhs = RhsFn(lambda kc: big[:, zT_i + kc, :N], N)
            z_keys = lambda kc: ("big", zT_i + kc)
            postnorm_residual(lambda m: dense_group(l, "w_merge", 128 * m, 128, z_rhs, z_keys), C_GPOST)
            if debug and l == 0 and sg is segs[0]:
                def dump(nm, t_, c0, n_, keyf, eng="pool"):
                    for k in range(n_):
                        R.dma(eng, lambda e, k=k: e.dma_start(out=dbg[nm][:, k * SEG:(k + 1) * SEG], in_=t_[:, c0 + k, :]),
                              "S_dbg", reads=(keyf(c0 + k),))
                dump("d_xn", xnT, 0, KC, lambda k: ("xn", k))
                dump("d_bc", big, 0, 8, lambda k: ("big", k))
                dump("d_qn", qnT, 0, 3, lambda k: ("qn", k))
                dump("d_attn", big, 8, 8, lambda k: ("big", k))
                dump("d_z", big, 16, 8, lambda k: ("big", k))
                dump("d_xres", xresT, 0, KC, lambda k: ("xres", k), "sp")
            chk("merge")
            prenorm(l, C_FPRE, N)
            for f in range(FC):
                b_g = dense_group(l, "w_gate_up", 128 * f, 128, xn_rhs, xn_keys)
                b_u = dense_group(l, "w_gate_up", DFF + 128 * f, 128, xn_rhs, xn_keys)
                sgb = f % 2
                act(sg_t[:, sgb, :N], ps[b_g][:, :N], AF.Silu, reads=(("ps", b_g),), writes=(("sg", sgb),))
                tt(big[:, f, :N], ps[b_u][:, :N], sg_t[:, sgb, :N], ALU.mult, reads=(("ps", b_u), ("sg", sgb)),
                   writes=(("big", f),))
            chk("ffn")
            h_rhs = RhsFn(lambda kc: big[:, kc, :N], N)
            h_keys = lambda kc: ("big", kc)
            postnorm_residual(lambda m: dense_group(l, "w_down", 128 * m, 128, h_rhs, h_keys,
                                                    pieces=[(0, 8), (8, 8), (16, 6)]), C_FPOST)

            chk("down")
            if debug and l == 0 and sg is segs[0]:
                for k in range(KC):
                    R.dma("sp", lambda e, k=k: e.dma_start(out=dbg["d_xres2"][:, k * SEG:(k + 1) * SEG], in_=xresT[:, k, :]),
                          "S_dbg", reads=(("xres", k),))
            if last_layer:
                for blk in range(nblk):
                    nb = min(128, N - blk * 128)
                    for half in range(2):
                        b = bank("mm")
                        for j in range(4):
                            kc = half * 4 + j
                            R.mmgroup(("ps", b) if j == 0 else ("psx", b), [
                                (tr(ps[b][0:nb, j * 128:(j + 1) * 128], xresT[:, kc, blk * 128:blk * 128 + nb],
                                    ident[:, :]), (("xres", kc), "ident"))])
                        R.res[("ps", b)] = [("E_pe", R.cnt["E_pe"]), {}]
                        acopy(xtok[0:nb, half * 512:(half + 1) * 512], ps[b][0:nb, :], reads=(("ps", b),),
                              writes=("xtok",))
                    if kind == "p":
                        r0 = pos0 + blk * 128
                        dst = y_p[seq, r0:r0 + nb, :]
                    else:
                        dst = y_s[blk * 128:blk * 128 + nb, :]
                    out_tickets.append(R.dma("sp", lambda e, dst=dst, nb=nb: e.dma_start(out=dst, in_=xtok[0:nb, :]),
                                             "S_xtok", reads=("xtok",)))

        try:
            for sg in segs:
                for l in range(n_layers):
                    segment_layer(sg, l, first_layer=(l == 0), last_layer=(l == n_layers - 1))
            assert wpos[0] == len(ws.sched), (wpos[0], len(ws.sched))
        except _Stop:
            pass
        for sk in list(R.cnt.keys()):
            if not sk.startswith("E_"):
                R.streams["sp"].append(("w", sk, R.cnt[sk]))

        fin = {}
        for (sk, v) in out_tickets:
            fin[sk] = max(fin.get(sk, 0), v)
        if "S_dbg" in R.cnt:
            fin["S_dbg"] = R.cnt["S_dbg"]
        for sk, v in fin.items():
            R.streams["sp"].append(("w", sk, v))

        semh = {}
        for sk in sorted(R.cnt.keys()):
            semh[sk] = es.enter_context(nc.semaphore(sk))
        with nc.Block() as block:
            @block.tensor
            def _(e):
                _replay(e, R.streams["pe"], semh)

            @block.scalar
            def _(e):
                _replay(e, R.streams["act"], semh)

            @block.vector
            def _(e):
                _replay(e, R.streams["dve"], semh)

            @block.gpsimd
            def _(e):
                _replay(e, R.streams["pool"], semh)

            @block.sync
            def _(e):
                _replay(e, R.streams["sp"], semh)
    stats = {k: len(v) for k, v in R.streams.items()}
    return nc, stats


def _host_constants():
    half = QK_ROPE // 2
    inv = ROPE_THETA ** (-np.arange(half, dtype=np.float32) / np.float32(half))
    pos = np.arange(NKEYMAX, dtype=np.float32)
    ang = pos[None, :] * inv[:, None].astype(np.float32)
    cos = np.cos(ang).astype(np.float32)
    sin = np.sin(ang).astype(np.float32)
    idx = np.arange(128) % half
    rope = np.stack([cos[idx], sin[idx]], axis=0).astype(np.float32)
    ident = np.eye(128, dtype=np.float32)
    return rope, ident


def _feat_major(v, nchunk):
    return np.ascontiguousarray(v.reshape(nchunk, 128).T)


def make_in_maps(inputs):
    f = lambda k: np.ascontiguousarray(np.asarray(inputs[k], dtype=np.float32))
    x_prompt, x_sample = f("x_prompt"), f("x_sample")
    state_conv, cache_ckv, cache_krope = f("state_conv"), f("cache_ckv"), f("cache_krope")
    rope, ident = _host_constants()
    smallp = np.zeros((128, L, NSP), np.float32)
    for l in range(L):
        smallp[:, l, C_GPRE:C_GPRE + 8] = _feat_major(f("norm_attn_pre")[l], 8)
        smallp[:, l, C_GPOST:C_GPOST + 8] = _feat_major(f("norm_attn_post")[l], 8)
        smallp[:, l, C_FPRE:C_FPRE + 8] = _feat_major(f("norm_ffn_pre")[l], 8)
        smallp[:, l, C_FPOST:C_FPOST + 8] = _feat_major(f("norm_ffn_post")[l], 8)
        smallp[:, l, C_GQ:C_GQ + 3] = _feat_major(f("norm_q")[l], 3)
        smallp[:, l, C_GKV:C_GKV + 2] = _feat_major(f("norm_kv")[l], 2)
        cw = f("conv_w")[l]
        for j in range(3):
            smallp[:, l, C_CW + j:C_CW + 24:3] = _feat_major(cw[j], 8)
    shared = dict(smallp=smallp.reshape(128, L * NSP), rope=rope, ident=ident,
                  w_in=f("w_in"), w_uq=f("w_uq"), w_ukv=f("w_ukv"), w_conv_out=f("w_conv_out"),
                  w_attn_out=f("w_attn_out"), w_merge=f("w_merge"), w_gate_up=f("w_gate_up"), w_down=f("w_down"))
    maps = []
    for c in range(NCORES):
        h0 = np.zeros((128, L, KC, 2), np.float32)
        for l in range(L):
            for j in range(2):
                h0[:, l, :, j] = _feat_major(state_conv[l, c, j], 8)
        m = dict(shared)
        m.update(x_p=np.ascontiguousarray(x_prompt[2 * c:2 * c + 2]), x_s=np.ascontiguousarray(x_sample[c]),
                 c_ckv=np.ascontiguousarray(cache_ckv[:, c]), c_kr=np.ascontiguousarray(cache_krope[:, c]),
                 hist0=h0.reshape(128, L * KC * 2))
        maps.append(m)
    return maps


_PROG = None


def kernel(**inputs):
    global _PROG
    if _PROG is None:
        _PROG = build_program()[0]
    in_maps = make_in_maps(inputs)
    res = run_bass_kernel_spmd(_PROG, in_maps, core_ids=list(range(NCORES)))
    rs = res.results
    B = 2 * NCORES
    y_prompt = np.concatenate([r["y_p"] for r in rs], axis=0)
    y_sample = np.stack([r["y_s"] for r in rs], axis=0)
    nconv_p = np.concatenate([r["nconv_p"] for r in rs], axis=1)
    nckv_p = np.concatenate([r["nckv_p"] for r in rs], axis=1)
    nkr_p = np.concatenate([r["nkr_p"] for r in rs], axis=1)
    nconv_s = np.stack([r["nconv_s"] for r in rs], axis=1)
    nckv_s = np.stack([r["nckv_s"] for r in rs], axis=1)
    nkr_s = np.stack([r["nkr_s"] for r in rs], axis=1)
    outs = (y_prompt, y_sample, nconv_p, nckv_p, nkr_p, nconv_s, nckv_s, nkr_s)
    return tuple(np.ascontiguousarray(o, dtype=np.float32) for o in outs)
```

```python
import numpy as np
from contextlib import ExitStack
import concourse.bass as bass
import concourse.mybir as mybir
from concourse.bass_utils import run_bass_kernel_spmd

F32 = mybir.dt.float32
BF16 = mybir.dt.bfloat16
ALU = mybir.AluOpType
AF = mybir.ActivationFunctionType

NCORES = 8
D = 1024
KC = 8
L = 2
SEQ = 2048
SEG = 512
DEC_SEQ = 32
PAST = 2048
NKEYMAX = PAST + DEC_SEQ
H = 16
QK_NOPE = 64
QK_ROPE = 32
QH = QK_NOPE + QK_ROPE
V_HEAD = 64
Q_LORA = 384
KV_LORA = 256
DFF = 2816
FC = DFF // 128
D_IN = 3 * D + Q_LORA + KV_LORA + QK_ROPE + 2 * D
O1, O2, O3 = D, 2 * D, 3 * D
O4 = O3 + Q_LORA
O5 = O4 + KV_LORA
O6 = O5 + QK_ROPE
O7 = O6 + D
EPS = 1e-6
ATTN_SCALE = float(QH ** -0.5)
ROPE_THETA = 10000.0
NSLOT = 6
NSP = 61
C_GPRE, C_GPOST, C_FPRE, C_FPOST, C_GQ, C_GKV, C_CW = 0, 8, 16, 24, 32, 35, 37


class Rec:
    ENG = ("pe", "act", "dve", "pool", "sp")

    def __init__(self):
        self.streams = {e: [] for e in self.ENG}
        self.cnt = {}
        self.waited = {e: {} for e in self.ENG}
        self.res = {}

    def _deps(self, eng, reads, writes):
        deps = {}

        def add(sk, v):
            if eng == "pe" and sk == "E_pe":
                return
            if deps.get(sk, 0) < v:
                deps[sk] = v

        for r in reads:
            st = self.res.get(r)
            if st and st[0] is not None:
                add(*st[0])
        for w in writes:
            st = self.res.get(w)
            if st:
                if st[0] is not None:
                    add(*st[0])
                for sk, v in st[1].items():
                    add(sk, v)
        wd = self.waited[eng]
        for sk, v in deps.items():
            if wd.get(sk, 0) < v:
                self.streams[eng].append(("w", sk, v))
                wd[sk] = v

    def _commit(self, t, reads, writes):
        for r in reads:
            st = self.res.setdefault(r, [None, {}])
            if st[1].get(t[0], 0) < t[1]:
                st[1][t[0]] = t[1]
        for w in writes:
            self.res[w] = [t, {}]

    def op(self, eng, fn, reads=(), writes=()):
        self._deps(eng, reads, writes)
        sk = "E_" + eng
        self.cnt[sk] = self.cnt.get(sk, 0) + 1
        t = (sk, self.cnt[sk])
        self.streams[eng].append(("i", fn, sk, 1))
        self._commit(t, reads, writes)
        return t

    def dma(self, eng, fn, semkey, reads=(), writes=()):
        self._deps(eng, reads, writes)
        self.cnt[semkey] = self.cnt.get(semkey, 0) + 16
        t = (semkey, self.cnt[semkey])
        self.streams[eng].append(("i", fn, semkey, 16))
        self._commit(t, reads, writes)
        return t

    def mmgroup(self, bank_key, mms):
        sk = "E_pe"
        final = (sk, self.cnt.get(sk, 0) + 1)
        n = len(mms)
        for i, (fn, reads) in enumerate(mms):
            self._deps("pe", reads, (bank_key,) if i == 0 else ())
            last = i == n - 1
            self.streams["pe"].append(
                ("i", (lambda e, fn=fn, st=(i == 0), sp=last: fn(e, st, sp)), sk if last else None, 1))
            for r in reads:
                st_ = self.res.setdefault(r, [None, {}])
                if st_[1].get(sk, 0) < final[1]:
                    st_[1][sk] = final[1]
        self.cnt[sk] = final[1]
        self.res[bank_key] = [final, {}]
        return final

    def mm1(self, bank_key, fn, reads, first):
        sk = "E_pe"
        self._deps("pe", reads, (bank_key,) if first else ())
        self.cnt[sk] = self.cnt.get(sk, 0) + 1
        t = (sk, self.cnt[sk])
        self.streams["pe"].append(("i", fn, sk, 1))
        for r in reads:
            st_ = self.res.setdefault(r, [None, {}])
            st_[1][sk] = t[1]
        old = self.res.get(bank_key)
        self.res[bank_key] = [t, {} if (first or not old) else old[1]]
        return t


def _replay(e, stream, semh):
    for it in stream:
        if it[0] == "w":
            e.wait_ge(semh[it[1]], it[2])
        else:
            inst = it[1](e)
            if it[2] is not None:
                inst.then_inc(semh[it[2]], it[3])


class _Stop(Exception):
    pass


def build_program(n_prompt_seq=2, n_quarters=4, with_sample=True, n_layers=L, debug=False, stop=None):
    nc = bass.Bass("TRN2", target_bir_lowering=False)
    R = Rec()

    def din(name, shape):
        return nc.dram_tensor(name, list(shape), F32, kind="ExternalInput").ap()

    def dout(name, shape):
        return nc.dram_tensor(name, list(shape), F32, kind="ExternalOutput").ap()

    x_p = din("x_p", (2, SEQ, D))
    x_s = din("x_s", (DEC_SEQ, D))
    c_ckv = din("c_ckv", (L, PAST, KV_LORA))
    c_kr = din("c_kr", (L, PAST, QK_ROPE))
    hist0 = din("hist0", (128, L * KC * 2))
    smallp_d = din("smallp", (128, L * NSP))
    rope_d = din("rope", (2, 128, NKEYMAX))
    ident_d = din("ident", (128, 128))
    w_in = din("w_in", (L, D, D_IN))
    w_uq = din("w_uq", (L, Q_LORA, H * QH))
    w_ukv = din("w_ukv", (L, KV_LORA, H * 128))
    w_conv_out = din("w_conv_out", (L, D, D))
    w_attn_out = din("w_attn_out", (L, D, D))
    w_merge = din("w_merge", (L, D, D))
    w_gate_up = din("w_gate_up", (L, D, 2 * DFF))
    w_down = din("w_down", (L, DFF, D))

    y_p = dout("y_p", (2, SEQ, D))
    y_s = dout("y_s", (DEC_SEQ, D))
    nconv_p = dout("nconv_p", (L, 2, 2, D))
    nckv_p = dout("nckv_p", (L, 2, SEQ, KV_LORA))
    nkr_p = dout("nkr_p", (L, 2, SEQ, QK_ROPE))
    nconv_s = dout("nconv_s", (L, 2, D))
    nckv_s = dout("nckv_s", (L, DEC_SEQ, KV_LORA))
    nkr_s = dout("nkr_s", (L, DEC_SEQ, QK_ROPE))
    dbg = {}
    if debug:
        for nm, shp in (("d_xn", (128, KC * SEG)), ("d_bc", (128, KC * SEG)), ("d_qn", (128, 3 * SEG)),
                        ("d_attn", (128, KC * SEG)), ("d_z", (128, KC * SEG)), ("d_xres", (128, KC * SEG)),
                        ("d_xres2", (128, KC * SEG)), ("d_qt", (128, SEG)), ("d_kt", (128, SEG))):
            dbg[nm] = dout(nm, shp)

    wv = {
        "w_in": w_in.rearrange("l (kc p) m -> l p kc m", p=128),
        "w_uq": w_uq.rearrange("l (kc p) m -> l p kc m", p=128),
        "w_ukv": w_ukv.rearrange("l (kc p) m -> l p kc m", p=128),
        "w_conv_out": w_conv_out.rearrange("l (kc p) m -> l p kc m", p=128),
        "w_attn_out": w_attn_out.rearrange("l (kc p) m -> l p kc m", p=128),
        "w_merge": w_merge.rearrange("l (kc p) m -> l p kc m", p=128),
        "w_gate_up": w_gate_up.rearrange("l (kc p) m -> l p kc m", p=128),
        "w_down": w_down.rearrange("l (kc p) m -> l p kc m", p=128),
    }

    with ExitStack() as es:
        def sb(name, shape, dt):
            return es.enter_context(nc.sbuf_tensor(name, list(shape), dt))

        xresT = sb("xresT", (128, KC, SEG), F32)
        xnT = sb("xnT", (128, KC, SEG), BF16)
        big = sb("big", (128, 24, SEG), BF16)
        qnT = sb("qnT", (128, 3, SEG), BF16)
        cT = sb("cT", (128, L, 2, NKEYMAX), BF16)
        krT = sb("krT", (128, L, NKEYMAX), BF16)
        scr = sb("scr", (128, KC, SEG), F32)
        wbuf = sb("wbuf", (128, NSLOT, KC, 128), BF16)
        wuq = sb("wuq", (128, 3, H * QH), BF16)
        wuqB = sb("wuqB", (128, 3, H, 32), BF16)
        wuqA = sb("wuqA", (128, 3, H, 32), BF16)
        wukv = sb("wukv", (128, 2, H * 128), BF16)
        wkrB = sb("wkrB", (128, KC, 32), BF16)
        KT = sb("KT", (128, 2, NKEYMAX), BF16)
        QT = sb("QT", (128, 2, SEG), BF16)
        VB = sb("VB", (128, 2, 17, 128), BF16)
        PT = sb("PT", (128, 3, SEG), BF16)
        Rr = sb("Rr", (128, SEG), F32)
        u_t = sb("u_t", (128, SEG + 2), F32)
        cv_t = sb("cv_t", (128, SEG), F32)
        cg_t = sb("cg_t", (128, SEG), F32)
        sa_t = sb("sa_t", (128, SEG), F32)
        sb_t = sb("sb_t", (128, SEG), F32)
        sg_t = sb("sg_t", (128, 2, SEG), F32)
        sq_t = sb("sq_t", (128, 2, SEG), BF16)
        rt_t = sb("rt_t", (128, SEG), F32)
        rstd_t = sb("rstd_t", (128, SEG), F32)
        kro_t = sb("kro_t", (128, SEG), F32)
        t1_t = sb("t1_t", (128, SEG), F32)
        t2_t = sb("t2_t", (128, SEG), F32)
        cos_t = sb("cos_t", (128, SEG), F32)
        sin_t = sb("sin_t", (128, SEG), F32)
        xtok = sb("xtok", (128, 2, D), F32)
        xin = sb("xin", (128, 2, D), F32)
        ost = sb("ost", (128, 2, 288), F32)
        hist = sb("hist", (128, L, KC, 2), F32)
        smallp = sb("smallp_sb", (128, L, NSP), F32)
        ident = sb("ident_sb", (128, 128), F32)
        ones = sb("ones_sb", (128, 128), BF16)
        ps = [es.enter_context(nc.psum_tensor(f"ps{i}", [128, 512], F32)) for i in range(8)]

        bcT = lambda m: big[:, m, :]
        attnT_i, zT_i = 8, 16
        qlat_i, ckvs_i, cnew_i = 0, 3, 5

        rr = {"mm": 0, "aux": 0, "acc": 0}
        MM_BANKS, AUX_BANKS, ACC_BANKS, SS_BANK = (0, 1, 2), (4, 7), (5, 6), 3

        def bank(cls):
            lst = {"mm": MM_BANKS, "aux": AUX_BANKS, "acc": ACC_BANKS}[cls]
            b = lst[rr[cls] % len(lst)]
            rr[cls] += 1
            return b

        def act(out, in_, func, reads, writes, scale=None, bias=None):
            kw = {}
            if scale is not None:
                kw["scale"] = scale
            if bias is not None:
                kw["bias"] = bias
            return R.op("act", lambda e: e.activation(out=out, in_=in_, func=func, **kw), reads, writes)

        def tt(out, in0, in1, op, reads, writes):
            return R.op("dve", lambda e: e.tensor_tensor(out=out, in0=in0, in1=in1, op=op), reads, writes)

        def stt(out, in0, scalar, in1, op0, op1, reads, writes):
            return R.op("dve", lambda e: e.scalar_tensor_tensor(out=out, in0=in0, scalar=scalar, in1=in1,
                                                               op0=op0, op1=op1), reads, writes)

        def dcopy(out, in_, reads, writes):
            return R.op("dve", lambda e: e.tensor_copy(out=out, in_=in_), reads, writes)

        def acopy(out, in_, reads, writes, scale=None):
            if scale is None:
                return R.op("act", lambda e: e.copy(out=out, in_=in_), reads, writes)
            return R.op("act", lambda e: e.mul(out=out, in_=in_, mul=scale), reads, writes)

        def mm(out, lhsT, rhs):
            return lambda e, st, sp: e.matmul(out, lhsT, rhs, start=st, stop=sp)

        def tr(out, in_, idn):
            return lambda e, st, sp: e.transpose(out, in_, idn)

        out_tickets = []

        def chk(name):
            if stop == name:
                raise _Stop()

        class WS:
            def __init__(self):
                self.sched = []
                self.emitted = 0

            def advance(self, upto):
                upto = min(upto, len(self.sched) - 1)
                while self.emitted <= upto:
                    i = self.emitted
                    (l, name, kc0, nkc, c0, ncols) = self.sched[i]
                    slot = i % NSLOT
                    src = wv[name][l, :, kc0:kc0 + nkc, c0:c0 + ncols]
                    dst = wbuf[:, slot, 0:nkc, 0:ncols]
                    R.dma("pool", lambda e, dst=dst, src=src: e.dma_start(out=dst, in_=src), f"W{slot}",
                          reads=(), writes=(("w", slot),))
                    self.emitted += 1

            def use(self, i, desc):
                assert self.sched[i] == desc, (i, self.sched[i], desc)
                assert i < self.emitted, (i, self.emitted)
                return i % NSLOT

        ws = WS()
        wpos = [0]

        def wtile(desc):
            i = wpos[0]
            wpos[0] += 1
            slot = ws.use(i, desc)
            return slot, ("w", slot)

        def wdone():
            ws.advance(wpos[0] - 1 + NSLOT)

        segs = []
        for s in range(n_prompt_seq):
            for q in range(n_quarters):
                segs.append(dict(kind="p", seq=s, q=q, N=SEG, pos0=q * SEG, key0=q * SEG))
        if with_sample:
            segs.append(dict(kind="s", seq=0, q=4, N=DEC_SEQ, pos0=PAST, key0=PAST))

        def sl_sched(l):
            out = []
            for m in range(KC):
                out.append((l, "w_in", 0, KC, O1 + 128 * m, 128))
                out.append((l, "w_in", 0, KC, O2 + 128 * m, 128))
                out.append((l, "w_in", 0, KC, 128 * m, 128))
            for i in range(3):
                out.append((l, "w_in", 0, KC, O3 + 128 * i, 128))
            for i in range(2):
                out.append((l, "w_in", 0, KC, O4 + 128 * i, 128))
            out.append((l, "w_in", 0, KC, O5, 32))
            for m in range(KC):
                out.append((l, "w_conv_out", 0, KC, 128 * m, 128))
                out.append((l, "w_in", 0, KC, O6 + 128 * m, 128))
                out.append((l, "w_attn_out", 0, KC, 128 * m, 128))
                out.append((l, "w_in", 0, KC, O7 + 128 * m, 128))
            for m in range(KC):
                out.append((l, "w_merge", 0, KC, 128 * m, 128))
            for f in range(FC):
                out.append((l, "w_gate_up", 0, KC, 128 * f, 128))
                out.append((l, "w_gate_up", 0, KC, DFF + 128 * f, 128))
            for m in range(KC):
                out.append((l, "w_down", 0, 8, 128 * m, 128))
                out.append((l, "w_down", 8, 8, 128 * m, 128))
                out.append((l, "w_down", 16, 6, 128 * m, 128))
            return out

        for sg in segs:
            for l in range(n_layers):
                ws.sched.extend(sl_sched(l))

        R.dma("sp", lambda e: e.dma_start(out=smallp[:, :, :].rearrange("p l c -> p (l c)"), in_=smallp_d[:, :]),
              "S_small", writes=("smallp",))
        R.dma("sp", lambda e: e.dma_start(out=ident[:, :], in_=ident_d[:, :]), "S_ident", writes=("ident",))
        R.op("dve", lambda e: e.memset(ones[:, :], 1.0), writes=("ones",))
        R.op("dve", lambda e: e.memset(VB[:, 0, :, 64:128], 1.0), writes=(("V", 0),))
        R.op("dve", lambda e: e.memset(VB[:, 1, :, 0:64], 1.0), writes=(("V", 1),))
        R.op("dve", lambda e: e.memset(KT[:, :, :], 0.0), writes=(("KT", 0), ("KT", 1)))
        R.op("dve", lambda e: e.memset(QT[:, :, :], 0.0), writes=(("QT", 0), ("QT", 1)))
        ws.advance(NSLOT - 1)

        def sp_col(l, c0, n=1):
            return smallp[:, l, c0:c0 + n]

        def load_wuq_wukv(l):
            for kc in range(3):
                R.dma("pool", lambda e, kc=kc: e.dma_start(out=wuq[:, kc, :], in_=wv["w_uq"][l, :, kc, :]),
                      "S_wuq", writes=("wuq",))
            for kc in range(2):
                for hf in range(2):
                    R.dma("pool", lambda e, kc=kc, hf=hf: e.dma_start(
                        out=wukv[:, kc, hf * 1024:(hf + 1) * 1024], in_=wv["w_ukv"][l, :, kc, hf * 1024:(hf + 1) * 1024]),
                        "S_wukv", writes=("wukv",))
            wv4 = wuq[:, :, :].rearrange("p k (h d) -> p k h d", d=QH)
            acopy(wuqB[:, :, :, 0:16], wv4[:, :, :, 80:96], reads=("wuq",), writes=("wuqB",), scale=-1.0)
            acopy(wuqB[:, :, :, 16:32], wv4[:, :, :, 64:80], reads=("wuq",), writes=("wuqB",))
            acopy(wuqA[:, :, :, :], wv4[:, :, :, 64:96], reads=("wuq",), writes=("wuqA",))

        def norm_finish(ssb, nfeat, N):
            act(rt_t[:, :N], ps[ssb][:, :N], AF.Ln, reads=(("ps", ssb),), writes=("rt",), scale=1.0 / nfeat,
                bias=EPS)
            act(rstd_t[:, :N], rt_t[:, :N], AF.Exp, reads=("rt",), writes=("rstd",), scale=-0.5)

        def prenorm(l, gcol, N):
            ssb = SS_BANK
            for kc in range(KC):
                b = kc % 2
                act(sq_t[:, b, :N], xresT[:, kc, :N], AF.Square, reads=(("xres", kc),), writes=(("sq", b),))
                R.mm1(("ps", ssb), (lambda e, b=b, kc=kc: e.matmul(ps[ssb][:, :N], ones[:, :], sq_t[:, b, :N],
                                                                 start=(kc == 0), stop=(kc == KC - 1))),
                      reads=(("sq", b), "ones"), first=(kc == 0))
            norm_finish(ssb, D, N)
            for kc in range(KC):
                stt(xnT[:, kc, :N], xresT[:, kc, :N], sp_col(l, gcol + kc), rstd_t[:, :N], ALU.mult, ALU.mult,
                    reads=(("xres", kc), "rstd", "smallp"), writes=(("xn", kc),))

        def dense_group(l, name, c0, ncols, rhs_fn, rhs_keys, nkc_total=KC, pieces=None):
            b = bank("mm")
            mms = []
            pieces = pieces or [(0, nkc_total)]
            for (k0, nk) in pieces:
                slot, wkey = wtile((l, name, k0, nk, c0, ncols))
                for k in range(nk):
                    kc = k0 + k
                    mms.append((mm(ps[b][0:ncols, :rhs_fn.N], wbuf[:, slot, k, 0:ncols], rhs_fn(kc)),
                                (wkey, rhs_keys(kc))))
            R.mmgroup(("ps", b), mms)
            wdone()
            return b

        class RhsFn:
            def __init__(self, fn, N):
                self.fn = fn
                self.N = N

            def __call__(self, kc):
                return self.fn(kc)

        def issue_input(sg, blk):
            if blk in sg.setdefault("in_issued", set()):
                return
            sg["in_issued"].add(blk)
            N_, pos0_ = sg["N"], sg["pos0"]
            nb = min(128, N_ - blk * 128)
            if sg["kind"] == "p":
                src = x_p[sg["seq"], pos0_ + blk * 128: pos0_ + blk * 128 + nb, :]
            else:
                src = x_s[blk * 128: blk * 128 + nb, :]
            xb = blk % 2
            R.dma("sp", lambda e: e.dma_start(out=xin[0:nb, xb, :], in_=src), f"S_xin{xb}", writes=(("xin", xb),))

        def issue_rope(sg):
            if sg.get("rope_issued"):
                return
            sg["rope_issued"] = True
            N_, pos0_ = sg["N"], sg["pos0"]
            R.dma("sp", lambda e: e.dma_start(out=cos_t[:, :N_], in_=rope_d[0, :, pos0_:pos0_ + N_]), "S_cos",
                  writes=("cos",))
            R.dma("sp", lambda e: e.dma_start(out=sin_t[:, :N_], in_=rope_d[1, :, pos0_:pos0_ + N_]), "S_sin",
                  writes=("sin",))

        def prefetch_next(sg):
            i = segs.index(sg)
            if i + 1 < len(segs):
                nx = segs[i + 1]
                for blk in range(min(2, (nx["N"] + 127) // 128)):
                    issue_input(nx, blk)
                issue_rope(nx)

        def segment_layer(sg, l, first_layer, last_layer):
            N = sg["N"]
            kind = sg["kind"]
            key0 = sg["key0"]
            pos0 = sg["pos0"]
            q = sg["q"]
            seq = sg["seq"]
            nblk = (N + 127) // 128
            xn_rhs = RhsFn(lambda kc: xnT[:, kc, :N], N)
            xn_keys = lambda kc: ("xn", kc)

            if first_layer:
                for blk in range(nblk):
                    nb = min(128, N - blk * 128)
                    issue_input(sg, blk)
                    xb = blk % 2
                    for half in range(2):
                        b = bank("mm")
                        for j in range(4):
                            kc = half * 4 + j
                            R.mmgroup(("ps", b) if j == 0 else ("psx", b), [
                                (tr(ps[b][:, j * 128: j * 128 + nb], xin[0:nb, xb, kc * 128:(kc + 1) * 128],
                                    ident[0:nb, 0:nb]), (("xin", xb), "ident"))])
                        R.res[("ps", b)] = [("E_pe", R.cnt["E_pe"]), {}]
                        src_ps = ps[b][:, :].rearrange("p (j t) -> p j t", t=128)[:, :, 0:nb]
                        acopy(xresT[:, half * 4:half * 4 + 4, blk * 128: blk * 128 + nb], src_ps,
                              reads=(("ps", b),), writes=tuple(("xres", half * 4 + j) for j in range(4)))
                issue_rope(sg)

            chk("input")
            if kind == "p" and q == 0:
                R.op("dve", lambda e: e.memset(hist[:, l, :, :], 0.0), writes=(("hist", l),))
            if kind == "s":
                R.dma("sp", lambda e: e.dma_start(
                    out=hist[:, l, :, :].rearrange("p k j -> p (k j)"), in_=hist0[:, l * 16:(l + 1) * 16]),
                    "S_hist", writes=(("hist", l),))
                for blk in range(PAST // 128):
                    ob = blk % 2
                    R.dma("sp", lambda e, blk=blk, ob=ob: e.dma_start(
                        out=ost[:, ob, 0:256], in_=c_ckv[l, blk * 128:(blk + 1) * 128, :]), f"S_ost{ob}",
                        writes=(("ost", ob),))
                    R.dma("sp", lambda e, blk=blk, ob=ob: e.dma_start(
                        out=ost[:, ob, 256:288], in_=c_kr[l, blk * 128:(blk + 1) * 128, :]), f"S_ost{ob}",
                        writes=())
                    R.res[("ost", ob)] = [(f"S_ost{ob}", R.cnt[f"S_ost{ob}"]), {}]
                    b = bank("aux")
                    R.mmgroup(("ps", b), [(tr(ps[b][:, 0:128], ost[:, ob, 0:128], ident[:, :]), (("ost", ob), "ident"))])
                    R.mmgroup(("psx", b), [(tr(ps[b][:, 128:256], ost[:, ob, 128:256], ident[:, :]), (("ost", ob), "ident"))])
                    R.mmgroup(("psx", b), [(tr(ps[b][0:32, 256:384], ost[:, ob, 256:288], ident[:, :]), (("ost", ob), "ident"))])
                    R.res[("ps", b)] = [("E_pe", R.cnt["E_pe"]), {}]
                    jt = blk // 4
                    acopy(cT[:, l, :, blk * 128:(blk + 1) * 128],
                          ps[b][:, 0:256].rearrange("p (k t) -> p k t", t=128),
                          reads=(("ps", b),), writes=(("cT", l, jt),))
                    acopy(krT[64:96, l, blk * 128:(blk + 1) * 128], ps[b][0:32, 256:384],
                          reads=(("ps", b),), writes=(("krT", l, jt),))

            chk("hist")
            load_wuq_wukv(l)

            chk("wuq")
            prenorm(l, C_GPRE, N)

            chk("prenorm")
            for m in range(KC):
                b_cg = dense_group(l, "w_in", O1 + 128 * m, 128, xn_rhs, xn_keys)
                b_xi = dense_group(l, "w_in", O2 + 128 * m, 128, xn_rhs, xn_keys)
                acopy(cg_t[:, :N], ps[b_cg][:, :N], reads=(("ps", b_cg),), writes=("cg",))
                dcopy(u_t[:, 0:2], hist[:, l, m, :], reads=(("hist", l),), writes=("u",))
                tt(u_t[:, 2:2 + N], ps[b_xi][:, :N], cg_t[:, :N], ALU.mult, reads=(("ps", b_xi), "cg", "u"),
                   writes=("u",))
                cws = [sp_col(l, C_CW + 3 * m + j) for j in range(3)]
                cw = lambda j, cws=cws: cws[j]
                R.op("dve", lambda e, c2=cws[2]: e.tensor_scalar(out=cv_t[:, :N], in0=u_t[:, 2:2 + N], scalar1=c2,
                                                                 scalar2=None, op0=ALU.mult),
                     reads=("u", "smallp"), writes=("cv",))
                stt(cv_t[:, :N], u_t[:, 1:1 + N], cw(1), cv_t[:, :N], ALU.mult, ALU.add, reads=("u", "cv", "smallp"),
                    writes=("cv",))
                stt(cv_t[:, :N], u_t[:, 0:N], cw(0), cv_t[:, :N], ALU.mult, ALU.add, reads=("u", "cv", "smallp"),
                    writes=("cv",))
                dcopy(hist[:, l, m, :], u_t[:, N:N + 2], reads=("u",), writes=(("hist", l),))
                b_bg = dense_group(l, "w_in", 128 * m, 128, xn_rhs, xn_keys)
                tt(big[:, m, :N], ps[b_bg][:, :N], cv_t[:, :N], ALU.mult, reads=(("ps", b_bg), "cv"),
                   writes=(("big", m),))
            if (kind == "p" and q == n_quarters - 1) or kind == "s":
                if kind == "p":
                    dst = nconv_p[l, seq, :, :].rearrange("j (k p) -> p k j", p=128)
                else:
                    dst = nconv_s[l, :, :].rearrange("j (k p) -> p k j", p=128)
                for kc in range(KC):
                    out_tickets.append(R.dma("sp", lambda e, dst=dst, kc=kc: e.dma_start(
                        out=dst[:, kc, :], in_=hist[:, l, kc, :], allow_slow_non_contiguous=True), f"S_histout{l}",
                        reads=(("hist", l),)))

            chk("conv")
            for i in range(3):
                bq = dense_group(l, "w_in", O3 + 128 * i, 128, xn_rhs, xn_keys)
                acopy(scr[:, qlat_i + i, :N], ps[bq][:, :N], reads=(("ps", bq),), writes=(("scr", qlat_i + i),))
            for i in range(3):
                b = i % 2
                act(sq_t[:, b, :N], scr[:, qlat_i + i, :N], AF.Square, reads=(("scr", qlat_i + i),),
                    writes=(("sq", b),))
                R.mm1(("ps", SS_BANK), (lambda e, b=b, i=i: e.matmul(ps[SS_BANK][:, :N], ones[:, :], sq_t[:, b, :N],
                                                                    start=(i == 0), stop=(i == 2))),
                      reads=(("sq", b), "ones"), first=(i == 0))
            norm_finish(SS_BANK, Q_LORA, N)
            for i in range(3):
                stt(qnT[:, i, :N], scr[:, qlat_i + i, :N], sp_col(l, C_GQ + i), rstd_t[:, :N], ALU.mult, ALU.mult,
                    reads=(("scr", qlat_i + i), "rstd", "smallp"), writes=(("qn", i),))
            for i in range(2):
                bq = dense_group(l, "w_in", O4 + 128 * i, 128, xn_rhs, xn_keys)
                acopy(scr[:, ckvs_i + i, :N], ps[bq][:, :N], reads=(("ps", bq),), writes=(("scr", ckvs_i + i),))
            for i in range(2):
                b = i % 2
                act(sq_t[:, b, :N], scr[:, ckvs_i + i, :N], AF.Square, reads=(("scr", ckvs_i + i),),
                    writes=(("sq", b),))
                R.mm1(("ps", SS_BANK), (lambda e, b=b, i=i: e.matmul(ps[SS_BANK][:, :N], ones[:, :], sq_t[:, b, :N],
                                                                    start=(i == 0), stop=(i == 1))),
                      reads=(("sq", b), "ones"), first=(i == 0))
            norm_finish(SS_BANK, KV_LORA, N)
            jt_new = q
            for i in range(2):
                stt(scr[:, cnew_i + i, :N], scr[:, ckvs_i + i, :N], sp_col(l, C_GKV + i), rstd_t[:, :N], ALU.mult,
                    ALU.mult, reads=(("scr", ckvs_i + i), "rstd", "smallp"), writes=(("scr", cnew_i + i),))
                acopy(cT[:, l, i, key0:key0 + N], scr[:, cnew_i + i, :N], reads=(("scr", cnew_i + i),),
                      writes=(("cT", l, jt_new),))
            slot, wkey = wtile((l, "w_in", 0, KC, O5, 32))
            acopy(wkrB[:, :, 0:16], wbuf[:, slot, :, 16:32], reads=(wkey,), writes=("wkrB",), scale=-1.0)
            acopy(wkrB[:, :, 16:32], wbuf[:, slot, :, 0:16], reads=(wkey,), writes=("wkrB",))
            bA = bank("aux")
            R.mmgroup(("ps", bA), [(mm(ps[bA][0:32, :N], wbuf[:, slot, kc, 0:32], xnT[:, kc, :N]),
                                    (wkey, ("xn", kc))) for kc in range(KC)])
            wdone()
            bB = bank("aux")
            R.mmgroup(("ps", bB), [(mm(ps[bB][0:32, :N], wkrB[:, kc, :], xnT[:, kc, :N]), ("wkrB", ("xn", kc)))
                                   for kc in range(KC)])
            tt(t1_t[0:32, :N], ps[bA][0:32, :N], cos_t[0:32, :N], ALU.mult, reads=(("ps", bA), "cos"), writes=("t1",))
            tt(t2_t[0:32, :N], ps[bB][0:32, :N], sin_t[0:32, :N], ALU.mult, reads=(("ps", bB), "sin"), writes=("t2",))
            tt(kro_t[0:32, :N], t1_t[0:32, :N], t2_t[0:32, :N], ALU.add, reads=("t1", "t2"), writes=("kro",))
            acopy(krT[64:96, l, key0:key0 + N], kro_t[0:32, :N], reads=("kro",), writes=(("krT", l, jt_new),))
            chk("lowrank")
            for blk in range(nblk):
                nb = min(128, N - blk * 128)
                ob = blk % 2
                b = bank("aux")
                R.mmgroup(("ps", b), [(tr(ps[b][0:nb, 0:128], scr[:, cnew_i, blk * 128:blk * 128 + nb], ident[:, :]),
                                       (("scr", cnew_i), "ident"))])
                R.mmgroup(("psx", b), [(tr(ps[b][0:nb, 128:256], scr[:, cnew_i + 1, blk * 128:blk * 128 + nb],
                                          ident[:, :]), (("scr", cnew_i + 1), "ident"))])
                R.mmgroup(("psx", b), [(tr(ps[b][0:nb, 256:288], kro_t[0:32, blk * 128:blk * 128 + nb],
                                          ident[0:32, 0:32]), ("kro", "ident"))])
                R.res[("ps", b)] = [("E_pe", R.cnt["E_pe"]), {}]
                dcopy(ost[0:nb, ob, :], ps[b][0:nb, 0:288], reads=(("ps", b),), writes=(("ost", ob),))
                if kind == "p":
                    r0 = pos0 + blk * 128
                    d1 = nckv_p[l, seq, r0:r0 + nb, :]
                    d2 = nkr_p[l, seq, r0:r0 + nb, :]
                else:
                    d1 = nckv_s[l, 0:nb, :]
                    d2 = nkr_s[l, 0:nb, :]
                out_tickets.append(R.dma("sp", lambda e, d1=d1, ob=ob, nb=nb: e.dma_start(
                    out=d1, in_=ost[0:nb, ob, 0:256]), f"S_ost{ob}", reads=(("ost", ob),)))
                out_tickets.append(R.dma("sp", lambda e, d2=d2, ob=ob, nb=nb: e.dma_start(
                    out=d2, in_=ost[0:nb, ob, 256:288]), f"S_ost{ob}", reads=(("ost", ob),)))

            chk("ctxout")
            nkeys = key0 + N
            ktiles = [(j * 512, min(512, nkeys - j * 512)) for j in range((nkeys + 511) // 512)]
            kblocks = [(j * 128, min(128, nkeys - j * 128)) for j in range((nkeys + 127) // 128)]
            for hb in range(2):
                for jt, (k0, nk) in enumerate(ktiles):
                    dcopy(KT[64:96, hb, k0:k0 + nk], krT[64:96, l, k0:k0 + nk], reads=(("krT", l, jt),),
                          writes=(("KT", hb),))
            b4 = SS_BANK

            def prep_pieces(h):
                hb = h % 2
                eo = h % 2
                pieces = []
                ev = acopy if len(kblocks) <= 8 else dcopy
                for jt, (k0, nk) in enumerate(ktiles):
                    def p_kt(jt=jt, k0=k0, nk=nk):
                        b = bank("aux")
                        R.mmgroup(("ps", b), [(mm(ps[b][0:64, :nk], wukv[:, kc, h * 128:h * 128 + 64],
                                                  cT[:, l, kc, k0:k0 + nk]), ("wukv", ("cT", l, jt)))
                                              for kc in range(2)])
                        ev(KT[0:64, hb, k0:k0 + nk], ps[b][0:64, :nk], reads=(("ps", b),), writes=(("KT", hb),))
                    pieces.append(p_kt)
                for g0 in range(0, len(kblocks), 8):
                    def p_v(g0=g0):
                        grp = kblocks[g0:g0 + 8]
                        b = bank("aux")
                        for j, (k0, nk) in enumerate(grp):
                            R.mmgroup(("ps", b) if j == 0 else ("psx", b), [
                                (mm(ps[b][0:nk, j * 64:(j + 1) * 64], cT[:, l, kc, k0:k0 + nk],
                                    wukv[:, kc, h * 128 + 64:h * 128 + 128]), ("wukv", ("cT", l, k0 // 512)))
                                for kc in range(2)])
                        R.res[("ps", b)] = [("E_pe", R.cnt["E_pe"]), {}]
                        vc0 = 0 if eo == 0 else 64
                        full = [g for g in grp if g[1] == 128]
                        if full:
                            nf = len(full)
                            ev(VB[:, eo, g0:g0 + nf, vc0:vc0 + 64],
                                  ps[b][:, 0:nf * 64].rearrange("p (j v) -> p j v", v=64),
                                  reads=(("ps", b),), writes=(("V", eo),))
                        if len(full) < len(grp):
                            j = len(full)
                            nk = grp[j][1]
                            ev(VB[0:nk, eo, g0 + j, vc0:vc0 + 64], ps[b][0:nk, j * 64:(j + 1) * 64],
                                  reads=(("ps", b),), writes=(("V", eo),))
                    pieces.append(p_v)

                def p_rot4():
                    g = h // 4
                    bA4 = bank("aux")
                    R.mmgroup(("ps", bA4), [(mm(ps[bA4][:, :N], wuqA[:, kc, 4 * g:4 * g + 4, :], qnT[:, kc, :N]),
                                             ("wuqA", ("qn", kc))) for kc in range(3)])
                    R.mmgroup(("ps", b4), [(mm(ps[b4][:, :N], wuqB[:, kc, 4 * g:4 * g + 4, :], qnT[:, kc, :N]),
                                            ("wuqB", ("qn", kc))) for kc in range(3)])
                    tt(t1_t[:, :N], ps[bA4][:, :N], cos_t[:, :N], ALU.mult, reads=(("ps", bA4), "cos"), writes=("t1",))
                    tt(t2_t[:, :N], ps[b4][:, :N], sin_t[:, :N], ALU.mult, reads=(("ps", b4), "sin"), writes=("t2",))
                    tt(kro_t[:, :N], t1_t[:, :N], t2_t[:, :N], ALU.add, reads=("t1", "t2"), writes=("kro",))

                def p_q():
                    bA = bank("aux")
                    R.mmgroup(("ps", bA), [(mm(ps[bA][0:64, :N], wuq[:, kc, h * QH:h * QH + 64], qnT[:, kc, :N]),
                                            ("wuq", ("qn", kc))) for kc in range(3)])
                    ev(QT[0:64, hb, :N], ps[bA][0:64, :N], reads=(("ps", bA),), writes=(("QT", hb),))
                    j4 = h % 4
                    R.op("pool", lambda e: e.tensor_copy(out=QT[64:96, hb, :N], in_=kro_t[32 * j4:32 * j4 + 32, :N]),
                         reads=("kro",), writes=(("QT", hb),))
                    if debug and h == 0 and l == 0 and sg is segs[0]:
                        R.dma("pool", lambda e: e.dma_start(out=dbg["d_qt"][:, :], in_=QT[:, 0, :]), "S_dbg",
                              reads=(("QT", 0),))
                        R.dma("pool", lambda e: e.dma_start(out=dbg["d_kt"][:, :], in_=KT[:, 0, 0:SEG]), "S_dbg",
                              reads=(("KT", 0),))
                if h % 4 == 0:
                    pieces.insert(0, p_rot4)
                    pieces.insert(1, p_q)
                else:
                    pieces.insert(0, p_q)
                return pieces

            def head_loop(h, pieces):
                hb = h % 2
                eo = h % 2
                accb = bank("acc")
                nkb = len(kblocks)
                info = []
                for kb, (k0, nk) in enumerate(kblocks):
                    if kind == "p" and k0 >= key0:
                        bd = (k0 - key0) // 128
                        info.append((k0, nk, bd, 128 * bd))
                    else:
                        info.append((k0, nk, None, 0))
                sbanks = {}

                def issue_s(kb):
                    k0, nk, bd, qlo = info[kb]
                    sbk = bank("mm")
                    R.mmgroup(("ps", sbk), [(mm(ps[sbk][0:nk, qlo:N], KT[0:QH, hb, k0:k0 + nk], QT[0:QH, hb, qlo:N]),
                                             (("KT", hb), ("QT", hb)))])
                    sbanks[kb] = sbk

                for kb in range(min(2, nkb)):
                    issue_s(kb)
                for kb in range(nkb):
                    k0, nk, bd, qlo = info[kb]
                    sbk = sbanks[kb]
                    pb = kb % 3
                    act(PT[0:nk, pb, qlo:N], ps[sbk][0:nk, qlo:N], AF.Exp, reads=(("ps", sbk),),
                        writes=(("PT", pb),), scale=ATTN_SCALE)
                    def pv(c0, c1, r1, first, last, kb=kb, pb=pb):
                        o_ap, l_ap, r_ap = ps[accb][:, c0:c1], VB[0:r1, eo, kb, :], PT[0:r1, pb, c0:c1]
                        R.mm1(("ps", accb), (lambda e: e.matmul(o_ap, l_ap, r_ap, start=first, stop=last)),
                              reads=(("V", eo), ("PT", pb)), first=first)
                    if bd is None:
                        pv(qlo, N, nk, kb == 0, kb == nkb - 1)
                    else:
                        pv(qlo + 64, N, nk, kb == 0, False)
                        pv(qlo, qlo + 64, 64, False, kb == nkb - 1)
                    if kb + 2 < nkb:
                        issue_s(kb + 2)
                    if pieces:
                        pieces.pop(0)()
                while pieces:
                    pieces.pop(0)()
                dlo, slo = (0, 64) if eo == 0 else (64, 0)
                act(Rr[dlo:dlo + 64, :N], ps[accb][slo:slo + 64, :N], AF.Ln, reads=(("ps", accb),), writes=("R",))
                act(Rr[dlo:dlo + 64, :N], Rr[dlo:dlo + 64, :N], AF.Exp, reads=("R",), writes=("R",), scale=-1.0)
                tt(big[dlo:dlo + 64, attnT_i + h // 2, :N], ps[accb][dlo:dlo + 64, :N], Rr[dlo:dlo + 64, :N], ALU.mult,
                   reads=(("ps", accb), "R"), writes=(("big", attnT_i + h // 2),))

            for p in prep_pieces(0):
                p()
            for h in range(H):
                nxt = prep_pieces(h + 1) if h + 1 < H else []
                head_loop(h, nxt)

            chk("attn")
            bc_rhs = RhsFn(lambda kc: big[:, kc, :N], N)
            bc_keys = lambda kc: ("big", kc)
            at_rhs = RhsFn(lambda kc: big[:, attnT_i + kc, :N], N)
            at_keys = lambda kc: ("big", attnT_i + kc)
            for m in range(KC):
                b_ya = dense_group(l, "w_conv_out", 128 * m, 128, bc_rhs, bc_keys)
                b_ga = dense_group(l, "w_in", O6 + 128 * m, 128, xn_rhs, xn_keys)
                act(sa_t[:, :N], ps[b_ga][:, :N], AF.Sigmoid, reads=(("ps", b_ga),), writes=("sa",))
                tt(sa_t[:, :N], ps[b_ya][:, :N], sa_t[:, :N], ALU.mult, reads=(("ps", b_ya), "sa"), writes=("sa",))
                b_yb = dense_group(l, "w_attn_out", 128 * m, 128, at_rhs, at_keys)
                b_gb = dense_group(l, "w_in", O7 + 128 * m, 128, xn_rhs, xn_keys)
                act(sb_t[:, :N], ps[b_gb][:, :N], AF.Sigmoid, reads=(("ps", b_gb),), writes=("sb",))
                tt(sb_t[:, :N], ps[b_yb][:, :N], sb_t[:, :N], ALU.mult, reads=(("ps", b_yb), "sb"), writes=("sb",))
                tt(big[:, zT_i + m, :N], sa_t[:, :N], sb_t[:, :N], ALU.add, reads=("sa", "sb"),
                   writes=(("big", zT_i + m),))

            chk("z")
            def postnorm_residual(groups_fn, gcol):
                def ss_mm(m):
                    b = m % 2
                    R.mm1(("ps", SS_BANK), (lambda e, b=b, m=m: e.matmul(ps[SS_BANK][:, :N], ones[:, :],
                                                                        sq_t[:, b, :N], start=(m == 0),
                                                                        stop=(m == KC - 1))),
                          reads=(("sq", b), "ones"), first=(m == 0))
                for m in range(KC):
                    bm = groups_fn(m)
                    if m > 0:
                        ss_mm(m - 1)
                    b = m % 2
                    dcopy(scr[:, m, :N], ps[bm][:, :N], reads=(("ps", bm),), writes=(("scr", m),))
                    act(sq_t[:, b, :N], scr[:, m, :N], AF.Square, reads=(("scr", m),), writes=(("sq", b),))
                ss_mm(KC - 1)
                chk("pn_a")
                norm_finish(SS_BANK, D, N)
                chk("pn_b")
                for m in range(KC):
                    stt(scr[:, m, :N], scr[:, m, :N], sp_col(l, gcol + m), rstd_t[:, :N], ALU.mult, ALU.mult,
                        reads=(("scr", m), "rstd", "smallp"), writes=(("scr", m),))
                    tt(xresT[:, m, :N], xresT[:, m, :N], scr[:, m, :N], ALU.add, reads=(("xres", m), ("scr", m)),
                       writes=(("xres", m),))

            z_rhs = RhsFn(lambda kc: big[:, zT_i + kc, :N], N)
            z_keys = lambda kc: ("big", zT_i + kc)
            postnorm_residual(lambda m: dense_group(l, "w_merge", 128 * m, 128, z_rhs, z_keys), C_GPOST)
            if debug and l == 0 and sg is segs[0]:
                def dump(nm, t_, c0, n_, keyf, eng="pool"):
                    for k in range(n_):
                        R.dma(eng, lambda e, k=k: e.dma_start(out=dbg[nm][:, k * SEG:(k + 1) * SEG], in_=t_[:, c0 + k, :]),
                              "S_dbg", reads=(keyf(c0 + k),))
                dump("d_xn", xnT, 0, KC, lambda k: ("xn", k))
                dump("d_bc", big, 0, 8, lambda k: ("big", k))
                dump("d_qn", qnT, 0, 3, lambda k: ("qn", k))
                dump("d_attn", big, 8, 8, lambda k: ("big", k))
                dump("d_z", big, 16, 8, lambda k: ("big", k))
                dump("d_xres", xresT, 0, KC, lambda k: ("xres", k), "sp")
            chk("merge")
            if last_layer:
                prefetch_next(sg)
            prenorm(l, C_FPRE, N)
            for f in range(FC):
                b_g = dense_group(l, "w_gate_up", 128 * f, 128, xn_rhs, xn_keys)
                b_u = dense_group(l, "w_gate_up", DFF + 128 * f, 128, xn_rhs, xn_keys)
                sgb = f % 2
                act(sg_t[:, sgb, :N], ps[b_g][:, :N], AF.Silu, reads=(("ps", b_g),), writes=(("sg", sgb),))
                tt(big[:, f, :N], ps[b_u][:, :N], sg_t[:, sgb, :N], ALU.mult, reads=(("ps", b_u), ("sg", sgb)),
                   writes=(("big", f),))
            chk("ffn")
            h_rhs = RhsFn(lambda kc: big[:, kc, :N], N)
            h_keys = lambda kc: ("big", kc)
            postnorm_residual(lambda m: dense_group(l, "w_down", 128 * m, 128, h_rhs, h_keys,
                                                    pieces=[(0, 8), (8, 8), (16, 6)]), C_FPOST)

            chk("down")
            if debug and l == 0 and sg is segs[0]:
                for k in range(KC):
                    R.dma("sp", lambda e, k=k: e.dma_start(out=dbg["d_xres2"][:, k * SEG:(k + 1) * SEG], in_=xresT[:, k, :]),
                          "S_dbg", reads=(("xres", k),))
            if last_layer:
                for blk in range(nblk):
                    nb = min(128, N - blk * 128)
                    for half in range(2):
                        b = bank("mm")
                        for j in range(4):
                            kc = half * 4 + j
                            R.mmgroup(("ps", b) if j == 0 else ("psx", b), [
                                (tr(ps[b][0:nb, j * 128:(j + 1) * 128], xresT[:, kc, blk * 128:blk * 128 + nb],
                                    ident[:, :]), (("xres", kc), "ident"))])
                        R.res[("ps", b)] = [("E_pe", R.cnt["E_pe"]), {}]
                        acopy(xtok[0:nb, blk % 2, half * 512:(half + 1) * 512], ps[b][0:nb, :], reads=(("ps", b),),
                              writes=(("xtok", blk % 2),))
                    if kind == "p":
                        r0 = pos0 + blk * 128
                        dst = y_p[seq, r0:r0 + nb, :]
                    else:
                        dst = y_s[blk * 128:blk * 128 + nb, :]
                    out_tickets.append(R.dma("sp", lambda e, dst=dst, nb=nb, ob=blk % 2: e.dma_start(
                        out=dst, in_=xtok[0:nb, ob, :]), f"S_xtok{blk % 2}", reads=(("xtok", blk % 2),)))

        try:
            for sg in segs:
                for l in range(n_layers):
                    segment_layer(sg, l, first_layer=(l == 0), last_layer=(l == n_layers - 1))
            assert wpos[0] == len(ws.sched), (wpos[0], len(ws.sched))
        except _Stop:
            pass
        for sk in list(R.cnt.keys()):
            if not sk.startswith("E_"):
                R.streams["sp"].append(("w", sk, R.cnt[sk]))

        fin = {}
        for (sk, v) in out_tickets:
            fin[sk] = max(fin.get(sk, 0), v)
        if "S_dbg" in R.cnt:
            fin["S_dbg"] = R.cnt["S_dbg"]
        for sk, v in fin.items():
            R.streams["sp"].append(("w", sk, v))

        semh = {}
        for sk in sorted(R.cnt.keys()):
            semh[sk] = es.enter_context(nc.semaphore(sk))
        with nc.Block() as block:
            @block.tensor
            def _(e):
                _replay(e, R.streams["pe"], semh)

            @block.scalar
            def _(e):
                _replay(e, R.streams["act"], semh)

            @block.vector
            def _(e):
                _replay(e, R.streams["dve"], semh)

            @block.gpsimd
            def _(e):
                _replay(e, R.streams["pool"], semh)

            @block.sync
            def _(e):
                _replay(e, R.streams["sp"], semh)
    stats = {k: len(v) for k, v in R.streams.items()}
    return nc, stats


def _host_constants():
    half = QK_ROPE // 2
    inv = ROPE_THETA ** (-np.arange(half, dtype=np.float32) / np.float32(half))
    pos = np.arange(NKEYMAX, dtype=np.float32)
    ang = pos[None, :] * inv[:, None].astype(np.float32)
    cos = np.cos(ang).astype(np.float32)
    sin = np.sin(ang).astype(np.float32)
    idx = np.arange(128) % half
    rope = np.stack([cos[idx], sin[idx]], axis=0).astype(np.float32)
    ident = np.eye(128, dtype=np.float32)
    return rope, ident


def _feat_major(v, nchunk):
    return np.ascontiguousarray(v.reshape(nchunk, 128).T)


def make_in_maps(inputs):
    f = lambda k: np.ascontiguousarray(np.asarray(inputs[k], dtype=np.float32))
    x_prompt, x_sample = f("x_prompt"), f("x_sample")
    state_conv, cache_ckv, cache_krope = f("state_conv"), f("cache_ckv"), f("cache_krope")
    rope, ident = _host_constants()
    smallp = np.zeros((128, L, NSP), np.float32)
    for l in range(L):
        smallp[:, l, C_GPRE:C_GPRE + 8] = _feat_major(f("norm_attn_pre")[l], 8)
        smallp[:, l, C_GPOST:C_GPOST + 8] = _feat_major(f("norm_attn_post")[l], 8)
        smallp[:, l, C_FPRE:C_FPRE + 8] = _feat_major(f("norm_ffn_pre")[l], 8)
        smallp[:, l, C_FPOST:C_FPOST + 8] = _feat_major(f("norm_ffn_post")[l], 8)
        smallp[:, l, C_GQ:C_GQ + 3] = _feat_major(f("norm_q")[l], 3)
        smallp[:, l, C_GKV:C_GKV + 2] = _feat_major(f("norm_kv")[l], 2)
        cw = f("conv_w")[l]
        for j in range(3):
            smallp[:, l, C_CW + j:C_CW + 24:3] = _feat_major(cw[j], 8)
    shared = dict(smallp=smallp.reshape(128, L * NSP), rope=rope, ident=ident,
                  w_in=f("w_in"), w_uq=f("w_uq"), w_ukv=f("w_ukv"), w_conv_out=f("w_conv_out"),
                  w_attn_out=f("w_attn_out"), w_merge=f("w_merge"), w_gate_up=f("w_gate_up"), w_down=f("w_down"))
    maps = []
    for c in range(NCORES):
        h0 = np.zeros((128, L, KC, 2), np.float32)
        for l in range(L):
            for j in range(2):
                h0[:, l, :, j] = _feat_major(state_conv[l, c, j], 8)
        m = dict(shared)
        m.update(x_p=np.ascontiguousarray(x_prompt[2 * c:2 * c + 2]), x_s=np.ascontiguousarray(x_sample[c]),
                 c_ckv=np.ascontiguousarray(cache_ckv[:, c]), c_kr=np.ascontiguousarray(cache_krope[:, c]),
                 hist0=h0.reshape(128, L * KC * 2))
        maps.append(m)
    return maps


_PROG = None


def kernel(**inputs):
    global _PROG
    if _PROG is None:
        _PROG = build_program()[0]
    in_maps = make_in_maps(inputs)
    res = run_bass_kernel_spmd(_PROG, in_maps, core_ids=list(range(NCORES)))
    rs = res.results
    B = 2 * NCORES
    y_prompt = np.concatenate([r["y_p"] for r in rs], axis=0)
    y_sample = np.stack([r["y_s"] for r in rs], axis=0)
    nconv_p = np.concatenate([r["nconv_p"] for r in rs], axis=1)
    nckv_p = np.concatenate([r["nckv_p"] for r in rs], axis=1)
    nkr_p = np.concatenate([r["nkr_p"] for r in rs], axis=1)
    nconv_s = np.stack([r["nconv_s"] for r in rs], axis=1)
    nckv_s = np.stack([r["nckv_s"] for r in rs], axis=1)
    nkr_s = np.stack([r["nkr_s"] for r in rs], axis=1)
    outs = (y_prompt, y_sample, nconv_p, nckv_p, nkr_p, nconv_s, nckv_s, nkr_s)
    return tuple(np.ascontiguousarray(o, dtype=np.float32) for o in outs)
```

```python
import numpy as np
from contextlib import ExitStack
import concourse.bass as bass
import concourse.mybir as mybir
from concourse.bass_utils import run_bass_kernel_spmd

F32 = mybir.dt.float32
BF16 = mybir.dt.bfloat16
ALU = mybir.AluOpType
AF = mybir.ActivationFunctionType

NCORES = 8
D = 1024
KC = 8
L = 2
SEQ = 2048
SEG = 512
DEC_SEQ = 32
PAST = 2048
NKEYMAX = PAST + DEC_SEQ
H = 16
QK_NOPE = 64
QK_ROPE = 32
QH = QK_NOPE + QK_ROPE
V_HEAD = 64
Q_LORA = 384
KV_LORA = 256
DFF = 2816
FC = DFF // 128
D_IN = 3 * D + Q_LORA + KV_LORA + QK_ROPE + 2 * D
O1, O2, O3 = D, 2 * D, 3 * D
O4 = O3 + Q_LORA
O5 = O4 + KV_LORA
O6 = O5 + QK_ROPE
O7 = O6 + D
EPS = 1e-6
ATTN_SCALE = float(QH ** -0.5)
ROPE_THETA = 10000.0
NSLOT = 10
NSP = 61
C_GPRE, C_GPOST, C_FPRE, C_FPOST, C_GQ, C_GKV, C_CW = 0, 8, 16, 24, 32, 35, 37


class Rec:
    ENG = ("pe", "act", "dve", "pool", "sp")

    def __init__(self):
        self.streams = {e: [] for e in self.ENG}
        self.cnt = {}
        self.waited = {e: {} for e in self.ENG}
        self.res = {}

    def _deps(self, eng, reads, writes):
        deps = {}

        def add(sk, v):
            if eng == "pe" and sk == "E_pe":
                return
            if deps.get(sk, 0) < v:
                deps[sk] = v

        for r in reads:
            st = self.res.get(r)
            if st and st[0] is not None:
                add(*st[0])
        for w in writes:
            st = self.res.get(w)
            if st:
                if st[0] is not None:
                    add(*st[0])
                for sk, v in st[1].items():
                    add(sk, v)
        wd = self.waited[eng]
        for sk, v in deps.items():
            if wd.get(sk, 0) < v:
                self.streams[eng].append(("w", sk, v))
                wd[sk] = v

    def _commit(self, t, reads, writes):
        for r in reads:
            st = self.res.setdefault(r, [None, {}])
            if st[1].get(t[0], 0) < t[1]:
                st[1][t[0]] = t[1]
        for w in writes:
            self.res[w] = [t, {}]

    def op(self, eng, fn, reads=(), writes=()):
        self._deps(eng, reads, writes)
        sk = "E_" + eng
        self.cnt[sk] = self.cnt.get(sk, 0) + 1
        t = (sk, self.cnt[sk])
        self.streams[eng].append(("i", fn, sk, 1))
        self._commit(t, reads, writes)
        return t

    def dma(self, eng, fn, semkey, reads=(), writes=()):
        self._deps(eng, reads, writes)
        self.cnt[semkey] = self.cnt.get(semkey, 0) + 16
        t = (semkey, self.cnt[semkey])
        self.streams[eng].append(("i", fn, semkey, 16))
        self._commit(t, reads, writes)
        return t

    def mmgroup(self, bank_key, mms):
        sk = "E_pe"
        final = (sk, self.cnt.get(sk, 0) + 1)
        n = len(mms)
        for i, (fn, reads) in enumerate(mms):
            self._deps("pe", reads, (bank_key,) if i == 0 else ())
            last = i == n - 1
            self.streams["pe"].append(
                ("i", (lambda e, fn=fn, st=(i == 0), sp=last: fn(e, st, sp)), sk if last else None, 1))
            for r in reads:
                st_ = self.res.setdefault(r, [None, {}])
                if st_[1].get(sk, 0) < final[1]:
                    st_[1][sk] = final[1]
        self.cnt[sk] = final[1]
        self.res[bank_key] = [final, {}]
        return final

    def mm1(self, bank_key, fn, reads, first):
        sk = "E_pe"
        self._deps("pe", reads, (bank_key,) if first else ())
        self.cnt[sk] = self.cnt.get(sk, 0) + 1
        t = (sk, self.cnt[sk])
        self.streams["pe"].append(("i", fn, sk, 1))
        for r in reads:
            st_ = self.res.setdefault(r, [None, {}])
            st_[1][sk] = t[1]
        old = self.res.get(bank_key)
        self.res[bank_key] = [t, {} if (first or not old) else old[1]]
        return t


def _replay(e, stream, semh):
    for it in stream:
        if it[0] == "w":
            e.wait_ge(semh[it[1]], it[2])
        else:
            inst = it[1](e)
            if it[2] is not None:
                inst.then_inc(semh[it[2]], it[3])


class _Stop(Exception):
    pass


def build_program(n_prompt_seq=2, n_quarters=4, with_sample=True, n_layers=L, debug=False, stop=None):
    nc = bass.Bass("TRN2", target_bir_lowering=False)
    R = Rec()

    def din(name, shape):
        return nc.dram_tensor(name, list(shape), F32, kind="ExternalInput").ap()

    def dout(name, shape):
        return nc.dram_tensor(name, list(shape), F32, kind="ExternalOutput").ap()

    x_p = din("x_p", (2, SEQ, D))
    x_s = din("x_s", (DEC_SEQ, D))
    c_ckv = din("c_ckv", (L, PAST, KV_LORA))
    c_kr = din("c_kr", (L, PAST, QK_ROPE))
    hist0 = din("hist0", (128, L * KC * 2))
    smallp_d = din("smallp", (128, L * NSP))
    rope_d = din("rope", (2, 128, NKEYMAX))
    ident_d = din("ident", (128, 128))
    w_in = din("w_in", (L, D, D_IN))
    w_uq = din("w_uq", (L, Q_LORA, H * QH))
    w_ukv = din("w_ukv", (L, KV_LORA, H * 128))
    w_conv_out = din("w_conv_out", (L, D, D))
    w_attn_out = din("w_attn_out", (L, D, D))
    w_merge = din("w_merge", (L, D, D))
    w_gate_up = din("w_gate_up", (L, D, 2 * DFF))
    w_down = din("w_down", (L, DFF, D))

    y_p = dout("y_p", (2, SEQ, D))
    y_s = dout("y_s", (DEC_SEQ, D))
    nconv_p = dout("nconv_p", (L, 2, 2, D))
    nckv_p = dout("nckv_p", (L, 2, SEQ, KV_LORA))
    nkr_p = dout("nkr_p", (L, 2, SEQ, QK_ROPE))
    nconv_s = dout("nconv_s", (L, 2, D))
    nckv_s = dout("nckv_s", (L, DEC_SEQ, KV_LORA))
    nkr_s = dout("nkr_s", (L, DEC_SEQ, QK_ROPE))
    dbg = {}
    if debug:
        for nm, shp in (("d_xn", (128, KC * SEG)), ("d_bc", (128, KC * SEG)), ("d_qn", (128, 3 * SEG)),
                        ("d_attn", (128, KC * SEG)), ("d_z", (128, KC * SEG)), ("d_xres", (128, KC * SEG)),
                        ("d_xres2", (128, KC * SEG)), ("d_qt", (128, SEG)), ("d_kt", (128, SEG))):
            dbg[nm] = dout(nm, shp)

    wv = {
        "w_in": w_in.rearrange("l (kc p) m -> l p kc m", p=128),
        "w_uq": w_uq.rearrange("l (kc p) m -> l p kc m", p=128),
        "w_ukv": w_ukv.rearrange("l (kc p) m -> l p kc m", p=128),
        "w_conv_out": w_conv_out.rearrange("l (kc p) m -> l p kc m", p=128),
        "w_attn_out": w_attn_out.rearrange("l (kc p) m -> l p kc m", p=128),
        "w_merge": w_merge.rearrange("l (kc p) m -> l p kc m", p=128),
        "w_gate_up": w_gate_up.rearrange("l (kc p) m -> l p kc m", p=128),
        "w_down": w_down.rearrange("l (kc p) m -> l p kc m", p=128),
    }

    with ExitStack() as es:
        def sb(name, shape, dt):
            return es.enter_context(nc.sbuf_tensor(name, list(shape), dt))

        xresT = sb("xresT", (128, KC, SEG), F32)
        xnT = sb("xnT", (128, KC, SEG), BF16)
        big = sb("big", (128, 24, SEG), BF16)
        qnT = sb("qnT", (128, 3, SEG), BF16)
        cT = sb("cT", (128, L, 2, NKEYMAX), BF16)
        krT = sb("krT", (128, L, NKEYMAX), BF16)
        scr = sb("scr", (128, KC, SEG), F32)
        wbuf = sb("wbuf", (128, NSLOT, KC, 128), BF16)
        wuq = sb("wuq", (128, 3, H * QH), BF16)
        wuqB = sb("wuqB", (128, 3, H, 32), BF16)
        wuqA = sb("wuqA", (128, 3, H, 32), BF16)
        wukv = sb("wukv", (128, 2, H * 128), BF16)
        wkrB = sb("wkrB", (128, KC, 32), BF16)
        KT = sb("KT", (128, 2, NKEYMAX), BF16)
        QT = sb("QT", (128, 2, SEG), BF16)
        VB = sb("VB", (128, 2, 17, 128), BF16)
        PT = sb("PT", (128, 3, SEG), BF16)
        Rr = sb("Rr", (128, SEG), F32)
        u_t = sb("u_t", (128, SEG + 2), F32)
        cv_t = sb("cv_t", (128, SEG), F32)
        cg_t = sb("cg_t", (128, SEG), F32)
        sa_t = sb("sa_t", (128, SEG), F32)
        sb_t = sb("sb_t", (128, SEG), F32)
        sg_t = sb("sg_t", (128, 1, SEG), F32)
        sq_t = sb("sq_t", (128, 2, SEG), BF16)
        rt_t = sb("rt_t", (128, SEG), F32)
        rstd_t = sb("rstd_t", (128, SEG), F32)
        kro_t = sb("kro_t", (128, SEG), F32)
        t1_t = sb("t1_t", (128, SEG), F32)
        t2_t = sb("t2_t", (128, SEG), F32)
        cos_t = sb("cos_t", (128, SEG), F32)
        sin_t = sb("sin_t", (128, SEG), F32)
        xtok = sb("xtok", (128, 2, D), F32)
        xin = sb("xin", (128, 2, D), F32)
        ost = sb("ost", (128, 2, 288), F32)
        hist = sb("hist", (128, L, KC, 2), F32)
        smallp = sb("smallp_sb", (128, L, NSP), F32)
        ident = sb("ident_sb", (128, 128), F32)
        ones = sb("ones_sb", (128, 128), BF16)
        ps = [es.enter_context(nc.psum_tensor(f"ps{i}", [128, 512], F32)) for i in range(8)]

        bcT = lambda m: big[:, m, :]
        attnT_i, zT_i = 8, 16
        qlat_i, ckvs_i, cnew_i = 0, 3, 5

        rr = {"mm": 0, "aux": 0, "acc": 0}
        MM_BANKS, AUX_BANKS, ACC_BANKS, SS_BANK = (0, 1, 2), (4, 7), (5, 6), 3

        def bank(cls):
            lst = {"mm": MM_BANKS, "aux": AUX_BANKS, "acc": ACC_BANKS}[cls]
            b = lst[rr[cls] % len(lst)]
            rr[cls] += 1
            return b

        def act(out, in_, func, reads, writes, scale=None, bias=None):
            kw = {}
            if scale is not None:
                kw["scale"] = scale
            if bias is not None:
                kw["bias"] = bias
            return R.op("act", lambda e: e.activation(out=out, in_=in_, func=func, **kw), reads, writes)

        def tt(out, in0, in1, op, reads, writes):
            return R.op("dve", lambda e: e.tensor_tensor(out=out, in0=in0, in1=in1, op=op), reads, writes)

        def stt(out, in0, scalar, in1, op0, op1, reads, writes):
            return R.op("dve", lambda e: e.scalar_tensor_tensor(out=out, in0=in0, scalar=scalar, in1=in1,
                                                               op0=op0, op1=op1), reads, writes)

        def dcopy(out, in_, reads, writes):
            return R.op("dve", lambda e: e.tensor_copy(out=out, in_=in_), reads, writes)

        def acopy(out, in_, reads, writes, scale=None):
            if scale is None:
                return R.op("act", lambda e: e.copy(out=out, in_=in_), reads, writes)
            return R.op("act", lambda e: e.mul(out=out, in_=in_, mul=scale), reads, writes)

        def mm(out, lhsT, rhs):
            return lambda e, st, sp: e.matmul(out, lhsT, rhs, start=st, stop=sp)

        def tr(out, in_, idn):
            return lambda e, st, sp: e.transpose(out, in_, idn)

        out_tickets = []

        def chk(name):
            if stop == name:
                raise _Stop()

        class WS:
            def __init__(self):
                self.sched = []
                self.emitted = 0

            def advance(self, upto):
                upto = min(upto, len(self.sched) - 1)
                while self.emitted <= upto:
                    i = self.emitted
                    (l, name, kc0, nkc, c0, ncols) = self.sched[i]
                    slot = i % NSLOT
                    src = wv[name][l, :, kc0:kc0 + nkc, c0:c0 + ncols]
                    dst = wbuf[:, slot, 0:nkc, 0:ncols]
                    R.dma("pool", lambda e, dst=dst, src=src: e.dma_start(out=dst, in_=src), f"W{slot}",
                          reads=(), writes=(("w", slot),))
                    self.emitted += 1

            def use(self, i, desc):
                assert self.sched[i] == desc, (i, self.sched[i], desc)
                assert i < self.emitted, (i, self.emitted)
                return i % NSLOT

        ws = WS()
        wpos = [0]

        def wtile(desc):
            i = wpos[0]
            wpos[0] += 1
            slot = ws.use(i, desc)
            return slot, ("w", slot)

        def wdone():
            ws.advance(wpos[0] - 1 + NSLOT)

        segs = []
        for s in range(n_prompt_seq):
            for q in range(n_quarters):
                segs.append(dict(kind="p", seq=s, q=q, N=SEG, pos0=q * SEG, key0=q * SEG))
        if with_sample:
            segs.append(dict(kind="s", seq=0, q=4, N=DEC_SEQ, pos0=PAST, key0=PAST))

        def sl_sched(l):
            out = []
            for m in range(KC):
                out.append((l, "w_in", 0, KC, O1 + 128 * m, 128))
                out.append((l, "w_in", 0, KC, O2 + 128 * m, 128))
                out.append((l, "w_in", 0, KC, 128 * m, 128))
            for i in range(3):
                out.append((l, "w_in", 0, KC, O3 + 128 * i, 128))
            for i in range(2):
                out.append((l, "w_in", 0, KC, O4 + 128 * i, 128))
            out.append((l, "w_in", 0, KC, O5, 32))
            for m in range(KC):
                out.append((l, "w_conv_out", 0, KC, 128 * m, 128))
                out.append((l, "w_in", 0, KC, O6 + 128 * m, 128))
                out.append((l, "w_attn_out", 0, KC, 128 * m, 128))
                out.append((l, "w_in", 0, KC, O7 + 128 * m, 128))
            for m in range(KC):
                out.append((l, "w_merge", 0, KC, 128 * m, 128))
            for f in range(FC):
                out.append((l, "w_gate_up", 0, KC, 128 * f, 128))
                out.append((l, "w_gate_up", 0, KC, DFF + 128 * f, 128))
            for m in range(KC):
                out.append((l, "w_down", 0, 8, 128 * m, 128))
                out.append((l, "w_down", 8, 8, 128 * m, 128))
                out.append((l, "w_down", 16, 6, 128 * m, 128))
            return out

        for sg in segs:
            for l in range(n_layers):
                ws.sched.extend(sl_sched(l))

        R.dma("sp", lambda e: e.dma_start(out=smallp[:, :, :].rearrange("p l c -> p (l c)"), in_=smallp_d[:, :]),
              "S_small", writes=("smallp",))
        R.dma("sp", lambda e: e.dma_start(out=ident[:, :], in_=ident_d[:, :]), "S_ident", writes=("ident",))
        R.op("dve", lambda e: e.memset(ones[:, :], 1.0), writes=("ones",))
        R.op("dve", lambda e: e.memset(VB[:, 0, :, 64:128], 1.0), writes=(("V", 0),))
        R.op("dve", lambda e: e.memset(VB[:, 1, :, 0:64], 1.0), writes=(("V", 1),))
        R.op("dve", lambda e: e.memset(KT[:, :, :], 0.0), writes=(("KT", 0), ("KT", 1)))
        R.op("dve", lambda e: e.memset(QT[:, :, :], 0.0), writes=(("QT", 0), ("QT", 1)))
        ws.advance(NSLOT - 1)

        def sp_col(l, c0, n=1):
            return smallp[:, l, c0:c0 + n]

        def load_wuq_wukv(l):
            for kc in range(3):
                R.dma("pool", lambda e, kc=kc: e.dma_start(out=wuq[:, kc, :], in_=wv["w_uq"][l, :, kc, :]),
                      "S_wuq", writes=("wuq",))
            for kc in range(2):
                for hf in range(2):
                    R.dma("pool", lambda e, kc=kc, hf=hf: e.dma_start(
                        out=wukv[:, kc, hf * 1024:(hf + 1) * 1024], in_=wv["w_ukv"][l, :, kc, hf * 1024:(hf + 1) * 1024]),
                        "S_wukv", writes=("wukv",))
            wv4 = wuq[:, :, :].rearrange("p k (h d) -> p k h d", d=QH)
            acopy(wuqB[:, :, :, 0:16], wv4[:, :, :, 80:96], reads=("wuq",), writes=("wuqB",), scale=-1.0)
            acopy(wuqB[:, :, :, 16:32], wv4[:, :, :, 64:80], reads=("wuq",), writes=("wuqB",))
            acopy(wuqA[:, :, :, :], wv4[:, :, :, 64:96], reads=("wuq",), writes=("wuqA",))

        def norm_finish(ssb, nfeat, N):
            act(rt_t[:, :N], ps[ssb][:, :N], AF.Ln, reads=(("ps", ssb),), writes=("rt",), scale=1.0 / nfeat,
                bias=EPS)
            act(rstd_t[:, :N], rt_t[:, :N], AF.Exp, reads=("rt",), writes=("rstd",), scale=-0.5)

        def prenorm(l, gcol, N):
            ssb = SS_BANK
            for kc in range(KC):
                b = kc % 2
                act(sq_t[:, b, :N], xresT[:, kc, :N], AF.Square, reads=(("xres", kc),), writes=(("sq", b),))
                R.mm1(("ps", ssb), (lambda e, b=b, kc=kc: e.matmul(ps[ssb][:, :N], ones[:, :], sq_t[:, b, :N],
                                                                 start=(kc == 0), stop=(kc == KC - 1))),
                      reads=(("sq", b), "ones"), first=(kc == 0))
            norm_finish(ssb, D, N)
            for kc in range(KC):
                stt(xnT[:, kc, :N], xresT[:, kc, :N], sp_col(l, gcol + kc), rstd_t[:, :N], ALU.mult, ALU.mult,
                    reads=(("xres", kc), "rstd", "smallp"), writes=(("xn", kc),))

        def dense_group(l, name, c0, ncols, rhs_fn, rhs_keys, nkc_total=KC, pieces=None):
            b = bank("mm")
            mms = []
            pieces = pieces or [(0, nkc_total)]
            for (k0, nk) in pieces:
                slot, wkey = wtile((l, name, k0, nk, c0, ncols))
                for k in range(nk):
                    kc = k0 + k
                    mms.append((mm(ps[b][0:ncols, :rhs_fn.N], wbuf[:, slot, k, 0:ncols], rhs_fn(kc)),
                                (wkey, rhs_keys(kc))))
            R.mmgroup(("ps", b), mms)
            wdone()
            return b

        class RhsFn:
            def __init__(self, fn, N):
                self.fn = fn
                self.N = N

            def __call__(self, kc):
                return self.fn(kc)

        def issue_input(sg, blk):
            if blk in sg.setdefault("in_issued", set()):
                return
            sg["in_issued"].add(blk)
            N_, pos0_ = sg["N"], sg["pos0"]
            nb = min(128, N_ - blk * 128)
            if sg["kind"] == "p":
                src = x_p[sg["seq"], pos0_ + blk * 128: pos0_ + blk * 128 + nb, :]
            else:
                src = x_s[blk * 128: blk * 128 + nb, :]
            xb = blk % 2
            R.dma("sp", lambda e: e.dma_start(out=xin[0:nb, xb, :], in_=src), f"S_xin{xb}", writes=(("xin", xb),))

        def issue_rope(sg):
            if sg.get("rope_issued"):
                return
            sg["rope_issued"] = True
            N_, pos0_ = sg["N"], sg["pos0"]
            R.dma("sp", lambda e: e.dma_start(out=cos_t[:, :N_], in_=rope_d[0, :, pos0_:pos0_ + N_]), "S_cos",
                  writes=("cos",))
            R.dma("sp", lambda e: e.dma_start(out=sin_t[:, :N_], in_=rope_d[1, :, pos0_:pos0_ + N_]), "S_sin",
                  writes=("sin",))

        def prefetch_next(sg):
            i = segs.index(sg)
            if i + 1 < len(segs):
                nx = segs[i + 1]
                for blk in range(min(2, (nx["N"] + 127) // 128)):
                    issue_input(nx, blk)
                issue_rope(nx)

        def segment_layer(sg, l, first_layer, last_layer):
            N = sg["N"]
            kind = sg["kind"]
            key0 = sg["key0"]
            pos0 = sg["pos0"]
            q = sg["q"]
            seq = sg["seq"]
            nblk = (N + 127) // 128
            xn_rhs = RhsFn(lambda kc: xnT[:, kc, :N], N)
            xn_keys = lambda kc: ("xn", kc)

            if first_layer:
                for blk in range(nblk):
                    nb = min(128, N - blk * 128)
                    issue_input(sg, blk)
                    xb = blk % 2
                    for half in range(2):
                        b = bank("mm")
                        for j in range(4):
                            kc = half * 4 + j
                            R.mmgroup(("ps", b) if j == 0 else ("psx", b), [
                                (tr(ps[b][:, j * 128: j * 128 + nb], xin[0:nb, xb, kc * 128:(kc + 1) * 128],
                                    ident[0:nb, 0:nb]), (("xin", xb), "ident"))])
                        R.res[("ps", b)] = [("E_pe", R.cnt["E_pe"]), {}]
                        src_ps = ps[b][:, :].rearrange("p (j t) -> p j t", t=128)[:, :, 0:nb]
                        acopy(xresT[:, half * 4:half * 4 + 4, blk * 128: blk * 128 + nb], src_ps,
                              reads=(("ps", b),), writes=tuple(("xres", half * 4 + j) for j in range(4)))
                issue_rope(sg)

            chk("input")
            if kind == "p" and q == 0:
                R.op("dve", lambda e: e.memset(hist[:, l, :, :], 0.0), writes=(("hist", l),))
            if kind == "s":
                R.dma("sp", lambda e: e.dma_start(
                    out=hist[:, l, :, :].rearrange("p k j -> p (k j)"), in_=hist0[:, l * 16:(l + 1) * 16]),
                    "S_hist", writes=(("hist", l),))
                for blk in range(PAST // 128):
                    ob = blk % 2
                    R.dma("sp", lambda e, blk=blk, ob=ob: e.dma_start(
                        out=ost[:, ob, 0:256], in_=c_ckv[l, blk * 128:(blk + 1) * 128, :]), f"S_ost{ob}",
                        writes=(("ost", ob),))
                    R.dma("sp", lambda e, blk=blk, ob=ob: e.dma_start(
                        out=ost[:, ob, 256:288], in_=c_kr[l, blk * 128:(blk + 1) * 128, :]), f"S_ost{ob}",
                        writes=())
                    R.res[("ost", ob)] = [(f"S_ost{ob}", R.cnt[f"S_ost{ob}"]), {}]
                    b = bank("aux")
                    R.mmgroup(("ps", b), [(tr(ps[b][:, 0:128], ost[:, ob, 0:128], ident[:, :]), (("ost", ob), "ident"))])
                    R.mmgroup(("psx", b), [(tr(ps[b][:, 128:256], ost[:, ob, 128:256], ident[:, :]), (("ost", ob), "ident"))])
                    R.mmgroup(("psx", b), [(tr(ps[b][0:32, 256:384], ost[:, ob, 256:288], ident[:, :]), (("ost", ob), "ident"))])
                    R.res[("ps", b)] = [("E_pe", R.cnt["E_pe"]), {}]
                    jt = blk // 4
                    acopy(cT[:, l, :, blk * 128:(blk + 1) * 128],
                          ps[b][:, 0:256].rearrange("p (k t) -> p k t", t=128),
                          reads=(("ps", b),), writes=(("cT", l, jt),))
                    acopy(krT[64:96, l, blk * 128:(blk + 1) * 128], ps[b][0:32, 256:384],
                          reads=(("ps", b),), writes=(("krT", l, jt),))

            chk("hist")
            load_wuq_wukv(l)

            chk("wuq")
            prenorm(l, C_GPRE, N)

            chk("prenorm")
            for m in range(KC):
                b_cg = dense_group(l, "w_in", O1 + 128 * m, 128, xn_rhs, xn_keys)
                b_xi = dense_group(l, "w_in", O2 + 128 * m, 128, xn_rhs, xn_keys)
                acopy(cg_t[:, :N], ps[b_cg][:, :N], reads=(("ps", b_cg),), writes=("cg",))
                dcopy(u_t[:, 0:2], hist[:, l, m, :], reads=(("hist", l),), writes=("u",))
                tt(u_t[:, 2:2 + N], ps[b_xi][:, :N], cg_t[:, :N], ALU.mult, reads=(("ps", b_xi), "cg", "u"),
                   writes=("u",))
                cws = [sp_col(l, C_CW + 3 * m + j) for j in range(3)]
                cw = lambda j, cws=cws: cws[j]
                R.op("dve", lambda e, c2=cws[2]: e.tensor_scalar(out=cv_t[:, :N], in0=u_t[:, 2:2 + N], scalar1=c2,
                                                                 scalar2=None, op0=ALU.mult),
                     reads=("u", "smallp"), writes=("cv",))
                stt(cv_t[:, :N], u_t[:, 1:1 + N], cw(1), cv_t[:, :N], ALU.mult, ALU.add, reads=("u", "cv", "smallp"),
                    writes=("cv",))
                stt(cv_t[:, :N], u_t[:, 0:N], cw(0), cv_t[:, :N], ALU.mult, ALU.add, reads=("u", "cv", "smallp"),
                    writes=("cv",))
                dcopy(hist[:, l, m, :], u_t[:, N:N + 2], reads=("u",), writes=(("hist", l),))
                b_bg = dense_group(l, "w_in", 128 * m, 128, xn_rhs, xn_keys)
                tt(big[:, m, :N], ps[b_bg][:, :N], cv_t[:, :N], ALU.mult, reads=(("ps", b_bg), "cv"),
                   writes=(("big", m),))
            if (kind == "p" and q == n_quarters - 1) or kind == "s":
                if kind == "p":
                    dst = nconv_p[l, seq, :, :].rearrange("j (k p) -> p k j", p=128)
                else:
                    dst = nconv_s[l, :, :].rearrange("j (k p) -> p k j", p=128)
                for kc in range(KC):
                    out_tickets.append(R.dma("sp", lambda e, dst=dst, kc=kc: e.dma_start(
                        out=dst[:, kc, :], in_=hist[:, l, kc, :], allow_slow_non_contiguous=True), f"S_histout{l}",
                        reads=(("hist", l),)))

            chk("conv")
            for i in range(3):
                bq = dense_group(l, "w_in", O3 + 128 * i, 128, xn_rhs, xn_keys)
                acopy(scr[:, qlat_i + i, :N], ps[bq][:, :N], reads=(("ps", bq),), writes=(("scr", qlat_i + i),))
            for i in range(3):
                b = i % 2
                act(sq_t[:, b, :N], scr[:, qlat_i + i, :N], AF.Square, reads=(("scr", qlat_i + i),),
                    writes=(("sq", b),))
                R.mm1(("ps", SS_BANK), (lambda e, b=b, i=i: e.matmul(ps[SS_BANK][:, :N], ones[:, :], sq_t[:, b, :N],
                                                                    start=(i == 0), stop=(i == 2))),
                      reads=(("sq", b), "ones"), first=(i == 0))
            norm_finish(SS_BANK, Q_LORA, N)
            for i in range(3):
                stt(qnT[:, i, :N], scr[:, qlat_i + i, :N], sp_col(l, C_GQ + i), rstd_t[:, :N], ALU.mult, ALU.mult,
                    reads=(("scr", qlat_i + i), "rstd", "smallp"), writes=(("qn", i),))
            for i in range(2):
                bq = dense_group(l, "w_in", O4 + 128 * i, 128, xn_rhs, xn_keys)
                acopy(scr[:, ckvs_i + i, :N], ps[bq][:, :N], reads=(("ps", bq),), writes=(("scr", ckvs_i + i),))
            for i in range(2):
                b = i % 2
                act(sq_t[:, b, :N], scr[:, ckvs_i + i, :N], AF.Square, reads=(("scr", ckvs_i + i),),
                    writes=(("sq", b),))
                R.mm1(("ps", SS_BANK), (lambda e, b=b, i=i: e.matmul(ps[SS_BANK][:, :N], ones[:, :], sq_t[:, b, :N],
                                                                    start=(i == 0), stop=(i == 1))),
                      reads=(("sq", b), "ones"), first=(i == 0))
            norm_finish(SS_BANK, KV_LORA, N)
            jt_new = q
            for i in range(2):
                stt(scr[:, cnew_i + i, :N], scr[:, ckvs_i + i, :N], sp_col(l, C_GKV + i), rstd_t[:, :N], ALU.mult,
                    ALU.mult, reads=(("scr", ckvs_i + i), "rstd", "smallp"), writes=(("scr", cnew_i + i),))
                acopy(cT[:, l, i, key0:key0 + N], scr[:, cnew_i + i, :N], reads=(("scr", cnew_i + i),),
                      writes=(("cT", l, jt_new),))
            slot, wkey = wtile((l, "w_in", 0, KC, O5, 32))
            acopy(wkrB[:, :, 0:16], wbuf[:, slot, :, 16:32], reads=(wkey,), writes=("wkrB",), scale=-1.0)
            acopy(wkrB[:, :, 16:32], wbuf[:, slot, :, 0:16], reads=(wkey,), writes=("wkrB",))
            bA = bank("aux")
            R.mmgroup(("ps", bA), [(mm(ps[bA][0:32, :N], wbuf[:, slot, kc, 0:32], xnT[:, kc, :N]),
                                    (wkey, ("xn", kc))) for kc in range(KC)])
            wdone()
            bB = bank("aux")
            R.mmgroup(("ps", bB), [(mm(ps[bB][0:32, :N], wkrB[:, kc, :], xnT[:, kc, :N]), ("wkrB", ("xn", kc)))
                                   for kc in range(KC)])
            tt(t1_t[0:32, :N], ps[bA][0:32, :N], cos_t[0:32, :N], ALU.mult, reads=(("ps", bA), "cos"), writes=("t1",))
            tt(t2_t[0:32, :N], ps[bB][0:32, :N], sin_t[0:32, :N], ALU.mult, reads=(("ps", bB), "sin"), writes=("t2",))
            tt(kro_t[0:32, :N], t1_t[0:32, :N], t2_t[0:32, :N], ALU.add, reads=("t1", "t2"), writes=("kro",))
            acopy(krT[64:96, l, key0:key0 + N], kro_t[0:32, :N], reads=("kro",), writes=(("krT", l, jt_new),))
            chk("lowrank")
            for blk in range(nblk):
                nb = min(128, N - blk * 128)
                ob = blk % 2
                b = bank("aux")
                R.mmgroup(("ps", b), [(tr(ps[b][0:nb, 0:128], scr[:, cnew_i, blk * 128:blk * 128 + nb], ident[:, :]),
                                       (("scr", cnew_i), "ident"))])
                R.mmgroup(("psx", b), [(tr(ps[b][0:nb, 128:256], scr[:, cnew_i + 1, blk * 128:blk * 128 + nb],
                                          ident[:, :]), (("scr", cnew_i + 1), "ident"))])
                R.mmgroup(("psx", b), [(tr(ps[b][0:nb, 256:288], kro_t[0:32, blk * 128:blk * 128 + nb],
                                          ident[0:32, 0:32]), ("kro", "ident"))])
                R.res[("ps", b)] = [("E_pe", R.cnt["E_pe"]), {}]
                dcopy(ost[0:nb, ob, :], ps[b][0:nb, 0:288], reads=(("ps", b),), writes=(("ost", ob),))
                if kind == "p":
                    r0 = pos0 + blk * 128
                    d1 = nckv_p[l, seq, r0:r0 + nb, :]
                    d2 = nkr_p[l, seq, r0:r0 + nb, :]
                else:
                    d1 = nckv_s[l, 0:nb, :]
                    d2 = nkr_s[l, 0:nb, :]
                out_tickets.append(R.dma("sp", lambda e, d1=d1, ob=ob, nb=nb: e.dma_start(
                    out=d1, in_=ost[0:nb, ob, 0:256]), f"S_ost{ob}", reads=(("ost", ob),)))
                out_tickets.append(R.dma("sp", lambda e, d2=d2, ob=ob, nb=nb: e.dma_start(
                    out=d2, in_=ost[0:nb, ob, 256:288]), f"S_ost{ob}", reads=(("ost", ob),)))

            chk("ctxout")
            nkeys = key0 + N
            ktiles = [(j * 512, min(512, nkeys - j * 512)) for j in range((nkeys + 511) // 512)]
            kblocks = [(j * 128, min(128, nkeys - j * 128)) for j in range((nkeys + 127) // 128)]
            for hb in range(2):
                for jt, (k0, nk) in enumerate(ktiles):
                    dcopy(KT[64:96, hb, k0:k0 + nk], krT[64:96, l, k0:k0 + nk], reads=(("krT", l, jt),),
                          writes=(("KT", hb),))
            b4 = SS_BANK

            def prep_pieces(h):
                hb = h % 2
                eo = h % 2
                pieces = []
                ev = acopy if len(kblocks) <= 8 else dcopy
                for jt, (k0, nk) in enumerate(ktiles):
                    def p_kt(jt=jt, k0=k0, nk=nk):
                        b = bank("aux")
                        R.mmgroup(("ps", b), [(mm(ps[b][0:64, :nk], wukv[:, kc, h * 128:h * 128 + 64],
                                                  cT[:, l, kc, k0:k0 + nk]), ("wukv", ("cT", l, jt)))
                                              for kc in range(2)])
                        ev(KT[0:64, hb, k0:k0 + nk], ps[b][0:64, :nk], reads=(("ps", b),), writes=(("KT", hb),))
                    pieces.append(p_kt)
                for g0 in range(0, len(kblocks), 8):
                    def p_v(g0=g0):
                        grp = kblocks[g0:g0 + 8]
                        b = bank("aux")
                        for j, (k0, nk) in enumerate(grp):
                            R.mmgroup(("ps", b) if j == 0 else ("psx", b), [
                                (mm(ps[b][0:nk, j * 64:(j + 1) * 64], cT[:, l, kc, k0:k0 + nk],
                                    wukv[:, kc, h * 128 + 64:h * 128 + 128]), ("wukv", ("cT", l, k0 // 512)))
                                for kc in range(2)])
                        R.res[("ps", b)] = [("E_pe", R.cnt["E_pe"]), {}]
                        vc0 = 0 if eo == 0 else 64
                        full = [g for g in grp if g[1] == 128]
                        if full:
                            nf = len(full)
                            ev(VB[:, eo, g0:g0 + nf, vc0:vc0 + 64],
                                  ps[b][:, 0:nf * 64].rearrange("p (j v) -> p j v", v=64),
                                  reads=(("ps", b),), writes=(("V", eo),))
                        if len(full) < len(grp):
                            j = len(full)
                            nk = grp[j][1]
                            ev(VB[0:nk, eo, g0 + j, vc0:vc0 + 64], ps[b][0:nk, j * 64:(j + 1) * 64],
                                  reads=(("ps", b),), writes=(("V", eo),))
                    pieces.append(p_v)

                def p_rot4():
                    g = h // 4
                    bA4 = bank("aux")
                    R.mmgroup(("ps", bA4), [(mm(ps[bA4][:, :N], wuqA[:, kc, 4 * g:4 * g + 4, :], qnT[:, kc, :N]),
                                             ("wuqA", ("qn", kc))) for kc in range(3)])
                    R.mmgroup(("ps", b4), [(mm(ps[b4][:, :N], wuqB[:, kc, 4 * g:4 * g + 4, :], qnT[:, kc, :N]),
                                            ("wuqB", ("qn", kc))) for kc in range(3)])
                    tt(t1_t[:, :N], ps[bA4][:, :N], cos_t[:, :N], ALU.mult, reads=(("ps", bA4), "cos"), writes=("t1",))
                    tt(t2_t[:, :N], ps[b4][:, :N], sin_t[:, :N], ALU.mult, reads=(("ps", b4), "sin"), writes=("t2",))
                    tt(kro_t[:, :N], t1_t[:, :N], t2_t[:, :N], ALU.add, reads=("t1", "t2"), writes=("kro",))

                def p_q():
                    bA = bank("aux")
                    R.mmgroup(("ps", bA), [(mm(ps[bA][0:64, :N], wuq[:, kc, h * QH:h * QH + 64], qnT[:, kc, :N]),
                                            ("wuq", ("qn", kc))) for kc in range(3)])
                    ev(QT[0:64, hb, :N], ps[bA][0:64, :N], reads=(("ps", bA),), writes=(("QT", hb),))
                    j4 = h % 4
                    R.op("pool", lambda e: e.tensor_copy(out=QT[64:96, hb, :N], in_=kro_t[32 * j4:32 * j4 + 32, :N]),
                         reads=("kro",), writes=(("QT", hb),))
                    if debug and h == 0 and l == 0 and sg is segs[0]:
                        R.dma("pool", lambda e: e.dma_start(out=dbg["d_qt"][:, :], in_=QT[:, 0, :]), "S_dbg",
                              reads=(("QT", 0),))
                        R.dma("pool", lambda e: e.dma_start(out=dbg["d_kt"][:, :], in_=KT[:, 0, 0:SEG]), "S_dbg",
                              reads=(("KT", 0),))
                if h % 4 == 0:
                    pieces.insert(0, p_rot4)
                    pieces.insert(1, p_q)
                else:
                    pieces.insert(0, p_q)
                return pieces

            def head_loop(h, pieces):
                hb = h % 2
                eo = h % 2
                accb = bank("acc")
                nkb = len(kblocks)
                info = []
                for kb, (k0, nk) in enumerate(kblocks):
                    if kind == "p" and k0 >= key0:
                        bd = (k0 - key0) // 128
                        info.append((k0, nk, bd, 128 * bd))
                    else:
                        info.append((k0, nk, None, 0))
                sbanks = {}

                def issue_s(kb):
                    k0, nk, bd, qlo = info[kb]
                    sbk = bank("mm")
                    R.mmgroup(("ps", sbk), [(mm(ps[sbk][0:nk, qlo:N], KT[0:QH, hb, k0:k0 + nk], QT[0:QH, hb, qlo:N]),
                                             (("KT", hb), ("QT", hb)))])
                    sbanks[kb] = sbk

                for kb in range(min(2, nkb)):
                    issue_s(kb)
                for kb in range(nkb):
                    k0, nk, bd, qlo = info[kb]
                    sbk = sbanks[kb]
                    pb = kb % 3
                    act(PT[0:nk, pb, qlo:N], ps[sbk][0:nk, qlo:N], AF.Exp, reads=(("ps", sbk),),
                        writes=(("PT", pb),), scale=ATTN_SCALE)
                    def pv(c0, c1, r1, first, last, kb=kb, pb=pb):
                        o_ap, l_ap, r_ap = ps[accb][:, c0:c1], VB[0:r1, eo, kb, :], PT[0:r1, pb, c0:c1]
                        R.mm1(("ps", accb), (lambda e: e.matmul(o_ap, l_ap, r_ap, start=first, stop=last)),
                              reads=(("V", eo), ("PT", pb)), first=first)
                    if bd is None:
                        pv(qlo, N, nk, kb == 0, kb == nkb - 1)
                    else:
                        pv(qlo + 64, N, nk, kb == 0, False)
                        pv(qlo, qlo + 64, 64, False, kb == nkb - 1)
                    if kb + 2 < nkb:
                        issue_s(kb + 2)
                    if pieces:
                        pieces.pop(0)()
                while pieces:
                    pieces.pop(0)()
                dlo, slo = (0, 64) if eo == 0 else (64, 0)
                act(Rr[dlo:dlo + 64, :N], ps[accb][slo:slo + 64, :N], AF.Ln, reads=(("ps", accb),), writes=("R",))
                act(Rr[dlo:dlo + 64, :N], Rr[dlo:dlo + 64, :N], AF.Exp, reads=("R",), writes=("R",), scale=-1.0)
                tt(big[dlo:dlo + 64, attnT_i + h // 2, :N], ps[accb][dlo:dlo + 64, :N], Rr[dlo:dlo + 64, :N], ALU.mult,
                   reads=(("ps", accb), "R"), writes=(("big", attnT_i + h // 2),))

            for p in prep_pieces(0):
                p()
            for h in range(H):
                nxt = prep_pieces(h + 1) if h + 1 < H else []
                head_loop(h, nxt)

            chk("attn")
            bc_rhs = RhsFn(lambda kc: big[:, kc, :N], N)
            bc_keys = lambda kc: ("big", kc)
            at_rhs = RhsFn(lambda kc: big[:, attnT_i + kc, :N], N)
            at_keys = lambda kc: ("big", attnT_i + kc)
            for m in range(KC):
                b_ya = dense_group(l, "w_conv_out", 128 * m, 128, bc_rhs, bc_keys)
                b_ga = dense_group(l, "w_in", O6 + 128 * m, 128, xn_rhs, xn_keys)
                act(sa_t[:, :N], ps[b_ga][:, :N], AF.Sigmoid, reads=(("ps", b_ga),), writes=("sa",))
                tt(sa_t[:, :N], ps[b_ya][:, :N], sa_t[:, :N], ALU.mult, reads=(("ps", b_ya), "sa"), writes=("sa",))
                b_yb = dense_group(l, "w_attn_out", 128 * m, 128, at_rhs, at_keys)
                b_gb = dense_group(l, "w_in", O7 + 128 * m, 128, xn_rhs, xn_keys)
                act(sb_t[:, :N], ps[b_gb][:, :N], AF.Sigmoid, reads=(("ps", b_gb),), writes=("sb",))
                tt(sb_t[:, :N], ps[b_yb][:, :N], sb_t[:, :N], ALU.mult, reads=(("ps", b_yb), "sb"), writes=("sb",))
                tt(big[:, zT_i + m, :N], sa_t[:, :N], sb_t[:, :N], ALU.add, reads=("sa", "sb"),
                   writes=(("big", zT_i + m),))

            chk("z")
            def postnorm_residual(groups_fn, gcol):
                def ss_mm(m):
                    b = m % 2
                    R.mm1(("ps", SS_BANK), (lambda e, b=b, m=m: e.matmul(ps[SS_BANK][:, :N], ones[:, :],
                                                                        sq_t[:, b, :N], start=(m == 0),
                                                                        stop=(m == KC - 1))),
                          reads=(("sq", b), "ones"), first=(m == 0))
                for m in range(KC):
                    bm = groups_fn(m)
                    if m > 0:
                        ss_mm(m - 1)
                    b = m % 2
                    dcopy(scr[:, m, :N], ps[bm][:, :N], reads=(("ps", bm),), writes=(("scr", m),))
                    act(sq_t[:, b, :N], scr[:, m, :N], AF.Square, reads=(("scr", m),), writes=(("sq", b),))
                ss_mm(KC - 1)
                chk("pn_a")
                norm_finish(SS_BANK, D, N)
                chk("pn_b")
                for m in range(KC):
                    stt(scr[:, m, :N], scr[:, m, :N], sp_col(l, gcol + m), rstd_t[:, :N], ALU.mult, ALU.mult,
                        reads=(("scr", m), "rstd", "smallp"), writes=(("scr", m),))
                    tt(xresT[:, m, :N], xresT[:, m, :N], scr[:, m, :N], ALU.add, reads=(("xres", m), ("scr", m)),
                       writes=(("xres", m),))

            z_rhs = RhsFn(lambda kc: big[:, zT_i + kc, :N], N)
            z_keys = lambda kc: ("big", zT_i + kc)
            postnorm_residual(lambda m: dense_group(l, "w_merge", 128 * m, 128, z_rhs, z_keys), C_GPOST)
            if debug and l == 0 and sg is segs[0]:
                def dump(nm, t_, c0, n_, keyf, eng="pool"):
                    for k in range(n_):
                        R.dma(eng, lambda e, k=k: e.dma_start(out=dbg[nm][:, k * SEG:(k + 1) * SEG], in_=t_[:, c0 + k, :]),
                              "S_dbg", reads=(keyf(c0 + k),))
                dump("d_xn", xnT, 0, KC, lambda k: ("xn", k))
                dump("d_bc", big, 0, 8, lambda k: ("big", k))
                dump("d_qn", qnT, 0, 3, lambda k: ("qn", k))
                dump("d_attn", big, 8, 8, lambda k: ("big", k))
                dump("d_z", big, 16, 8, lambda k: ("big", k))
                dump("d_xres", xresT, 0, KC, lambda k: ("xres", k), "sp")
            chk("merge")
            if last_layer:
                prefetch_next(sg)
            prenorm(l, C_FPRE, N)
            for f in range(FC):
                b_g = dense_group(l, "w_gate_up", 128 * f, 128, xn_rhs, xn_keys)
                b_u = dense_group(l, "w_gate_up", DFF + 128 * f, 128, xn_rhs, xn_keys)
                sgb = 0
                act(sg_t[:, sgb, :N], ps[b_g][:, :N], AF.Silu, reads=(("ps", b_g),), writes=(("sg", sgb),))
                tt(big[:, f, :N], ps[b_u][:, :N], sg_t[:, sgb, :N], ALU.mult, reads=(("ps", b_u), ("sg", sgb)),
                   writes=(("big", f),))
            chk("ffn")
            h_rhs = RhsFn(lambda kc: big[:, kc, :N], N)
            h_keys = lambda kc: ("big", kc)
            postnorm_residual(lambda m: dense_group(l, "w_down", 128 * m, 128, h_rhs, h_keys,
                                                    pieces=[(0, 8), (8, 8), (16, 6)]), C_FPOST)

            chk("down")
            if debug and l == 0 and sg is segs[0]:
                for k in range(KC):
                    R.dma("sp", lambda e, k=k: e.dma_start(out=dbg["d_xres2"][:, k * SEG:(k + 1) * SEG], in_=xresT[:, k, :]),
                          "S_dbg", reads=(("xres", k),))
            if last_layer:
                for blk in range(nblk):
                    nb = min(128, N - blk * 128)
                    for half in range(2):
                        b = bank("mm")
                        for j in range(4):
                            kc = half * 4 + j
                            R.mmgroup(("ps", b) if j == 0 else ("psx", b), [
                                (tr(ps[b][0:nb, j * 128:(j + 1) * 128], xresT[:, kc, blk * 128:blk * 128 + nb],
                                    ident[:, :]), (("xres", kc), "ident"))])
                        R.res[("ps", b)] = [("E_pe", R.cnt["E_pe"]), {}]
                        acopy(xtok[0:nb, blk % 2, half * 512:(half + 1) * 512], ps[b][0:nb, :], reads=(("ps", b),),
                              writes=(("xtok", blk % 2),))
                    if kind == "p":
                        r0 = pos0 + blk * 128
                        dst = y_p[seq, r0:r0 + nb, :]
                    else:
                        dst = y_s[blk * 128:blk * 128 + nb, :]
                    out_tickets.append(R.dma("sp", lambda e, dst=dst, nb=nb, ob=blk % 2: e.dma_start(
                        out=dst, in_=xtok[0:nb, ob, :]), f"S_xtok{blk % 2}", reads=(("xtok", blk % 2),)))

        try:
            for sg in segs:
                for l in range(n_layers):
                    segment_layer(sg, l, first_layer=(l == 0), last_layer=(l == n_layers - 1))
            assert wpos[0] == len(ws.sched), (wpos[0], len(ws.sched))
        except _Stop:
            pass
        for sk in list(R.cnt.keys()):
            if not sk.startswith("E_"):
                R.streams["sp"].append(("w", sk, R.cnt[sk]))

        fin = {}
        for (sk, v) in out_tickets:
            fin[sk] = max(fin.get(sk, 0), v)
        if "S_dbg" in R.cnt:
            fin["S_dbg"] = R.cnt["S_dbg"]
        for sk, v in fin.items():
            R.streams["sp"].append(("w", sk, v))

        semh = {}
        for sk in sorted(R.cnt.keys()):
            semh[sk] = es.enter_context(nc.semaphore(sk))
        with nc.Block() as block:
            @block.tensor
            def _(e):
                _replay(e, R.streams["pe"], semh)

            @block.scalar
            def _(e):
                _replay(e, R.streams["act"], semh)

            @block.vector
            def _(e):
                _replay(e, R.streams["dve"], semh)

            @block.gpsimd
            def _(e):
                _replay(e, R.streams["pool"], semh)

            @block.sync
            def _(e):
                _replay(e, R.streams["sp"], semh)
    stats = {k: len(v) for k, v in R.streams.items()}
    return nc, stats


def _host_constants():
    half = QK_ROPE // 2
    inv = ROPE_THETA ** (-np.arange(half, dtype=np.float32) / np.float32(half))
    pos = np.arange(NKEYMAX, dtype=np.float32)
    ang = pos[None, :] * inv[:, None].astype(np.float32)
    cos = np.cos(ang).astype(np.float32)
    sin = np.sin(ang).astype(np.float32)
    idx = np.arange(128) % half
    rope = np.stack([cos[idx], sin[idx]], axis=0).astype(np.float32)
    ident = np.eye(128, dtype=np.float32)
    return rope, ident


def _feat_major(v, nchunk):
    return np.ascontiguousarray(v.reshape(nchunk, 128).T)


def make_in_maps(inputs):
    f = lambda k: np.ascontiguousarray(np.asarray(inputs[k], dtype=np.float32))
    x_prompt, x_sample = f("x_prompt"), f("x_sample")
    state_conv, cache_ckv, cache_krope = f("state_conv"), f("cache_ckv"), f("cache_krope")
    rope, ident = _host_constants()
    smallp = np.zeros((128, L, NSP), np.float32)
    for l in range(L):
        smallp[:, l, C_GPRE:C_GPRE + 8] = _feat_major(f("norm_attn_pre")[l], 8)
        smallp[:, l, C_GPOST:C_GPOST + 8] = _feat_major(f("norm_attn_post")[l], 8)
        smallp[:, l, C_FPRE:C_FPRE + 8] = _feat_major(f("norm_ffn_pre")[l], 8)
        smallp[:, l, C_FPOST:C_FPOST + 8] = _feat_major(f("norm_ffn_post")[l], 8)
        smallp[:, l, C_GQ:C_GQ + 3] = _feat_major(f("norm_q")[l], 3)
        smallp[:, l, C_GKV:C_GKV + 2] = _feat_major(f("norm_kv")[l], 2)
        cw = f("conv_w")[l]
        for j in range(3):
            smallp[:, l, C_CW + j:C_CW + 24:3] = _feat_major(cw[j], 8)
    shared = dict(smallp=smallp.reshape(128, L * NSP), rope=rope, ident=ident,
                  w_in=f("w_in"), w_uq=f("w_uq"), w_ukv=f("w_ukv"), w_conv_out=f("w_conv_out"),
                  w_attn_out=f("w_attn_out"), w_merge=f("w_merge"), w_gate_up=f("w_gate_up"), w_down=f("w_down"))
    maps = []
    for c in range(NCORES):
        h0 = np.zeros((128, L, KC, 2), np.float32)
        for l in range(L):
            for j in range(2):
                h0[:, l, :, j] = _feat_major(state_conv[l, c, j], 8)
        m = dict(shared)
        m.update(x_p=np.ascontiguousarray(x_prompt[2 * c:2 * c + 2]), x_s=np.ascontiguousarray(x_sample[c]),
                 c_ckv=np.ascontiguousarray(cache_ckv[:, c]), c_kr=np.ascontiguousarray(cache_krope[:, c]),
                 hist0=h0.reshape(128, L * KC * 2))
        maps.append(m)
    return maps


_PROG = None


def kernel(**inputs):
    global _PROG
    if _PROG is None:
        _PROG = build_program()[0]
    in_maps = make_in_maps(inputs)
    res = run_bass_kernel_spmd(_PROG, in_maps, core_ids=list(range(NCORES)))
    rs = res.results
    B = 2 * NCORES
    y_prompt = np.concatenate([r["y_p"] for r in rs], axis=0)
    y_sample = np.stack([r["y_s"] for r in rs], axis=0)
    nconv_p = np.concatenate([r["nconv_p"] for r in rs], axis=1)
    nckv_p = np.concatenate([r["nckv_p"] for r in rs], axis=1)
    nkr_p = np.concatenate([r["nkr_p"] for r in rs], axis=1)
    nconv_s = np.stack([r["nconv_s"] for r in rs], axis=1)
    nckv_s = np.stack([r["nckv_s"] for r in rs], axis=1)
    nkr_s = np.stack([r["nkr_s"] for r in rs], axis=1)
    outs = (y_prompt, y_sample, nconv_p, nckv_p, nkr_p, nconv_s, nckv_s, nkr_s)
    return tuple(np.ascontiguousarray(o, dtype=np.float32) for o in outs)
```

```python
import numpy as np
from contextlib import ExitStack
import concourse.bass as bass
import concourse.mybir as mybir
from concourse.bass_utils import run_bass_kernel_spmd

F32 = mybir.dt.float32
BF16 = mybir.dt.bfloat16
ALU = mybir.AluOpType
AF = mybir.ActivationFunctionType

NCORES = 8
D = 1024
KC = 8
L = 2
SEQ = 2048
SEG = 512
DEC_SEQ = 32
PAST = 2048
NKEYMAX = PAST + DEC_SEQ
H = 16
QK_NOPE = 64
QK_ROPE = 32
QH = QK_NOPE + QK_ROPE
V_HEAD = 64
Q_LORA = 384
KV_LORA = 256
DFF = 2816
FC = DFF // 128
D_IN = 3 * D + Q_LORA + KV_LORA + QK_ROPE + 2 * D
O1, O2, O3 = D, 2 * D, 3 * D
O4 = O3 + Q_LORA
O5 = O4 + KV_LORA
O6 = O5 + QK_ROPE
O7 = O6 + D
EPS = 1e-6
ATTN_SCALE = float(QH ** -0.5)
ROPE_THETA = 10000.0
NSLOT = 9
NSP = 61
C_GPRE, C_GPOST, C_FPRE, C_FPOST, C_GQ, C_GKV, C_CW = 0, 8, 16, 24, 32, 35, 37


class Rec:
    ENG = ("pe", "act", "dve", "pool", "sp")

    def __init__(self):
        self.streams = {e: [] for e in self.ENG}
        self.cnt = {}
        self.waited = {e: {} for e in self.ENG}
        self.res = {}

    def _deps(self, eng, reads, writes):
        deps = {}

        def add(sk, v):
            if eng == "pe" and sk == "E_pe":
                return
            if deps.get(sk, 0) < v:
                deps[sk] = v

        for r in reads:
            st = self.res.get(r)
            if st and st[0] is not None:
                add(*st[0])
        for w in writes:
            st = self.res.get(w)
            if st:
                if st[0] is not None:
                    add(*st[0])
                for sk, v in st[1].items():
                    add(sk, v)
        wd = self.waited[eng]
        for sk, v in deps.items():
            if wd.get(sk, 0) < v:
                self.streams[eng].append(("w", sk, v))
                wd[sk] = v

    def _commit(self, t, reads, writes):
        for r in reads:
            st = self.res.setdefault(r, [None, {}])
            if st[1].get(t[0], 0) < t[1]:
                st[1][t[0]] = t[1]
        for w in writes:
            self.res[w] = [t, {}]

    def op(self, eng, fn, reads=(), writes=()):
        self._deps(eng, reads, writes)
        sk = "E_" + eng
        self.cnt[sk] = self.cnt.get(sk, 0) + 1
        t = (sk, self.cnt[sk])
        self.streams[eng].append(("i", fn, sk, 1))
        self._commit(t, reads, writes)
        return t

    def dma(self, eng, fn, semkey, reads=(), writes=()):
        self._deps(eng, reads, writes)
        self.cnt[semkey] = self.cnt.get(semkey, 0) + 16
        t = (semkey, self.cnt[semkey])
        self.streams[eng].append(("i", fn, semkey, 16))
        self._commit(t, reads, writes)
        return t

    def mmgroup(self, bank_key, mms):
        sk = "E_pe"
        final = (sk, self.cnt.get(sk, 0) + 1)
        n = len(mms)
        for i, (fn, reads) in enumerate(mms):
            self._deps("pe", reads, (bank_key,) if i == 0 else ())
            last = i == n - 1
            self.streams["pe"].append(
                ("i", (lambda e, fn=fn, st=(i == 0), sp=last: fn(e, st, sp)), sk if last else None, 1))
            for r in reads:
                st_ = self.res.setdefault(r, [None, {}])
                if st_[1].get(sk, 0) < final[1]:
                    st_[1][sk] = final[1]
        self.cnt[sk] = final[1]
        self.res[bank_key] = [final, {}]
        return final

    def mm1(self, bank_key, fn, reads, first):
        sk = "E_pe"
        self._deps("pe", reads, (bank_key,) if first else ())
        self.cnt[sk] = self.cnt.get(sk, 0) + 1
        t = (sk, self.cnt[sk])
        self.streams["pe"].append(("i", fn, sk, 1))
        for r in reads:
            st_ = self.res.setdefault(r, [None, {}])
            st_[1][sk] = t[1]
        old = self.res.get(bank_key)
        self.res[bank_key] = [t, {} if (first or not old) else old[1]]
        return t


def _replay(e, stream, semh):
    for it in stream:
        if it[0] == "w":
            e.wait_ge(semh[it[1]], it[2])
        else:
            inst = it[1](e)
            if it[2] is not None:
                inst.then_inc(semh[it[2]], it[3])


class _Stop(Exception):
    pass


def build_program(n_prompt_seq=2, n_quarters=4, with_sample=True, n_layers=L, debug=False, stop=None):
    nc = bass.Bass("TRN2", target_bir_lowering=False)
    R = Rec()

    def din(name, shape):
        return nc.dram_tensor(name, list(shape), F32, kind="ExternalInput").ap()

    def dout(name, shape):
        return nc.dram_tensor(name, list(shape), F32, kind="ExternalOutput").ap()

    x_p = din("x_p", (2, SEQ, D))
    x_s = din("x_s", (DEC_SEQ, D))
    c_ckv = din("c_ckv", (L, PAST, KV_LORA))
    c_kr = din("c_kr", (L, PAST, QK_ROPE))
    hist0 = din("hist0", (128, L * KC * 2))
    smallp_d = din("smallp", (128, L * NSP))
    rope_d = din("rope", (2, 128, NKEYMAX))
    ident_d = din("ident", (128, 128))
    w_in = din("w_in", (L, D, D_IN))
    w_uq = din("w_uq", (L, Q_LORA, H * QH))
    w_ukv = din("w_ukv", (L, KV_LORA, H * 128))
    w_conv_out = din("w_conv_out", (L, D, D))
    w_attn_out = din("w_attn_out", (L, D, D))
    w_merge = din("w_merge", (L, D, D))
    w_gate_up = din("w_gate_up", (L, D, 2 * DFF))
    w_down = din("w_down", (L, DFF, D))

    y_p = dout("y_p", (2, SEQ, D))
    y_s = dout("y_s", (DEC_SEQ, D))
    nconv_p = dout("nconv_p", (L, 2, 2, D))
    nckv_p = dout("nckv_p", (L, 2, SEQ, KV_LORA))
    nkr_p = dout("nkr_p", (L, 2, SEQ, QK_ROPE))
    nconv_s = dout("nconv_s", (L, 2, D))
    nckv_s = dout("nckv_s", (L, DEC_SEQ, KV_LORA))
    nkr_s = dout("nkr_s", (L, DEC_SEQ, QK_ROPE))
    dbg = {}
    if debug:
        for nm, shp in (("d_xn", (128, KC * SEG)), ("d_bc", (128, KC * SEG)), ("d_qn", (128, 3 * SEG)),
                        ("d_attn", (128, KC * SEG)), ("d_z", (128, KC * SEG)), ("d_xres", (128, KC * SEG)),
                        ("d_xres2", (128, KC * SEG)), ("d_qt", (128, SEG)), ("d_kt", (128, SEG))):
            dbg[nm] = dout(nm, shp)

    wv = {
        "w_in": w_in.rearrange("l (kc p) m -> l p kc m", p=128),
        "w_uq": w_uq.rearrange("l (kc p) m -> l p kc m", p=128),
        "w_ukv": w_ukv.rearrange("l (kc p) m -> l p kc m", p=128),
        "w_conv_out": w_conv_out.rearrange("l (kc p) m -> l p kc m", p=128),
        "w_attn_out": w_attn_out.rearrange("l (kc p) m -> l p kc m", p=128),
        "w_merge": w_merge.rearrange("l (kc p) m -> l p kc m", p=128),
        "w_gate_up": w_gate_up.rearrange("l (kc p) m -> l p kc m", p=128),
        "w_down": w_down.rearrange("l (kc p) m -> l p kc m", p=128),
    }

    with ExitStack() as es:
        def sb(name, shape, dt):
            return es.enter_context(nc.sbuf_tensor(name, list(shape), dt))

        xresT = sb("xresT", (128, KC, SEG), F32)
        xnT = sb("xnT", (128, KC, SEG), BF16)
        big = sb("big", (128, 24, SEG), BF16)
        qnT = sb("qnT", (128, 3, SEG), BF16)
        cT = sb("cT", (128, L, 2, NKEYMAX), BF16)
        krT = sb("krT", (128, L, NKEYMAX), BF16)
        scr = sb("scr", (128, KC, SEG), F32)
        wbuf = sb("wbuf", (128, NSLOT, KC, 128), BF16)
        wuq = sb("wuq", (128, 3, H * QH), BF16)
        wuqB = sb("wuqB", (128, 3, H, 32), BF16)
        wuqA = sb("wuqA", (128, 3, H, 32), BF16)
        wukv = sb("wukv", (128, 2, H * 128), BF16)
        wkrB = sb("wkrB", (128, KC, 32), BF16)
        KT = sb("KT", (128, 2, NKEYMAX), BF16)
        QT = sb("QT", (128, 2, SEG), BF16)
        VB = sb("VB", (128, 2, 17, 128), BF16)
        PT = sb("PT", (128, 3, SEG), BF16)
        Rr = sb("Rr", (128, SEG), F32)
        u_t = sb("u_t", (128, SEG + 2), F32)
        cv_t = sb("cv_t", (128, SEG), F32)
        cg_t = sb("cg_t", (128, SEG), F32)
        sa_t = sb("sa_t", (128, SEG), F32)
        sb_t = sb("sb_t", (128, SEG), F32)
        sg_t = sb("sg_t", (128, 2, SEG), F32)
        sq_t = sb("sq_t", (128, 2, SEG), BF16)
        rt_t = sb("rt_t", (128, SEG), F32)
        rstd_t = sb("rstd_t", (128, SEG), F32)
        kro_t = sb("kro_t", (128, SEG), F32)
        t1_t = sb("t1_t", (128, SEG), F32)
        t2_t = sb("t2_t", (128, SEG), F32)
        cos_t = sb("cos_t", (128, SEG), F32)
        sin_t = sb("sin_t", (128, SEG), F32)
        xtok = sb("xtok", (128, 2, D), F32)
        xin = sb("xin", (128, 2, D), F32)
        ost = sb("ost", (128, 2, 288), F32)
        hist = sb("hist", (128, L, KC, 2), F32)
        smallp = sb("smallp_sb", (128, L, NSP), F32)
        ident = sb("ident_sb", (128, 128), F32)
        ones = sb("ones_sb", (128, 128), BF16)
        ps = [es.enter_context(nc.psum_tensor(f"ps{i}", [128, 512], F32)) for i in range(8)]

        bcT = lambda m: big[:, m, :]
        attnT_i, zT_i = 8, 16
        qlat_i, ckvs_i, cnew_i = 0, 3, 5

        rr = {"mm": 0, "aux": 0, "acc": 0}
        MM_BANKS, AUX_BANKS, ACC_BANKS, SS_BANK = (0, 1, 2), (4, 7), (5, 6), 3

        def bank(cls):
            lst = {"mm": MM_BANKS, "aux": AUX_BANKS, "acc": ACC_BANKS}[cls]
            b = lst[rr[cls] % len(lst)]
            rr[cls] += 1
            return b

        def act(out, in_, func, reads, writes, scale=None, bias=None):
            kw = {}
            if scale is not None:
                kw["scale"] = scale
            if bias is not None:
                kw["bias"] = bias
            return R.op("act", lambda e: e.activation(out=out, in_=in_, func=func, **kw), reads, writes)

        def tt(out, in0, in1, op, reads, writes):
            return R.op("dve", lambda e: e.tensor_tensor(out=out, in0=in0, in1=in1, op=op), reads, writes)

        def stt(out, in0, scalar, in1, op0, op1, reads, writes):
            return R.op("dve", lambda e: e.scalar_tensor_tensor(out=out, in0=in0, scalar=scalar, in1=in1,
                                                               op0=op0, op1=op1), reads, writes)

        def dcopy(out, in_, reads, writes):
            return R.op("dve", lambda e: e.tensor_copy(out=out, in_=in_), reads, writes)

        def acopy(out, in_, reads, writes, scale=None):
            if scale is None:
                return R.op("act", lambda e: e.copy(out=out, in_=in_), reads, writes)
            return R.op("act", lambda e: e.mul(out=out, in_=in_, mul=scale), reads, writes)

        def mm(out, lhsT, rhs):
            return lambda e, st, sp: e.matmul(out, lhsT, rhs, start=st, stop=sp)

        def tr(out, in_, idn):
            return lambda e, st, sp: e.transpose(out, in_, idn)

        out_tickets = []

        def chk(name):
            if stop == name:
                raise _Stop()

        class WS:
            def __init__(self):
                self.sched = []
                self.emitted = 0

            def advance(self, upto):
                upto = min(upto, len(self.sched) - 1)
                while self.emitted <= upto:
                    i = self.emitted
                    (l, name, kc0, nkc, c0, ncols) = self.sched[i]
                    slot = i % NSLOT
                    src = wv[name][l, :, kc0:kc0 + nkc, c0:c0 + ncols]
                    dst = wbuf[:, slot, 0:nkc, 0:ncols]
                    R.dma("pool", lambda e, dst=dst, src=src: e.dma_start(out=dst, in_=src), f"W{slot}",
                          reads=(), writes=(("w", slot),))
                    self.emitted += 1

            def use(self, i, desc):
                assert self.sched[i] == desc, (i, self.sched[i], desc)
                assert i < self.emitted, (i, self.emitted)
                return i % NSLOT

        ws = WS()
        wpos = [0]

        def wtile(desc):
            i = wpos[0]
            wpos[0] += 1
            slot = ws.use(i, desc)
            return slot, ("w", slot)

        def wdone():
            ws.advance(wpos[0] - 1 + NSLOT)

        segs = []
        for s in range(n_prompt_seq):
            for q in range(n_quarters):
                segs.append(dict(kind="p", seq=s, q=q, N=SEG, pos0=q * SEG, key0=q * SEG))
        if with_sample:
            segs.append(dict(kind="s", seq=0, q=4, N=DEC_SEQ, pos0=PAST, key0=PAST))

        def sl_sched(l):
            out = []
            for m in range(KC):
                out.append((l, "w_in", 0, KC, O1 + 128 * m, 128))
                out.append((l, "w_in", 0, KC, O2 + 128 * m, 128))
                out.append((l, "w_in", 0, KC, 128 * m, 128))
            for i in range(3):
                out.append((l, "w_in", 0, KC, O3 + 128 * i, 128))
            for i in range(2):
                out.append((l, "w_in", 0, KC, O4 + 128 * i, 128))
            out.append((l, "w_in", 0, KC, O5, 32))
            for m in range(KC):
                out.append((l, "w_conv_out", 0, KC, 128 * m, 128))
                out.append((l, "w_in", 0, KC, O6 + 128 * m, 128))
                out.append((l, "w_attn_out", 0, KC, 128 * m, 128))
                out.append((l, "w_in", 0, KC, O7 + 128 * m, 128))
            for m in range(KC):
                out.append((l, "w_merge", 0, KC, 128 * m, 128))
            for f in range(FC):
                out.append((l, "w_gate_up", 0, KC, 128 * f, 128))
                out.append((l, "w_gate_up", 0, KC, DFF + 128 * f, 128))
            for m in range(KC):
                out.append((l, "w_down", 0, 8, 128 * m, 128))
                out.append((l, "w_down", 8, 8, 128 * m, 128))
                out.append((l, "w_down", 16, 6, 128 * m, 128))
            return out

        for sg in segs:
            for l in range(n_layers):
                ws.sched.extend(sl_sched(l))

        R.dma("sp", lambda e: e.dma_start(out=smallp[:, :, :].rearrange("p l c -> p (l c)"), in_=smallp_d[:, :]),
              "S_small", writes=("smallp",))
        R.dma("sp", lambda e: e.dma_start(out=ident[:, :], in_=ident_d[:, :]), "S_ident", writes=("ident",))
        R.op("dve", lambda e: e.memset(ones[:, :], 1.0), writes=("ones",))
        R.op("dve", lambda e: e.memset(VB[:, 0, :, 64:128], 1.0), writes=(("V", 0),))
        R.op("dve", lambda e: e.memset(VB[:, 1, :, 0:64], 1.0), writes=(("V", 1),))
        R.op("dve", lambda e: e.memset(KT[:, :, :], 0.0), writes=(("KT", 0), ("KT", 1)))
        R.op("dve", lambda e: e.memset(QT[:, :, :], 0.0), writes=(("QT", 0), ("QT", 1)))
        ws.advance(NSLOT - 1)

        def sp_col(l, c0, n=1):
            return smallp[:, l, c0:c0 + n]

        def load_wuq_wukv(l):
            for kc in range(3):
                R.dma("pool", lambda e, kc=kc: e.dma_start(out=wuq[:, kc, :], in_=wv["w_uq"][l, :, kc, :]),
                      "S_wuq", writes=("wuq",) if kc == 0 else ())
            R.res["wuq"] = [("S_wuq", R.cnt["S_wuq"]), {}]
            first = True
            for kc in range(2):
                for hf in range(2):
                    R.dma("pool", lambda e, kc=kc, hf=hf: e.dma_start(
                        out=wukv[:, kc, hf * 1024:(hf + 1) * 1024], in_=wv["w_ukv"][l, :, kc, hf * 1024:(hf + 1) * 1024]),
                        "S_wukv", writes=("wukv",) if first else ())
                    first = False
            R.res["wukv"] = [("S_wukv", R.cnt["S_wukv"]), {}]

        def prep_wuqAB():
            wv4 = wuq[:, :, :].rearrange("p k (h d) -> p k h d", d=QH)
            acopy(wuqB[:, :, :, 0:16], wv4[:, :, :, 80:96], reads=("wuq",), writes=("wuqB",), scale=-1.0)
            acopy(wuqB[:, :, :, 16:32], wv4[:, :, :, 64:80], reads=("wuq",), writes=("wuqB",))
            acopy(wuqA[:, :, :, :], wv4[:, :, :, 64:96], reads=("wuq",), writes=("wuqA",))

        def norm_finish(ssb, nfeat, N):
            act(rt_t[:, :N], ps[ssb][:, :N], AF.Ln, reads=(("ps", ssb),), writes=("rt",), scale=1.0 / nfeat,
                bias=EPS)
            act(rstd_t[:, :N], rt_t[:, :N], AF.Exp, reads=("rt",), writes=("rstd",), scale=-0.5)

        def prenorm(l, gcol, N):
            ssb = SS_BANK
            for kc in range(KC):
                b = kc % 2
                act(sq_t[:, b, :N], xresT[:, kc, :N], AF.Square, reads=(("xres", kc),), writes=(("sq", b),))
                R.mm1(("ps", ssb), (lambda e, b=b, kc=kc: e.matmul(ps[ssb][:, :N], ones[:, :], sq_t[:, b, :N],
                                                                 start=(kc == 0), stop=(kc == KC - 1))),
                      reads=(("sq", b), "ones"), first=(kc == 0))
            norm_finish(ssb, D, N)
            for kc in range(KC):
                stt(xnT[:, kc, :N], xresT[:, kc, :N], sp_col(l, gcol + kc), rstd_t[:, :N], ALU.mult, ALU.mult,
                    reads=(("xres", kc), "rstd", "smallp"), writes=(("xn", kc),))

        def dense_group(l, name, c0, ncols, rhs_fn, rhs_keys, nkc_total=KC, pieces=None):
            b = bank("mm")
            mms = []
            pieces = pieces or [(0, nkc_total)]
            for (k0, nk) in pieces:
                slot, wkey = wtile((l, name, k0, nk, c0, ncols))
                for k in range(nk):
                    kc = k0 + k
                    mms.append((mm(ps[b][0:ncols, :rhs_fn.N], wbuf[:, slot, k, 0:ncols], rhs_fn(kc)),
                                (wkey, rhs_keys(kc))))
            R.mmgroup(("ps", b), mms)
            wdone()
            return b

        class RhsFn:
            def __init__(self, fn, N):
                self.fn = fn
                self.N = N

            def __call__(self, kc):
                return self.fn(kc)

        def issue_input(sg, blk):
            if blk in sg.setdefault("in_issued", set()):
                return
            sg["in_issued"].add(blk)
            N_, pos0_ = sg["N"], sg["pos0"]
            nb = min(128, N_ - blk * 128)
            if sg["kind"] == "p":
                src = x_p[sg["seq"], pos0_ + blk * 128: pos0_ + blk * 128 + nb, :]
            else:
                src = x_s[blk * 128: blk * 128 + nb, :]
            xb = blk % 2
            R.dma("sp", lambda e: e.dma_start(out=xin[0:nb, xb, :], in_=src), f"S_xin{xb}", writes=(("xin", xb),))

        def issue_rope(sg):
            if sg.get("rope_issued"):
                return
            sg["rope_issued"] = True
            N_, pos0_ = sg["N"], sg["pos0"]
            R.dma("sp", lambda e: e.dma_start(out=cos_t[:, :N_], in_=rope_d[0, :, pos0_:pos0_ + N_]), "S_cos",
                  writes=("cos",))
            R.dma("sp", lambda e: e.dma_start(out=sin_t[:, :N_], in_=rope_d[1, :, pos0_:pos0_ + N_]), "S_sin",
                  writes=("sin",))

        def prefetch_next(sg):
            i = segs.index(sg)
            if i + 1 < len(segs):
                nx = segs[i + 1]
                for blk in range(min(2, (nx["N"] + 127) // 128)):
                    issue_input(nx, blk)
                issue_rope(nx)

        def segment_layer(sg, l, first_layer, last_layer, nxt_l=None):
            N = sg["N"]
            kind = sg["kind"]
            key0 = sg["key0"]
            pos0 = sg["pos0"]
            q = sg["q"]
            seq = sg["seq"]
            nblk = (N + 127) // 128
            xn_rhs = RhsFn(lambda kc: xnT[:, kc, :N], N)
            xn_keys = lambda kc: ("xn", kc)

            if first_layer:
                for blk in range(nblk):
                    nb = min(128, N - blk * 128)
                    issue_input(sg, blk)
                    xb = blk % 2
                    for half in range(2):
                        b = bank("mm")
                        for j in range(4):
                            kc = half * 4 + j
                            R.mmgroup(("ps", b) if j == 0 else ("psx", b), [
                                (tr(ps[b][:, j * 128: j * 128 + nb], xin[0:nb, xb, kc * 128:(kc + 1) * 128],
                                    ident[0:nb, 0:nb]), (("xin", xb), "ident"))])
                        R.res[("ps", b)] = [("E_pe", R.cnt["E_pe"]), {}]
                        src_ps = ps[b][:, :].rearrange("p (j t) -> p j t", t=128)[:, :, 0:nb]
                        acopy(xresT[:, half * 4:half * 4 + 4, blk * 128: blk * 128 + nb], src_ps,
                              reads=(("ps", b),), writes=tuple(("xres", half * 4 + j) for j in range(4)))
                issue_rope(sg)

            chk("input")
            if kind == "p" and q == 0:
                R.op("dve", lambda e: e.memset(hist[:, l, :, :], 0.0), writes=(("hist", l),))
            if kind == "s":
                R.dma("sp", lambda e: e.dma_start(
                    out=hist[:, l, :, :].rearrange("p k j -> p (k j)"), in_=hist0[:, l * 16:(l + 1) * 16]),
                    "S_hist", writes=(("hist", l),))
                for blk in range(PAST // 128):
                    ob = blk % 2
                    R.dma("sp", lambda e, blk=blk, ob=ob: e.dma_start(
                        out=ost[:, ob, 0:256], in_=c_ckv[l, blk * 128:(blk + 1) * 128, :]), f"S_ost{ob}",
                        writes=(("ost", ob),))
                    R.dma("sp", lambda e, blk=blk, ob=ob: e.dma_start(
                        out=ost[:, ob, 256:288], in_=c_kr[l, blk * 128:(blk + 1) * 128, :]), f"S_ost{ob}",
                        writes=())
                    R.res[("ost", ob)] = [(f"S_ost{ob}", R.cnt[f"S_ost{ob}"]), {}]
                    b = bank("aux")
                    R.mmgroup(("ps", b), [(tr(ps[b][:, 0:128], ost[:, ob, 0:128], ident[:, :]), (("ost", ob), "ident"))])
                    R.mmgroup(("psx", b), [(tr(ps[b][:, 128:256], ost[:, ob, 128:256], ident[:, :]), (("ost", ob), "ident"))])
                    R.mmgroup(("psx", b), [(tr(ps[b][0:32, 256:384], ost[:, ob, 256:288], ident[:, :]), (("ost", ob), "ident"))])
                    R.res[("ps", b)] = [("E_pe", R.cnt["E_pe"]), {}]
                    jt = blk // 4
                    acopy(cT[:, l, :, blk * 128:(blk + 1) * 128],
                          ps[b][:, 0:256].rearrange("p (k t) -> p k t", t=128),
                          reads=(("ps", b),), writes=(("cT", l, jt),))
                    acopy(krT[64:96, l, blk * 128:(blk + 1) * 128], ps[b][0:32, 256:384],
                          reads=(("ps", b),), writes=(("krT", l, jt),))

            chk("hist")
            if sg.get("first_sl") and l == 0:
                load_wuq_wukv(l)
                prep_wuqAB()

            chk("wuq")
            prenorm(l, C_GPRE, N)

            chk("prenorm")
            for m in range(KC):
                b_cg = dense_group(l, "w_in", O1 + 128 * m, 128, xn_rhs, xn_keys)
                b_xi = dense_group(l, "w_in", O2 + 128 * m, 128, xn_rhs, xn_keys)
                acopy(cg_t[:, :N], ps[b_cg][:, :N], reads=(("ps", b_cg),), writes=("cg",))
                dcopy(u_t[:, 0:2], hist[:, l, m, :], reads=(("hist", l),), writes=("u",))
                tt(u_t[:, 2:2 + N], ps[b_xi][:, :N], cg_t[:, :N], ALU.mult, reads=(("ps", b_xi), "cg", "u"),
                   writes=("u",))
                cws = [sp_col(l, C_CW + 3 * m + j) for j in range(3)]
                cw = lambda j, cws=cws: cws[j]
                R.op("dve", lambda e, c2=cws[2]: e.tensor_scalar(out=cv_t[:, :N], in0=u_t[:, 2:2 + N], scalar1=c2,
                                                                 scalar2=None, op0=ALU.mult),
                     reads=("u", "smallp"), writes=("cv",))
                stt(cv_t[:, :N], u_t[:, 1:1 + N], cw(1), cv_t[:, :N], ALU.mult, ALU.add, reads=("u", "cv", "smallp"),
                    writes=("cv",))
                stt(cv_t[:, :N], u_t[:, 0:N], cw(0), cv_t[:, :N], ALU.mult, ALU.add, reads=("u", "cv", "smallp"),
                    writes=("cv",))
                dcopy(hist[:, l, m, :], u_t[:, N:N + 2], reads=("u",), writes=(("hist", l),))
                b_bg = dense_group(l, "w_in", 128 * m, 128, xn_rhs, xn_keys)
                tt(big[:, m, :N], ps[b_bg][:, :N], cv_t[:, :N], ALU.mult, reads=(("ps", b_bg), "cv"),
                   writes=(("big", m),))
            if (kind == "p" and q == n_quarters - 1) or kind == "s":
                if kind == "p":
                    dst = nconv_p[l, seq, :, :].rearrange("j (k p) -> p k j", p=128)
                else:
                    dst = nconv_s[l, :, :].rearrange("j (k p) -> p k j", p=128)
                for kc in range(KC):
                    out_tickets.append(R.dma("sp", lambda e, dst=dst, kc=kc: e.dma_start(
                        out=dst[:, kc, :], in_=hist[:, l, kc, :], allow_slow_non_contiguous=True), f"S_histout{l}",
                        reads=(("hist", l),)))

            chk("conv")
            for i in range(3):
                bq = dense_group(l, "w_in", O3 + 128 * i, 128, xn_rhs, xn_keys)
                acopy(scr[:, qlat_i + i, :N], ps[bq][:, :N], reads=(("ps", bq),), writes=(("scr", qlat_i + i),))
            for i in range(2):
                bq = dense_group(l, "w_in", O4 + 128 * i, 128, xn_rhs, xn_keys)
                acopy(scr[:, ckvs_i + i, :N], ps[bq][:, :N], reads=(("ps", bq),), writes=(("scr", ckvs_i + i),))
            jt_new = q
            slot, wkey = wtile((l, "w_in", 0, KC, O5, 32))
            acopy(wkrB[:, :, 0:16], wbuf[:, slot, :, 16:32], reads=(wkey,), writes=("wkrB",), scale=-1.0)
            acopy(wkrB[:, :, 16:32], wbuf[:, slot, :, 0:16], reads=(wkey,), writes=("wkrB",))
            bA = bank("aux")
            R.mmgroup(("ps", bA), [(mm(ps[bA][0:32, :N], wbuf[:, slot, kc, 0:32], xnT[:, kc, :N]),
                                    (wkey, ("xn", kc))) for kc in range(KC)])
            wdone()
            bB = bank("aux")
            R.mmgroup(("ps", bB), [(mm(ps[bB][0:32, :N], wkrB[:, kc, :], xnT[:, kc, :N]), ("wkrB", ("xn", kc)))
                                   for kc in range(KC)])
            tt(t1_t[0:32, :N], ps[bA][0:32, :N], cos_t[0:32, :N], ALU.mult, reads=(("ps", bA), "cos"), writes=("t1",))
            tt(t2_t[0:32, :N], ps[bB][0:32, :N], sin_t[0:32, :N], ALU.mult, reads=(("ps", bB), "sin"), writes=("t2",))
            tt(kro_t[0:32, :N], t1_t[0:32, :N], t2_t[0:32, :N], ALU.add, reads=("t1", "t2"), writes=("kro",))
            acopy(krT[64:96, l, key0:key0 + N], kro_t[0:32, :N], reads=("kro",), writes=(("krT", l, jt_new),))
            for i in range(3):
                b = i % 2
                act(sq_t[:, b, :N], scr[:, qlat_i + i, :N], AF.Square, reads=(("scr", qlat_i + i),),
                    writes=(("sq", b),))
                R.mm1(("ps", SS_BANK), (lambda e, b=b, i=i: e.matmul(ps[SS_BANK][:, :N], ones[:, :], sq_t[:, b, :N],
                                                                    start=(i == 0), stop=(i == 2))),
                      reads=(("sq", b), "ones"), first=(i == 0))
            norm_finish(SS_BANK, Q_LORA, N)
            for i in range(3):
                stt(qnT[:, i, :N], scr[:, qlat_i + i, :N], sp_col(l, C_GQ + i), rstd_t[:, :N], ALU.mult, ALU.mult,
                    reads=(("scr", qlat_i + i), "rstd", "smallp"), writes=(("qn", i),))
            for i in range(2):
                b = i % 2
                act(sq_t[:, b, :N], scr[:, ckvs_i + i, :N], AF.Square, reads=(("scr", ckvs_i + i),),
                    writes=(("sq", b),))
                R.mm1(("ps", SS_BANK), (lambda e, b=b, i=i: e.matmul(ps[SS_BANK][:, :N], ones[:, :], sq_t[:, b, :N],
                                                                    start=(i == 0), stop=(i == 1))),
                      reads=(("sq", b), "ones"), first=(i == 0))
            norm_finish(SS_BANK, KV_LORA, N)
            for i in range(2):
                stt(scr[:, cnew_i + i, :N], scr[:, ckvs_i + i, :N], sp_col(l, C_GKV + i), rstd_t[:, :N], ALU.mult,
                    ALU.mult, reads=(("scr", ckvs_i + i), "rstd", "smallp"), writes=(("scr", cnew_i + i),))
                acopy(cT[:, l, i, key0:key0 + N], scr[:, cnew_i + i, :N], reads=(("scr", cnew_i + i),),
                      writes=(("cT", l, jt_new),))
            chk("lowrank")
            for blk in range(nblk):
                nb = min(128, N - blk * 128)
                ob = blk % 2
                b = bank("aux")
                R.mmgroup(("ps", b), [(tr(ps[b][0:nb, 0:128], scr[:, cnew_i, blk * 128:blk * 128 + nb], ident[:, :]),
                                       (("scr", cnew_i), "ident"))])
                R.mmgroup(("psx", b), [(tr(ps[b][0:nb, 128:256], scr[:, cnew_i + 1, blk * 128:blk * 128 + nb],
                                          ident[:, :]), (("scr", cnew_i + 1), "ident"))])
                R.mmgroup(("psx", b), [(tr(ps[b][0:nb, 256:288], kro_t[0:32, blk * 128:blk * 128 + nb],
                                          ident[0:32, 0:32]), ("kro", "ident"))])
                R.res[("ps", b)] = [("E_pe", R.cnt["E_pe"]), {}]
                dcopy(ost[0:nb, ob, :], ps[b][0:nb, 0:288], reads=(("ps", b),), writes=(("ost", ob),))
                if kind == "p":
                    r0 = pos0 + blk * 128
                    d1 = nckv_p[l, seq, r0:r0 + nb, :]
                    d2 = nkr_p[l, seq, r0:r0 + nb, :]
                else:
                    d1 = nckv_s[l, 0:nb, :]
                    d2 = nkr_s[l, 0:nb, :]
                out_tickets.append(R.dma("sp", lambda e, d1=d1, ob=ob, nb=nb: e.dma_start(
                    out=d1, in_=ost[0:nb, ob, 0:256]), f"S_ost{ob}", reads=(("ost", ob),)))
                out_tickets.append(R.dma("sp", lambda e, d2=d2, ob=ob, nb=nb: e.dma_start(
                    out=d2, in_=ost[0:nb, ob, 256:288]), f"S_ost{ob}", reads=(("ost", ob),)))

            chk("ctxout")
            nkeys = key0 + N
            ktiles = [(j * 512, min(512, nkeys - j * 512)) for j in range((nkeys + 511) // 512)]
            kblocks = [(j * 128, min(128, nkeys - j * 128)) for j in range((nkeys + 127) // 128)]
            for hb in range(2):
                for jt, (k0, nk) in enumerate(ktiles):
                    dcopy(KT[64:96, hb, k0:k0 + nk], krT[64:96, l, k0:k0 + nk], reads=(("krT", l, jt),),
                          writes=(("KT", hb),))
            b4 = SS_BANK

            def prep_pieces(h):
                hb = h % 2
                eo = h % 2
                pieces = []
                ev = acopy if len(kblocks) <= 8 else dcopy
                ev_kt = acopy if kind == "s" else ev
                for jt, (k0, nk) in enumerate(ktiles):
                    def p_kt(jt=jt, k0=k0, nk=nk):
                        b = bank("aux")
                        R.mmgroup(("ps", b), [(mm(ps[b][0:64, :nk], wukv[:, kc, h * 128:h * 128 + 64],
                                                  cT[:, l, kc, k0:k0 + nk]), ("wukv", ("cT", l, jt)))
                                              for kc in range(2)])
                        ev_kt(KT[0:64, hb, k0:k0 + nk], ps[b][0:64, :nk], reads=(("ps", b),), writes=(("KT", hb),))
                    pieces.append(p_kt)
                for g0 in range(0, len(kblocks), 8):
                    def p_v(g0=g0):
                        grp = kblocks[g0:g0 + 8]
                        b = bank("aux")
                        for j, (k0, nk) in enumerate(grp):
                            R.mmgroup(("ps", b) if j == 0 else ("psx", b), [
                                (mm(ps[b][0:nk, j * 64:(j + 1) * 64], cT[:, l, kc, k0:k0 + nk],
                                    wukv[:, kc, h * 128 + 64:h * 128 + 128]), ("wukv", ("cT", l, k0 // 512)))
                                for kc in range(2)])
                        R.res[("ps", b)] = [("E_pe", R.cnt["E_pe"]), {}]
                        vc0 = 0 if eo == 0 else 64
                        full = [g for g in grp if g[1] == 128]
                        if full:
                            nf = len(full)
                            ev(VB[:, eo, g0:g0 + nf, vc0:vc0 + 64],
                                  ps[b][:, 0:nf * 64].rearrange("p (j v) -> p j v", v=64),
                                  reads=(("ps", b),), writes=(("V", eo),))
                        if len(full) < len(grp):
                            j = len(full)
                            nk = grp[j][1]
                            ev(VB[0:nk, eo, g0 + j, vc0:vc0 + 64], ps[b][0:nk, j * 64:(j + 1) * 64],
                                  reads=(("ps", b),), writes=(("V", eo),))
                    pieces.append(p_v)

                def p_rot4():
                    g = h // 4
                    bA4 = bank("aux")
                    R.mmgroup(("ps", bA4), [(mm(ps[bA4][:, :N], wuqA[:, kc, 4 * g:4 * g + 4, :], qnT[:, kc, :N]),
                                             ("wuqA", ("qn", kc))) for kc in range(3)])
                    R.mmgroup(("ps", b4), [(mm(ps[b4][:, :N], wuqB[:, kc, 4 * g:4 * g + 4, :], qnT[:, kc, :N]),
                                            ("wuqB", ("qn", kc))) for kc in range(3)])
                    tt(t1_t[:, :N], ps[bA4][:, :N], cos_t[:, :N], ALU.mult, reads=(("ps", bA4), "cos"), writes=("t1",))
                    tt(t2_t[:, :N], ps[b4][:, :N], sin_t[:, :N], ALU.mult, reads=(("ps", b4), "sin"), writes=("t2",))
                    tt(kro_t[:, :N], t1_t[:, :N], t2_t[:, :N], ALU.add, reads=("t1", "t2"), writes=("kro",))

                def p_q():
                    bA = bank("aux")
                    R.mmgroup(("ps", bA), [(mm(ps[bA][0:64, :N], wuq[:, kc, h * QH:h * QH + 64], qnT[:, kc, :N]),
                                            ("wuq", ("qn", kc))) for kc in range(3)])
                    ev(QT[0:64, hb, :N], ps[bA][0:64, :N], reads=(("ps", bA),), writes=(("QT", hb),))
                    j4 = h % 4
                    R.op("pool", lambda e: e.tensor_copy(out=QT[64:96, hb, :N], in_=kro_t[32 * j4:32 * j4 + 32, :N]),
                         reads=("kro",), writes=(("QT", hb),))
                    if debug and h == 0 and l == 0 and sg is segs[0]:
                        R.dma("pool", lambda e: e.dma_start(out=dbg["d_qt"][:, :], in_=QT[:, 0, :]), "S_dbg",
                              reads=(("QT", 0),))
                        R.dma("pool", lambda e: e.dma_start(out=dbg["d_kt"][:, :], in_=KT[:, 0, 0:SEG]), "S_dbg",
                              reads=(("KT", 0),))
                if h % 4 == 0:
                    pieces.insert(0, p_rot4)
                    pieces.insert(1, p_q)
                else:
                    pieces.insert(0, p_q)
                return pieces

            def head_loop(h, pieces):
                hb = h % 2
                eo = h % 2
                accb = bank("acc")
                nkb = len(kblocks)
                info = []
                for kb, (k0, nk) in enumerate(kblocks):
                    if kind == "p" and k0 >= key0:
                        bd = (k0 - key0) // 128
                        info.append((k0, nk, bd, 128 * bd))
                    else:
                        info.append((k0, nk, None, 0))
                sbanks = {}

                def issue_s(kb):
                    k0, nk, bd, qlo = info[kb]
                    sbk = bank("mm")
                    R.mmgroup(("ps", sbk), [(mm(ps[sbk][0:nk, qlo:N], KT[0:QH, hb, k0:k0 + nk], QT[0:QH, hb, qlo:N]),
                                             (("KT", hb), ("QT", hb)))])
                    sbanks[kb] = sbk

                for kb in range(min(2, nkb)):
                    issue_s(kb)
                for kb in range(nkb):
                    k0, nk, bd, qlo = info[kb]
                    sbk = sbanks[kb]
                    pb = kb % 3
                    act(PT[0:nk, pb, qlo:N], ps[sbk][0:nk, qlo:N], AF.Exp, reads=(("ps", sbk),),
                        writes=(("PT", pb),), scale=ATTN_SCALE)
                    def pv(c0, c1, r1, first, last, kb=kb, pb=pb):
                        o_ap, l_ap, r_ap = ps[accb][:, c0:c1], VB[0:r1, eo, kb, :], PT[0:r1, pb, c0:c1]
                        R.mm1(("ps", accb), (lambda e: e.matmul(o_ap, l_ap, r_ap, start=first, stop=last)),
                              reads=(("V", eo), ("PT", pb)), first=first)
                    if bd is None:
                        pv(qlo, N, nk, kb == 0, kb == nkb - 1)
                    else:
                        pv(qlo + 64, N, nk, kb == 0, False)
                        pv(qlo, qlo + 64, 64, False, kb == nkb - 1)
                    if kb + 2 < nkb:
                        issue_s(kb + 2)
                    if pieces:
                        pieces.pop(0)()
                while pieces:
                    pieces.pop(0)()
                dlo, slo = (0, 64) if eo == 0 else (64, 0)
                act(Rr[dlo:dlo + 64, :N], ps[accb][slo:slo + 64, :N], AF.Ln, reads=(("ps", accb),), writes=("R",))
                act(Rr[dlo:dlo + 64, :N], Rr[dlo:dlo + 64, :N], AF.Exp, reads=("R",), writes=("R",), scale=-1.0)
                tt(big[dlo:dlo + 64, attnT_i + h // 2, :N], ps[accb][dlo:dlo + 64, :N], Rr[dlo:dlo + 64, :N], ALU.mult,
                   reads=(("ps", accb), "R"), writes=(("big", attnT_i + h // 2),))

            for p in prep_pieces(0):
                p()
            for h in range(H):
                nxt = prep_pieces(h + 1) if h + 1 < H else []
                head_loop(h, nxt)

            chk("attn")
            if nxt_l is not None:
                load_wuq_wukv(nxt_l)
            bc_rhs = RhsFn(lambda kc: big[:, kc, :N], N)
            bc_keys = lambda kc: ("big", kc)
            at_rhs = RhsFn(lambda kc: big[:, attnT_i + kc, :N], N)
            at_keys = lambda kc: ("big", attnT_i + kc)
            for m in range(KC):
                b_ya = dense_group(l, "w_conv_out", 128 * m, 128, bc_rhs, bc_keys)
                b_ga = dense_group(l, "w_in", O6 + 128 * m, 128, xn_rhs, xn_keys)
                act(sa_t[:, :N], ps[b_ga][:, :N], AF.Sigmoid, reads=(("ps", b_ga),), writes=("sa",))
                tt(sa_t[:, :N], ps[b_ya][:, :N], sa_t[:, :N], ALU.mult, reads=(("ps", b_ya), "sa"), writes=("sa",))
                b_yb = dense_group(l, "w_attn_out", 128 * m, 128, at_rhs, at_keys)
                b_gb = dense_group(l, "w_in", O7 + 128 * m, 128, xn_rhs, xn_keys)
                act(sb_t[:, :N], ps[b_gb][:, :N], AF.Sigmoid, reads=(("ps", b_gb),), writes=("sb",))
                tt(sb_t[:, :N], ps[b_yb][:, :N], sb_t[:, :N], ALU.mult, reads=(("ps", b_yb), "sb"), writes=("sb",))
                tt(big[:, zT_i + m, :N], sa_t[:, :N], sb_t[:, :N], ALU.add, reads=("sa", "sb"),
                   writes=(("big", zT_i + m),))

            chk("z")
            def postnorm_residual(groups_fn, gcol):
                def ss_mm(m):
                    b = m % 2
                    R.mm1(("ps", SS_BANK), (lambda e, b=b, m=m: e.matmul(ps[SS_BANK][:, :N], ones[:, :],
                                                                        sq_t[:, b, :N], start=(m == 0),
                                                                        stop=(m == KC - 1))),
                          reads=(("sq", b), "ones"), first=(m == 0))
                for m in range(KC):
                    bm = groups_fn(m)
                    if m > 0:
                        ss_mm(m - 1)
                    b = m % 2
                    dcopy(scr[:, m, :N], ps[bm][:, :N], reads=(("ps", bm),), writes=(("scr", m),))
                    act(sq_t[:, b, :N], scr[:, m, :N], AF.Square, reads=(("scr", m),), writes=(("sq", b),))
                ss_mm(KC - 1)
                chk("pn_a")
                norm_finish(SS_BANK, D, N)
                chk("pn_b")
                for m in range(KC):
                    stt(scr[:, m, :N], scr[:, m, :N], sp_col(l, gcol + m), rstd_t[:, :N], ALU.mult, ALU.mult,
                        reads=(("scr", m), "rstd", "smallp"), writes=(("scr", m),))
                    tt(xresT[:, m, :N], xresT[:, m, :N], scr[:, m, :N], ALU.add, reads=(("xres", m), ("scr", m)),
                       writes=(("xres", m),))

            z_rhs = RhsFn(lambda kc: big[:, zT_i + kc, :N], N)
            z_keys = lambda kc: ("big", zT_i + kc)
            postnorm_residual(lambda m: dense_group(l, "w_merge", 128 * m, 128, z_rhs, z_keys), C_GPOST)
            if debug and l == 0 and sg is segs[0]:
                def dump(nm, t_, c0, n_, keyf, eng="pool"):
                    for k in range(n_):
                        R.dma(eng, lambda e, k=k: e.dma_start(out=dbg[nm][:, k * SEG:(k + 1) * SEG], in_=t_[:, c0 + k, :]),
                              "S_dbg", reads=(keyf(c0 + k),))
                dump("d_xn", xnT, 0, KC, lambda k: ("xn", k))
                dump("d_bc", big, 0, 8, lambda k: ("big", k))
                dump("d_qn", qnT, 0, 3, lambda k: ("qn", k))
                dump("d_attn", big, 8, 8, lambda k: ("big", k))
                dump("d_z", big, 16, 8, lambda k: ("big", k))
                dump("d_xres", xresT, 0, KC, lambda k: ("xres", k), "sp")
            chk("merge")
            if last_layer:
                prefetch_next(sg)
            prenorm(l, C_FPRE, N)
            for f in range(FC):
                b_g = dense_group(l, "w_gate_up", 128 * f, 128, xn_rhs, xn_keys)
                b_u = dense_group(l, "w_gate_up", DFF + 128 * f, 128, xn_rhs, xn_keys)
                sgb = f % 2
                act(sg_t[:, sgb, :N], ps[b_g][:, :N], AF.Silu, reads=(("ps", b_g),), writes=(("sg", sgb),))
                tt(big[:, f, :N], ps[b_u][:, :N], sg_t[:, sgb, :N], ALU.mult, reads=(("ps", b_u), ("sg", sgb)),
                   writes=(("big", f),))
            chk("ffn")
            if nxt_l is not None:
                prep_wuqAB()
            h_rhs = RhsFn(lambda kc: big[:, kc, :N], N)
            h_keys = lambda kc: ("big", kc)
            postnorm_residual(lambda m: dense_group(l, "w_down", 128 * m, 128, h_rhs, h_keys,
                                                    pieces=[(0, 8), (8, 8), (16, 6)]), C_FPOST)

            chk("down")
            if debug and l == 0 and sg is segs[0]:
                for k in range(KC):
                    R.dma("sp", lambda e, k=k: e.dma_start(out=dbg["d_xres2"][:, k * SEG:(k + 1) * SEG], in_=xresT[:, k, :]),
                          "S_dbg", reads=(("xres", k),))
            if last_layer:
                for blk in range(nblk):
                    nb = min(128, N - blk * 128)
                    for half in range(2):
                        b = bank("mm")
                        for j in range(4):
                            kc = half * 4 + j
                            R.mmgroup(("ps", b) if j == 0 else ("psx", b), [
                                (tr(ps[b][0:nb, j * 128:(j + 1) * 128], xresT[:, kc, blk * 128:blk * 128 + nb],
                                    ident[:, :]), (("xres", kc), "ident"))])
                        R.res[("ps", b)] = [("E_pe", R.cnt["E_pe"]), {}]
                        acopy(xtok[0:nb, blk % 2, half * 512:(half + 1) * 512], ps[b][0:nb, :], reads=(("ps", b),),
                              writes=(("xtok", blk % 2),))
                    if kind == "p":
                        r0 = pos0 + blk * 128
                        dst = y_p[seq, r0:r0 + nb, :]
                    else:
                        dst = y_s[blk * 128:blk * 128 + nb, :]
                    out_tickets.append(R.dma("sp", lambda e, dst=dst, nb=nb, ob=blk % 2: e.dma_start(
                        out=dst, in_=xtok[0:nb, ob, :]), f"S_xtok{blk % 2}", reads=(("xtok", blk % 2),)))

        try:
            sls = [(sg, l) for sg in segs for l in range(n_layers)]
            segs[0]["first_sl"] = True
            for i, (sg, l) in enumerate(sls):
                nxt_l = sls[i + 1][1] if i + 1 < len(sls) else None
                segment_layer(sg, l, first_layer=(l == 0), last_layer=(l == n_layers - 1), nxt_l=nxt_l)
            assert wpos[0] == len(ws.sched), (wpos[0], len(ws.sched))
        except _Stop:
            pass
        for sk in list(R.cnt.keys()):
            if not sk.startswith("E_"):
                R.streams["sp"].append(("w", sk, R.cnt[sk]))

        fin = {}
        for (sk, v) in out_tickets:
            fin[sk] = max(fin.get(sk, 0), v)
        if "S_dbg" in R.cnt:
            fin["S_dbg"] = R.cnt["S_dbg"]
        for sk, v in fin.items():
            R.streams["sp"].append(("w", sk, v))

        semh = {}
        for sk in sorted(R.cnt.keys()):
            semh[sk] = es.enter_context(nc.semaphore(sk))
        with nc.Block() as block:
            @block.tensor
            def _(e):
                _replay(e, R.streams["pe"], semh)

            @block.scalar
            def _(e):
                _replay(e, R.streams["act"], semh)

            @block.vector
            def _(e):
                _replay(e, R.streams["dve"], semh)

            @block.gpsimd
            def _(e):
                _replay(e, R.streams["pool"], semh)

            @block.sync
            def _(e):
                _replay(e, R.streams["sp"], semh)
    stats = {k: len(v) for k, v in R.streams.items()}
    return nc, stats


def _host_constants():
    half = QK_ROPE // 2
    inv = ROPE_THETA ** (-np.arange(half, dtype=np.float32) / np.float32(half))
    pos = np.arange(NKEYMAX, dtype=np.float32)
    ang = pos[None, :] * inv[:, None].astype(np.float32)
    cos = np.cos(ang).astype(np.float32)
    sin = np.sin(ang).astype(np.float32)
    idx = np.arange(128) % half
    rope = np.stack([cos[idx], sin[idx]], axis=0).astype(np.float32)
    ident = np.eye(128, dtype=np.float32)
    return rope, ident


def _feat_major(v, nchunk):
    return np.ascontiguousarray(v.reshape(nchunk, 128).T)


def make_in_maps(inputs):
    f = lambda k: np.ascontiguousarray(np.asarray(inputs[k], dtype=np.float32))
    x_prompt, x_sample = f("x_prompt"), f("x_sample")
    state_conv, cache_ckv, cache_krope = f("state_conv"), f("cache_ckv"), f("cache_krope")
    rope, ident = _host_constants()
    smallp = np.zeros((128, L, NSP), np.float32)
    for l in range(L):
        smallp[:, l, C_GPRE:C_GPRE + 8] = _feat_major(f("norm_attn_pre")[l], 8)
        smallp[:, l, C_GPOST:C_GPOST + 8] = _feat_major(f("norm_attn_post")[l], 8)
        smallp[:, l, C_FPRE:C_FPRE + 8] = _feat_major(f("norm_ffn_pre")[l], 8)
        smallp[:, l, C_FPOST:C_FPOST + 8] = _feat_major(f("norm_ffn_post")[l], 8)
        smallp[:, l, C_GQ:C_GQ + 3] = _feat_major(f("norm_q")[l], 3)
        smallp[:, l, C_GKV:C_GKV + 2] = _feat_major(f("norm_kv")[l], 2)
        cw = f("conv_w")[l]
        for j in range(3):
            smallp[:, l, C_CW + j:C_CW + 24:3] = _feat_major(cw[j], 8)
    shared = dict(smallp=smallp.reshape(128, L * NSP), rope=rope, ident=ident,
                  w_in=f("w_in"), w_uq=f("w_uq"), w_ukv=f("w_ukv"), w_conv_out=f("w_conv_out"),
                  w_attn_out=f("w_attn_out"), w_merge=f("w_merge"), w_gate_up=f("w_gate_up"), w_down=f("w_down"))
    maps = []
    for c in range(NCORES):
        h0 = np.zeros((128, L, KC, 2), np.float32)
        for l in range(L):
            for j in range(2):
                h0[:, l, :, j] = _feat_major(state_conv[l, c, j], 8)
        m = dict(shared)
        m.update(x_p=np.ascontiguousarray(x_prompt[2 * c:2 * c + 2]), x_s=np.ascontiguousarray(x_sample[c]),
                 c_ckv=np.ascontiguousarray(cache_ckv[:, c]), c_kr=np.ascontiguousarray(cache_krope[:, c]),
                 hist0=h0.reshape(128, L * KC * 2))
        maps.append(m)
    return maps


_PROG = None


def kernel(**inputs):
    global _PROG
    if _PROG is None:
        _PROG = build_program()[0]
    in_maps = make_in_maps(inputs)
    res = run_bass_kernel_spmd(_PROG, in_maps, core_ids=list(range(NCORES)))
    rs = res.results
    B = 2 * NCORES
    y_prompt = np.concatenate([r["y_p"] for r in rs], axis=0)
    y_sample = np.stack([r["y_s"] for r in rs], axis=0)
    nconv_p = np.concatenate([r["nconv_p"] for r in rs], axis=1)
    nckv_p = np.concatenate([r["nckv_p"] for r in rs], axis=1)
    nkr_p = np.concatenate([r["nkr_p"] for r in rs], axis=1)
    nconv_s = np.stack([r["nconv_s"] for r in rs], axis=1)
    nckv_s = np.stack([r["nckv_s"] for r in rs], axis=1)
    nkr_s = np.stack([r["nkr_s"] for r in rs], axis=1)
    outs = (y_prompt, y_sample, nconv_p, nckv_p, nkr_p, nconv_s, nckv_s, nkr_s)
    return tuple(np.ascontiguousarray(o, dtype=np.float32) for o in outs)
```

```python
import numpy as np
from contextlib import ExitStack
import concourse.bass as bass
import concourse.mybir as mybir
from concourse.bass_utils import run_bass_kernel_spmd

F32 = mybir.dt.float32
BF16 = mybir.dt.bfloat16
ALU = mybir.AluOpType
AF = mybir.ActivationFunctionType

NCORES = 8
D = 1024
KC = 8
L = 2
SEQ = 2048
SEG = 512
DEC_SEQ = 32
PAST = 2048
NKEYMAX = PAST + DEC_SEQ
H = 16
QK_NOPE = 64
QK_ROPE = 32
QH = QK_NOPE + QK_ROPE
V_HEAD = 64
Q_LORA = 384
KV_LORA = 256
DFF = 2816
FC = DFF // 128
D_IN = 3 * D + Q_LORA + KV_LORA + QK_ROPE + 2 * D
O1, O2, O3 = D, 2 * D, 3 * D
O4 = O3 + Q_LORA
O5 = O4 + KV_LORA
O6 = O5 + QK_ROPE
O7 = O6 + D
EPS = 1e-6
ATTN_SCALE = float(QH ** -0.5)
ROPE_THETA = 10000.0
NSLOT = 9
NSP = 61
C_GPRE, C_GPOST, C_FPRE, C_FPOST, C_GQ, C_GKV, C_CW = 0, 8, 16, 24, 32, 35, 37


class Rec:
    ENG = ("pe", "act", "dve", "pool", "sp")

    def __init__(self):
        self.streams = {e: [] for e in self.ENG}
        self.cnt = {}
        self.waited = {e: {} for e in self.ENG}
        self.res = {}

    def _deps(self, eng, reads, writes):
        deps = {}

        def add(sk, v):
            if eng == "pe" and sk == "E_pe":
                return
            if deps.get(sk, 0) < v:
                deps[sk] = v

        for r in reads:
            st = self.res.get(r)
            if st and st[0] is not None:
                add(*st[0])
        for w in writes:
            st = self.res.get(w)
            if st:
                if st[0] is not None:
                    add(*st[0])
                for sk, v in st[1].items():
                    add(sk, v)
        wd = self.waited[eng]
        for sk, v in deps.items():
            if wd.get(sk, 0) < v:
                self.streams[eng].append(("w", sk, v))
                wd[sk] = v

    def _commit(self, t, reads, writes):
        for r in reads:
            st = self.res.setdefault(r, [None, {}])
            if st[1].get(t[0], 0) < t[1]:
                st[1][t[0]] = t[1]
        for w in writes:
            self.res[w] = [t, {}]

    def op(self, eng, fn, reads=(), writes=()):
        self._deps(eng, reads, writes)
        sk = "E_" + eng
        self.cnt[sk] = self.cnt.get(sk, 0) + 1
        t = (sk, self.cnt[sk])
        self.streams[eng].append(("i", fn, sk, 1))
        self._commit(t, reads, writes)
        return t

    def dma(self, eng, fn, semkey, reads=(), writes=()):
        self._deps(eng, reads, writes)
        self.cnt[semkey] = self.cnt.get(semkey, 0) + 16
        t = (semkey, self.cnt[semkey])
        self.streams[eng].append(("i", fn, semkey, 16))
        self._commit(t, reads, writes)
        return t

    def mmgroup(self, bank_key, mms):
        sk = "E_pe"
        final = (sk, self.cnt.get(sk, 0) + 1)
        n = len(mms)
        for i, (fn, reads) in enumerate(mms):
            self._deps("pe", reads, (bank_key,) if i == 0 else ())
            last = i == n - 1
            self.streams["pe"].append(
                ("i", (lambda e, fn=fn, st=(i == 0), sp=last: fn(e, st, sp)), sk if last else None, 1))
            for r in reads:
                st_ = self.res.setdefault(r, [None, {}])
                if st_[1].get(sk, 0) < final[1]:
                    st_[1][sk] = final[1]
        self.cnt[sk] = final[1]
        self.res[bank_key] = [final, {}]
        return final

    def mm1(self, bank_key, fn, reads, first):
        sk = "E_pe"
        self._deps("pe", reads, (bank_key,) if first else ())
        self.cnt[sk] = self.cnt.get(sk, 0) + 1
        t = (sk, self.cnt[sk])
        self.streams["pe"].append(("i", fn, sk, 1))
        for r in reads:
            st_ = self.res.setdefault(r, [None, {}])
            st_[1][sk] = t[1]
        old = self.res.get(bank_key)
        self.res[bank_key] = [t, {} if (first or not old) else old[1]]
        return t


def _replay(e, stream, semh):
    for it in stream:
        if it[0] == "w":
            e.wait_ge(semh[it[1]], it[2])
        else:
            inst = it[1](e)
            if it[2] is not None:
                inst.then_inc(semh[it[2]], it[3])


class _Stop(Exception):
    pass


def build_program(n_prompt_seq=2, n_quarters=4, with_sample=True, n_layers=L, debug=False, stop=None):
    nc = bass.Bass("TRN2", target_bir_lowering=False)
    R = Rec()

    def din(name, shape):
        return nc.dram_tensor(name, list(shape), F32, kind="ExternalInput").ap()

    def dout(name, shape):
        return nc.dram_tensor(name, list(shape), F32, kind="ExternalOutput").ap()

    x_p = din("x_p", (2, SEQ, D))
    x_s = din("x_s", (DEC_SEQ, D))
    c_ckv = din("c_ckv", (L, PAST, KV_LORA))
    c_kr = din("c_kr", (L, PAST, QK_ROPE))
    hist0 = din("hist0", (128, L * KC * 2))
    smallp_d = din("smallp", (128, L * NSP))
    rope_d = din("rope", (2, 128, NKEYMAX))
    ident_d = din("ident", (128, 128))
    w_in = din("w_in", (L, D, D_IN))
    w_uq = din("w_uq", (L, Q_LORA, H * QH))
    w_ukv = din("w_ukv", (L, KV_LORA, H * 128))
    w_conv_out = din("w_conv_out", (L, D, D))
    w_attn_out = din("w_attn_out", (L, D, D))
    w_merge = din("w_merge", (L, D, D))
    w_gate_up = din("w_gate_up", (L, D, 2 * DFF))
    w_down = din("w_down", (L, DFF, D))

    y_p = dout("y_p", (2, SEQ, D))
    y_s = dout("y_s", (DEC_SEQ, D))
    nconv_p = dout("nconv_p", (L, 2, 2, D))
    nckv_p = dout("nckv_p", (L, 2, SEQ, KV_LORA))
    nkr_p = dout("nkr_p", (L, 2, SEQ, QK_ROPE))
    nconv_s = dout("nconv_s", (L, 2, D))
    nckv_s = dout("nckv_s", (L, DEC_SEQ, KV_LORA))
    nkr_s = dout("nkr_s", (L, DEC_SEQ, QK_ROPE))
    dbg = {}
    if debug:
        for nm, shp in (("d_xn", (128, KC * SEG)), ("d_bc", (128, KC * SEG)), ("d_qn", (128, 3 * SEG)),
                        ("d_attn", (128, KC * SEG)), ("d_z", (128, KC * SEG)), ("d_xres", (128, KC * SEG)),
                        ("d_xres2", (128, KC * SEG)), ("d_qt", (128, SEG)), ("d_kt", (128, SEG))):
            dbg[nm] = dout(nm, shp)

    wv = {
        "w_in": w_in.rearrange("l (kc p) m -> l p kc m", p=128),
        "w_uq": w_uq.rearrange("l (kc p) m -> l p kc m", p=128),
        "w_ukv": w_ukv.rearrange("l (kc p) m -> l p kc m", p=128),
        "w_conv_out": w_conv_out.rearrange("l (kc p) m -> l p kc m", p=128),
        "w_attn_out": w_attn_out.rearrange("l (kc p) m -> l p kc m", p=128),
        "w_merge": w_merge.rearrange("l (kc p) m -> l p kc m", p=128),
        "w_gate_up": w_gate_up.rearrange("l (kc p) m -> l p kc m", p=128),
        "w_down": w_down.rearrange("l (kc p) m -> l p kc m", p=128),
    }

    with ExitStack() as es:
        def sb(name, shape, dt):
            return es.enter_context(nc.sbuf_tensor(name, list(shape), dt))

        xresT = sb("xresT", (128, KC, SEG), F32)
        xnT = sb("xnT", (128, KC, SEG), BF16)
        big = sb("big", (128, 24, SEG), BF16)
        qnT = sb("qnT", (128, 3, SEG), BF16)
        cT = sb("cT", (128, L, 2, NKEYMAX), BF16)
        krT = sb("krT", (128, L, NKEYMAX), BF16)
        scr = sb("scr", (128, KC, SEG), F32)
        wbuf = sb("wbuf", (128, NSLOT, KC, 128), BF16)
        wuq = sb("wuq", (128, 3, H * QH), BF16)
        wuqB = sb("wuqB", (128, 3, H, 32), BF16)
        wuqA = sb("wuqA", (128, 3, H, 32), BF16)
        wukv = sb("wukv", (128, 2, H * 128), BF16)
        wkrB = sb("wkrB", (128, KC, 32), BF16)
        KT = sb("KT", (128, 2, NKEYMAX), BF16)
        QT = sb("QT", (128, 2, SEG), BF16)
        VB = sb("VB", (128, 2, 17, 128), BF16)
        PT = sb("PT", (128, 3, SEG), BF16)
        Rr = sb("Rr", (128, SEG), F32)
        u_t = sb("u_t", (128, SEG + 2), F32)
        cv_t = sb("cv_t", (128, SEG), F32)
        cg_t = sb("cg_t", (128, SEG), F32)
        sa_t = sb("sa_t", (128, SEG), F32)
        sb_t = sb("sb_t", (128, SEG), F32)
        sg_t = sb("sg_t", (128, 2, SEG), F32)
        sq_t = sb("sq_t", (128, 2, SEG), BF16)
        rt_t = sb("rt_t", (128, SEG), F32)
        rstd_t = sb("rstd_t", (128, SEG), F32)
        kro_t = sb("kro_t", (128, SEG), F32)
        t1_t = sb("t1_t", (128, SEG), F32)
        t2_t = sb("t2_t", (128, SEG), F32)
        cos_t = sb("cos_t", (128, SEG), F32)
        sin_t = sb("sin_t", (128, SEG), F32)
        xtok = sb("xtok", (128, 2, D), F32)
        xin = sb("xin", (128, 2, D), F32)
        ost = sb("ost", (128, 2, 288), F32)
        hist = sb("hist", (128, L, KC, 2), F32)
        smallp = sb("smallp_sb", (128, L, NSP), F32)
        ident = sb("ident_sb", (128, 128), F32)
        ones = sb("ones_sb", (128, 128), BF16)
        ps = [es.enter_context(nc.psum_tensor(f"ps{i}", [128, 512], F32)) for i in range(8)]

        bcT = lambda m: big[:, m, :]
        attnT_i, zT_i = 8, 16
        qlat_i, ckvs_i, cnew_i = 0, 3, 5

        rr = {"mm": 0, "aux": 0, "acc": 0}
        MM_BANKS, AUX_BANKS, ACC_BANKS, SS_BANK = (0, 1, 2), (4, 7), (5, 6), 3

        def bank(cls):
            lst = {"mm": MM_BANKS, "aux": AUX_BANKS, "acc": ACC_BANKS}[cls]
            b = lst[rr[cls] % len(lst)]
            rr[cls] += 1
            return b

        def act(out, in_, func, reads, writes, scale=None, bias=None):
            kw = {}
            if scale is not None:
                kw["scale"] = scale
            if bias is not None:
                kw["bias"] = bias
            return R.op("act", lambda e: e.activation(out=out, in_=in_, func=func, **kw), reads, writes)

        def tt(out, in0, in1, op, reads, writes):
            return R.op("dve", lambda e: e.tensor_tensor(out=out, in0=in0, in1=in1, op=op), reads, writes)

        def stt(out, in0, scalar, in1, op0, op1, reads, writes):
            return R.op("dve", lambda e: e.scalar_tensor_tensor(out=out, in0=in0, scalar=scalar, in1=in1,
                                                               op0=op0, op1=op1), reads, writes)

        def dcopy(out, in_, reads, writes):
            return R.op("dve", lambda e: e.tensor_copy(out=out, in_=in_), reads, writes)

        def acopy(out, in_, reads, writes, scale=None):
            if scale is None:
                return R.op("act", lambda e: e.copy(out=out, in_=in_), reads, writes)
            return R.op("act", lambda e: e.mul(out=out, in_=in_, mul=scale), reads, writes)

        def mm(out, lhsT, rhs):
            return lambda e, st, sp: e.matmul(out, lhsT, rhs, start=st, stop=sp)

        def tr(out, in_, idn):
            return lambda e, st, sp: e.transpose(out, in_, idn)

        out_tickets = []

        def chk(name):
            if stop == name:
                raise _Stop()

        class WS:
            def __init__(self):
                self.sched = []
                self.emitted = 0

            def advance(self, upto):
                upto = min(upto, len(self.sched) - 1)
                while self.emitted <= upto:
                    i = self.emitted
                    (l, name, kc0, nkc, c0, ncols) = self.sched[i]
                    slot = i % NSLOT
                    src = wv[name][l, :, kc0:kc0 + nkc, c0:c0 + ncols]
                    dst = wbuf[:, slot, 0:nkc, 0:ncols]
                    R.dma("pool", lambda e, dst=dst, src=src: e.dma_start(out=dst, in_=src), f"W{slot}",
                          reads=(), writes=(("w", slot),))
                    self.emitted += 1

            def use(self, i, desc):
                assert self.sched[i] == desc, (i, self.sched[i], desc)
                assert i < self.emitted, (i, self.emitted)
                return i % NSLOT

        ws = WS()
        wpos = [0]

        def wtile(desc):
            i = wpos[0]
            wpos[0] += 1
            slot = ws.use(i, desc)
            return slot, ("w", slot)

        def wdone():
            ws.advance(wpos[0] - 1 + NSLOT)

        segs = []
        for s in range(n_prompt_seq):
            for q in range(n_quarters):
                segs.append(dict(kind="p", seq=s, q=q, N=SEG, pos0=q * SEG, key0=q * SEG))
        if with_sample:
            segs.append(dict(kind="s", seq=0, q=4, N=DEC_SEQ, pos0=PAST, key0=PAST))

        def sl_sched(l):
            out = []
            for m in range(KC):
                out.append((l, "w_in", 0, KC, O1 + 128 * m, 128))
                out.append((l, "w_in", 0, KC, O2 + 128 * m, 128))
                out.append((l, "w_in", 0, KC, 128 * m, 128))
            for i in range(3):
                out.append((l, "w_in", 0, KC, O3 + 128 * i, 128))
            for i in range(2):
                out.append((l, "w_in", 0, KC, O4 + 128 * i, 128))
            out.append((l, "w_in", 0, KC, O5, 32))
            for m in range(KC):
                out.append((l, "w_conv_out", 0, KC, 128 * m, 128))
                out.append((l, "w_in", 0, KC, O6 + 128 * m, 128))
                out.append((l, "w_attn_out", 0, KC, 128 * m, 128))
                out.append((l, "w_in", 0, KC, O7 + 128 * m, 128))
            for m in range(KC):
                out.append((l, "w_merge", 0, KC, 128 * m, 128))
            for f in range(FC):
                out.append((l, "w_gate_up", 0, KC, 128 * f, 128))
                out.append((l, "w_gate_up", 0, KC, DFF + 128 * f, 128))
            for m in range(KC):
                out.append((l, "w_down", 0, 8, 128 * m, 128))
                out.append((l, "w_down", 8, 8, 128 * m, 128))
                out.append((l, "w_down", 16, 6, 128 * m, 128))
            return out

        for sg in segs:
            for l in range(n_layers):
                ws.sched.extend(sl_sched(l))

        R.dma("sp", lambda e: e.dma_start(out=smallp[:, :, :].rearrange("p l c -> p (l c)"), in_=smallp_d[:, :]),
              "S_small", writes=("smallp",))
        R.dma("sp", lambda e: e.dma_start(out=ident[:, :], in_=ident_d[:, :]), "S_ident", writes=("ident",))
        R.op("dve", lambda e: e.memset(ones[:, :], 1.0), writes=("ones",))
        R.op("dve", lambda e: e.memset(VB[:, 0, :, 64:128], 1.0), writes=(("V", 0),))
        R.op("dve", lambda e: e.memset(VB[:, 1, :, 0:64], 1.0), writes=(("V", 1),))
        R.op("dve", lambda e: e.memset(KT[:, :, :], 0.0), writes=(("KT", 0), ("KT", 1)))
        R.op("dve", lambda e: e.memset(QT[:, :, :], 0.0), writes=(("QT", 0), ("QT", 1)))
        ws.advance(NSLOT - 1)

        def sp_col(l, c0, n=1):
            return smallp[:, l, c0:c0 + n]

        def load_wuq_wukv(l):
            for kc in range(3):
                R.dma("pool", lambda e, kc=kc: e.dma_start(out=wuq[:, kc, :], in_=wv["w_uq"][l, :, kc, :]),
                      "S_wuq", writes=("wuq",) if kc == 0 else ())
            R.res["wuq"] = [("S_wuq", R.cnt["S_wuq"]), {}]
            first = True
            for kc in range(2):
                for hf in range(2):
                    R.dma("pool", lambda e, kc=kc, hf=hf: e.dma_start(
                        out=wukv[:, kc, hf * 1024:(hf + 1) * 1024], in_=wv["w_ukv"][l, :, kc, hf * 1024:(hf + 1) * 1024]),
                        "S_wukv", writes=("wukv",) if first else ())
                    first = False
            R.res["wukv"] = [("S_wukv", R.cnt["S_wukv"]), {}]

        def prep_wuqAB():
            wv4 = wuq[:, :, :].rearrange("p k (h d) -> p k h d", d=QH)
            acopy(wuqB[:, :, :, 0:16], wv4[:, :, :, 80:96], reads=("wuq",), writes=("wuqB",), scale=-1.0)
            acopy(wuqB[:, :, :, 16:32], wv4[:, :, :, 64:80], reads=("wuq",), writes=("wuqB",))
            acopy(wuqA[:, :, :, :], wv4[:, :, :, 64:96], reads=("wuq",), writes=("wuqA",))

        def norm_finish(ssb, nfeat, N):
            act(rt_t[:, :N], ps[ssb][:, :N], AF.Ln, reads=(("ps", ssb),), writes=("rt",), scale=1.0 / nfeat,
                bias=EPS)
            act(rstd_t[:, :N], rt_t[:, :N], AF.Exp, reads=("rt",), writes=("rstd",), scale=-0.5)

        def prenorm(l, gcol, N):
            ssb = SS_BANK
            for kc in range(KC):
                b = kc % 2
                act(sq_t[:, b, :N], xresT[:, kc, :N], AF.Square, reads=(("xres", kc),), writes=(("sq", b),))
                R.mm1(("ps", ssb), (lambda e, b=b, kc=kc: e.matmul(ps[ssb][:, :N], ones[:, :], sq_t[:, b, :N],
                                                                 start=(kc == 0), stop=(kc == KC - 1))),
                      reads=(("sq", b), "ones"), first=(kc == 0))
            norm_finish(ssb, D, N)
            for kc in range(KC):
                stt(xnT[:, kc, :N], xresT[:, kc, :N], sp_col(l, gcol + kc), rstd_t[:, :N], ALU.mult, ALU.mult,
                    reads=(("xres", kc), "rstd", "smallp"), writes=(("xn", kc),))

        def dense_group(l, name, c0, ncols, rhs_fn, rhs_keys, nkc_total=KC, pieces=None):
            b = bank("mm")
            mms = []
            pieces = pieces or [(0, nkc_total)]
            for (k0, nk) in pieces:
                slot, wkey = wtile((l, name, k0, nk, c0, ncols))
                for k in range(nk):
                    kc = k0 + k
                    mms.append((mm(ps[b][0:ncols, :rhs_fn.N], wbuf[:, slot, k, 0:ncols], rhs_fn(kc)),
                                (wkey, rhs_keys(kc))))
            R.mmgroup(("ps", b), mms)
            wdone()
            return b

        class RhsFn:
            def __init__(self, fn, N):
                self.fn = fn
                self.N = N

            def __call__(self, kc):
                return self.fn(kc)

        def issue_input(sg, blk):
            if blk in sg.setdefault("in_issued", set()):
                return
            sg["in_issued"].add(blk)
            N_, pos0_ = sg["N"], sg["pos0"]
            nb = min(128, N_ - blk * 128)
            if sg["kind"] == "p":
                src = x_p[sg["seq"], pos0_ + blk * 128: pos0_ + blk * 128 + nb, :]
            else:
                src = x_s[blk * 128: blk * 128 + nb, :]
            xb = blk % 2
            R.dma("sp", lambda e: e.dma_start(out=xin[0:nb, xb, :], in_=src), f"S_xin{xb}", writes=(("xin", xb),))

        def issue_rope(sg):
            if sg.get("rope_issued"):
                return
            sg["rope_issued"] = True
            N_, pos0_ = sg["N"], sg["pos0"]
            R.dma("sp", lambda e: e.dma_start(out=cos_t[:, :N_], in_=rope_d[0, :, pos0_:pos0_ + N_]), "S_cos",
                  writes=("cos",))
            R.dma("sp", lambda e: e.dma_start(out=sin_t[:, :N_], in_=rope_d[1, :, pos0_:pos0_ + N_]), "S_sin",
                  writes=("sin",))

        def prefetch_next(sg):
            i = segs.index(sg)
            if i + 1 < len(segs):
                nx = segs[i + 1]
                for blk in range(min(2, (nx["N"] + 127) // 128)):
                    issue_input(nx, blk)
                issue_rope(nx)

        def segment_layer(sg, l, first_layer, last_layer, nxt_l=None):
            N = sg["N"]
            kind = sg["kind"]
            key0 = sg["key0"]
            pos0 = sg["pos0"]
            q = sg["q"]
            seq = sg["seq"]
            nblk = (N + 127) // 128
            xn_rhs = RhsFn(lambda kc: xnT[:, kc, :N], N)
            xn_keys = lambda kc: ("xn", kc)

            if first_layer:
                for blk in range(nblk):
                    nb = min(128, N - blk * 128)
                    issue_input(sg, blk)
                    xb = blk % 2
                    for half in range(2):
                        b = bank("mm")
                        for j in range(4):
                            kc = half * 4 + j
                            R.mmgroup(("ps", b) if j == 0 else ("psx", b), [
                                (tr(ps[b][:, j * 128: j * 128 + nb], xin[0:nb, xb, kc * 128:(kc + 1) * 128],
                                    ident[0:nb, 0:nb]), (("xin", xb), "ident"))])
                        R.res[("ps", b)] = [("E_pe", R.cnt["E_pe"]), {}]
                        src_ps = ps[b][:, :].rearrange("p (j t) -> p j t", t=128)[:, :, 0:nb]
                        acopy(xresT[:, half * 4:half * 4 + 4, blk * 128: blk * 128 + nb], src_ps,
                              reads=(("ps", b),), writes=tuple(("xres", half * 4 + j) for j in range(4)))
                issue_rope(sg)

            chk("input")
            if kind == "p" and q == 0:
                R.op("dve", lambda e: e.memset(hist[:, l, :, :], 0.0), writes=(("hist", l),))
            if kind == "s":
                R.dma("sp", lambda e: e.dma_start(
                    out=hist[:, l, :, :].rearrange("p k j -> p (k j)"), in_=hist0[:, l * 16:(l + 1) * 16]),
                    "S_hist", writes=(("hist", l),))
                for blk in range(PAST // 128):
                    ob = blk % 2
                    R.dma("sp", lambda e, blk=blk, ob=ob: e.dma_start(
                        out=ost[:, ob, 0:256], in_=c_ckv[l, blk * 128:(blk + 1) * 128, :]), f"S_ost{ob}",
                        writes=(("ost", ob),))
                    R.dma("sp", lambda e, blk=blk, ob=ob: e.dma_start(
                        out=ost[:, ob, 256:288], in_=c_kr[l, blk * 128:(blk + 1) * 128, :]), f"S_ost{ob}",
                        writes=())
                    R.res[("ost", ob)] = [(f"S_ost{ob}", R.cnt[f"S_ost{ob}"]), {}]
                    b = bank("aux")
                    R.mmgroup(("ps", b), [(tr(ps[b][:, 0:128], ost[:, ob, 0:128], ident[:, :]), (("ost", ob), "ident"))])
                    R.mmgroup(("psx", b), [(tr(ps[b][:, 128:256], ost[:, ob, 128:256], ident[:, :]), (("ost", ob), "ident"))])
                    R.mmgroup(("psx", b), [(tr(ps[b][0:32, 256:384], ost[:, ob, 256:288], ident[:, :]), (("ost", ob), "ident"))])
                    R.res[("ps", b)] = [("E_pe", R.cnt["E_pe"]), {}]
                    jt = blk // 4
                    acopy(cT[:, l, :, blk * 128:(blk + 1) * 128],
                          ps[b][:, 0:256].rearrange("p (k t) -> p k t", t=128),
                          reads=(("ps", b),), writes=(("cT", l, jt),))
                    acopy(krT[64:96, l, blk * 128:(blk + 1) * 128], ps[b][0:32, 256:384],
                          reads=(("ps", b),), writes=(("krT", l, jt),))

            chk("hist")
            if sg.get("first_sl") and l == 0:
                load_wuq_wukv(l)
                prep_wuqAB()

            chk("wuq")
            prenorm(l, C_GPRE, N)

            chk("prenorm")
            for m in range(KC):
                b_cg = dense_group(l, "w_in", O1 + 128 * m, 128, xn_rhs, xn_keys)
                b_xi = dense_group(l, "w_in", O2 + 128 * m, 128, xn_rhs, xn_keys)
                acopy(cg_t[:, :N], ps[b_cg][:, :N], reads=(("ps", b_cg),), writes=("cg",))
                dcopy(u_t[:, 0:2], hist[:, l, m, :], reads=(("hist", l),), writes=("u",))
                tt(u_t[:, 2:2 + N], ps[b_xi][:, :N], cg_t[:, :N], ALU.mult, reads=(("ps", b_xi), "cg", "u"),
                   writes=("u",))
                cws = [sp_col(l, C_CW + 3 * m + j) for j in range(3)]
                cw = lambda j, cws=cws: cws[j]
                R.op("dve", lambda e, c2=cws[2]: e.tensor_scalar(out=cv_t[:, :N], in0=u_t[:, 2:2 + N], scalar1=c2,
                                                                 scalar2=None, op0=ALU.mult),
                     reads=("u", "smallp"), writes=("cv",))
                stt(cv_t[:, :N], u_t[:, 1:1 + N], cw(1), cv_t[:, :N], ALU.mult, ALU.add, reads=("u", "cv", "smallp"),
                    writes=("cv",))
                stt(cv_t[:, :N], u_t[:, 0:N], cw(0), cv_t[:, :N], ALU.mult, ALU.add, reads=("u", "cv", "smallp"),
                    writes=("cv",))
                dcopy(hist[:, l, m, :], u_t[:, N:N + 2], reads=("u",), writes=(("hist", l),))
                b_bg = dense_group(l, "w_in", 128 * m, 128, xn_rhs, xn_keys)
                tt(big[:, m, :N], ps[b_bg][:, :N], cv_t[:, :N], ALU.mult, reads=(("ps", b_bg), "cv"),
                   writes=(("big", m),))
            if (kind == "p" and q == n_quarters - 1) or kind == "s":
                if kind == "p":
                    dst = nconv_p[l, seq, :, :].rearrange("j (k p) -> p k j", p=128)
                else:
                    dst = nconv_s[l, :, :].rearrange("j (k p) -> p k j", p=128)
                for kc in range(KC):
                    out_tickets.append(R.dma("sp", lambda e, dst=dst, kc=kc: e.dma_start(
                        out=dst[:, kc, :], in_=hist[:, l, kc, :], allow_slow_non_contiguous=True), f"S_histout{l}",
                        reads=(("hist", l),)))

            chk("conv")
            for i in range(3):
                bq = dense_group(l, "w_in", O3 + 128 * i, 128, xn_rhs, xn_keys)
                acopy(scr[:, qlat_i + i, :N], ps[bq][:, :N], reads=(("ps", bq),), writes=(("scr", qlat_i + i),))
            for i in range(2):
                bq = dense_group(l, "w_in", O4 + 128 * i, 128, xn_rhs, xn_keys)
                acopy(scr[:, ckvs_i + i, :N], ps[bq][:, :N], reads=(("ps", bq),), writes=(("scr", ckvs_i + i),))
            jt_new = q
            slot, wkey = wtile((l, "w_in", 0, KC, O5, 32))
            acopy(wkrB[:, :, 0:16], wbuf[:, slot, :, 16:32], reads=(wkey,), writes=("wkrB",), scale=-1.0)
            acopy(wkrB[:, :, 16:32], wbuf[:, slot, :, 0:16], reads=(wkey,), writes=("wkrB",))
            bA = bank("aux")
            R.mmgroup(("ps", bA), [(mm(ps[bA][0:32, :N], wbuf[:, slot, kc, 0:32], xnT[:, kc, :N]),
                                    (wkey, ("xn", kc))) for kc in range(KC)])
            wdone()
            bB = bank("aux")
            R.mmgroup(("ps", bB), [(mm(ps[bB][0:32, :N], wkrB[:, kc, :], xnT[:, kc, :N]), ("wkrB", ("xn", kc)))
                                   for kc in range(KC)])
            tt(t1_t[0:32, :N], ps[bA][0:32, :N], cos_t[0:32, :N], ALU.mult, reads=(("ps", bA), "cos"), writes=("t1",))
            tt(t2_t[0:32, :N], ps[bB][0:32, :N], sin_t[0:32, :N], ALU.mult, reads=(("ps", bB), "sin"), writes=("t2",))
            tt(kro_t[0:32, :N], t1_t[0:32, :N], t2_t[0:32, :N], ALU.add, reads=("t1", "t2"), writes=("kro",))
            acopy(krT[64:96, l, key0:key0 + N], kro_t[0:32, :N], reads=("kro",), writes=(("krT", l, jt_new),))
            for i in range(3):
                b = i % 2
                act(sq_t[:, b, :N], scr[:, qlat_i + i, :N], AF.Square, reads=(("scr", qlat_i + i),),
                    writes=(("sq", b),))
                R.mm1(("ps", SS_BANK), (lambda e, b=b, i=i: e.matmul(ps[SS_BANK][:, :N], ones[:, :], sq_t[:, b, :N],
                                                                    start=(i == 0), stop=(i == 2))),
                      reads=(("sq", b), "ones"), first=(i == 0))
            norm_finish(SS_BANK, Q_LORA, N)
            for i in range(3):
                stt(qnT[:, i, :N], scr[:, qlat_i + i, :N], sp_col(l, C_GQ + i), rstd_t[:, :N], ALU.mult, ALU.mult,
                    reads=(("scr", qlat_i + i), "rstd", "smallp"), writes=(("qn", i),))
            for i in range(2):
                b = i % 2
                act(sq_t[:, b, :N], scr[:, ckvs_i + i, :N], AF.Square, reads=(("scr", ckvs_i + i),),
                    writes=(("sq", b),))
                R.mm1(("ps", SS_BANK), (lambda e, b=b, i=i: e.matmul(ps[SS_BANK][:, :N], ones[:, :], sq_t[:, b, :N],
                                                                    start=(i == 0), stop=(i == 1))),
                      reads=(("sq", b), "ones"), first=(i == 0))
            norm_finish(SS_BANK, KV_LORA, N)
            for i in range(2):
                stt(scr[:, cnew_i + i, :N], scr[:, ckvs_i + i, :N], sp_col(l, C_GKV + i), rstd_t[:, :N], ALU.mult,
                    ALU.mult, reads=(("scr", ckvs_i + i), "rstd", "smallp"), writes=(("scr", cnew_i + i),))
                acopy(cT[:, l, i, key0:key0 + N], scr[:, cnew_i + i, :N], reads=(("scr", cnew_i + i),),
                      writes=(("cT", l, jt_new),))
            chk("lowrank")
            for blk in range(nblk):
                nb = min(128, N - blk * 128)
                ob = blk % 2
                b = bank("aux")
                R.mmgroup(("ps", b), [(tr(ps[b][0:nb, 0:128], scr[:, cnew_i, blk * 128:blk * 128 + nb], ident[:, :]),
                                       (("scr", cnew_i), "ident"))])
                R.mmgroup(("psx", b), [(tr(ps[b][0:nb, 128:256], scr[:, cnew_i + 1, blk * 128:blk * 128 + nb],
                                          ident[:, :]), (("scr", cnew_i + 1), "ident"))])
                R.mmgroup(("psx", b), [(tr(ps[b][0:nb, 256:288], kro_t[0:32, blk * 128:blk * 128 + nb],
                                          ident[0:32, 0:32]), ("kro", "ident"))])
                R.res[("ps", b)] = [("E_pe", R.cnt["E_pe"]), {}]
                dcopy(ost[0:nb, ob, :], ps[b][0:nb, 0:288], reads=(("ps", b),), writes=(("ost", ob),))
                if kind == "p":
                    r0 = pos0 + blk * 128
                    d1 = nckv_p[l, seq, r0:r0 + nb, :]
                    d2 = nkr_p[l, seq, r0:r0 + nb, :]
                else:
                    d1 = nckv_s[l, 0:nb, :]
                    d2 = nkr_s[l, 0:nb, :]
                out_tickets.append(R.dma("sp", lambda e, d1=d1, ob=ob, nb=nb: e.dma_start(
                    out=d1, in_=ost[0:nb, ob, 0:256]), f"S_ost{ob}", reads=(("ost", ob),)))
                out_tickets.append(R.dma("sp", lambda e, d2=d2, ob=ob, nb=nb: e.dma_start(
                    out=d2, in_=ost[0:nb, ob, 256:288]), f"S_ost{ob}", reads=(("ost", ob),)))

            chk("ctxout")
            nkeys = key0 + N
            ktiles = [(j * 512, min(512, nkeys - j * 512)) for j in range((nkeys + 511) // 512)]
            kblocks = [(j * 128, min(128, nkeys - j * 128)) for j in range((nkeys + 127) // 128)]
            for hb in range(2):
                for jt, (k0, nk) in enumerate(ktiles):
                    dcopy(KT[64:96, hb, k0:k0 + nk], krT[64:96, l, k0:k0 + nk], reads=(("krT", l, jt),),
                          writes=(("KT", hb),))
            b4 = SS_BANK

            def prep_pieces(h):
                hb = h % 2
                eo = h % 2
                pieces = []
                ev = dcopy
                ev_kt = acopy if (kind == "s" or len(kblocks) <= 8) else dcopy
                for jt, (k0, nk) in enumerate(ktiles):
                    def p_kt(jt=jt, k0=k0, nk=nk):
                        b = bank("aux")
                        R.mmgroup(("ps", b), [(mm(ps[b][0:64, :nk], wukv[:, kc, h * 128:h * 128 + 64],
                                                  cT[:, l, kc, k0:k0 + nk]), ("wukv", ("cT", l, jt)))
                                              for kc in range(2)])
                        ev_kt(KT[0:64, hb, k0:k0 + nk], ps[b][0:64, :nk], reads=(("ps", b),), writes=(("KT", hb),))
                    pieces.append(p_kt)
                for g0 in range(0, len(kblocks), 8):
                    def p_v(g0=g0):
                        grp = kblocks[g0:g0 + 8]
                        b = bank("aux")
                        for j, (k0, nk) in enumerate(grp):
                            R.mmgroup(("ps", b) if j == 0 else ("psx", b), [
                                (mm(ps[b][0:nk, j * 64:(j + 1) * 64], cT[:, l, kc, k0:k0 + nk],
                                    wukv[:, kc, h * 128 + 64:h * 128 + 128]), ("wukv", ("cT", l, k0 // 512)))
                                for kc in range(2)])
                        R.res[("ps", b)] = [("E_pe", R.cnt["E_pe"]), {}]
                        vc0 = 0 if eo == 0 else 64
                        full = [g for g in grp if g[1] == 128]
                        if full:
                            nf = len(full)
                            ev(VB[:, eo, g0:g0 + nf, vc0:vc0 + 64],
                                  ps[b][:, 0:nf * 64].rearrange("p (j v) -> p j v", v=64),
                                  reads=(("ps", b),), writes=(("V", eo),))
                        if len(full) < len(grp):
                            j = len(full)
                            nk = grp[j][1]
                            ev(VB[0:nk, eo, g0 + j, vc0:vc0 + 64], ps[b][0:nk, j * 64:(j + 1) * 64],
                                  reads=(("ps", b),), writes=(("V", eo),))
                    pieces.append(p_v)

                def p_rot4():
                    g = h // 4
                    bA4 = bank("aux")
                    R.mmgroup(("ps", bA4), [(mm(ps[bA4][:, :N], wuqA[:, kc, 4 * g:4 * g + 4, :], qnT[:, kc, :N]),
                                             ("wuqA", ("qn", kc))) for kc in range(3)])
                    R.mmgroup(("ps", b4), [(mm(ps[b4][:, :N], wuqB[:, kc, 4 * g:4 * g + 4, :], qnT[:, kc, :N]),
                                            ("wuqB", ("qn", kc))) for kc in range(3)])
                    tt(t1_t[:, :N], ps[bA4][:, :N], cos_t[:, :N], ALU.mult, reads=(("ps", bA4), "cos"), writes=("t1",))
                    tt(t2_t[:, :N], ps[b4][:, :N], sin_t[:, :N], ALU.mult, reads=(("ps", b4), "sin"), writes=("t2",))
                    tt(kro_t[:, :N], t1_t[:, :N], t2_t[:, :N], ALU.add, reads=("t1", "t2"), writes=("kro",))

                def p_q():
                    bA = bank("aux")
                    R.mmgroup(("ps", bA), [(mm(ps[bA][0:64, :N], wuq[:, kc, h * QH:h * QH + 64], qnT[:, kc, :N]),
                                            ("wuq", ("qn", kc))) for kc in range(3)])
                    ev(QT[0:64, hb, :N], ps[bA][0:64, :N], reads=(("ps", bA),), writes=(("QT", hb),))
                    j4 = h % 4
                    R.op("pool", lambda e: e.tensor_copy(out=QT[64:96, hb, :N], in_=kro_t[32 * j4:32 * j4 + 32, :N]),
                         reads=("kro",), writes=(("QT", hb),))
                    if debug and h == 0 and l == 0 and sg is segs[0]:
                        R.dma("pool", lambda e: e.dma_start(out=dbg["d_qt"][:, :], in_=QT[:, 0, :]), "S_dbg",
                              reads=(("QT", 0),))
                        R.dma("pool", lambda e: e.dma_start(out=dbg["d_kt"][:, :], in_=KT[:, 0, 0:SEG]), "S_dbg",
                              reads=(("KT", 0),))
                if h % 4 == 0:
                    pieces.insert(0, p_rot4)
                    pieces.insert(1, p_q)
                else:
                    pieces.insert(0, p_q)
                return pieces

            def head_loop(h, pieces, deferred):
                hb = h % 2
                eo = h % 2
                accb = bank("acc")
                nkb = len(kblocks)
                info = []
                for kb, (k0, nk) in enumerate(kblocks):
                    if kind == "p" and k0 >= key0:
                        bd = (k0 - key0) // 128
                        info.append((k0, nk, bd, 128 * bd))
                    else:
                        info.append((k0, nk, None, 0))
                sbanks = {}

                def issue_s(kb):
                    k0, nk, bd, qlo = info[kb]
                    sbk = bank("mm")
                    R.mmgroup(("ps", sbk), [(mm(ps[sbk][0:nk, qlo:N], KT[0:QH, hb, k0:k0 + nk], QT[0:QH, hb, qlo:N]),
                                             (("KT", hb), ("QT", hb)))])
                    sbanks[kb] = sbk

                for kb in range(min(2, nkb)):
                    issue_s(kb)
                for kb in range(nkb):
                    k0, nk, bd, qlo = info[kb]
                    sbk = sbanks[kb]
                    pb = kb % 3
                    act(PT[0:nk, pb, qlo:N], ps[sbk][0:nk, qlo:N], AF.Exp, reads=(("ps", sbk),),
                        writes=(("PT", pb),), scale=ATTN_SCALE)
                    def pv(c0, c1, r1, first, last, kb=kb, pb=pb):
                        o_ap, l_ap, r_ap = ps[accb][:, c0:c1], VB[0:r1, eo, kb, :], PT[0:r1, pb, c0:c1]
                        R.mm1(("ps", accb), (lambda e: e.matmul(o_ap, l_ap, r_ap, start=first, stop=last)),
                              reads=(("V", eo), ("PT", pb)), first=first)
                    if bd is None:
                        pv(qlo, N, nk, kb == 0, kb == nkb - 1)
                    else:
                        pv(qlo + 64, N, nk, kb == 0, False)
                        pv(qlo, qlo + 64, 64, False, kb == nkb - 1)
                    if kb + 2 < nkb:
                        issue_s(kb + 2)
                    if kb == 1 and deferred is not None:
                        deferred()
                        deferred = None
                    if pieces:
                        pieces.pop(0)()
                while pieces:
                    pieces.pop(0)()
                if deferred is not None:
                    deferred()
                dlo, slo = (0, 64) if eo == 0 else (64, 0)

                def norm():
                    act(Rr[dlo:dlo + 64, :N], ps[accb][slo:slo + 64, :N], AF.Ln, reads=(("ps", accb),), writes=("R",))
                    act(Rr[dlo:dlo + 64, :N], Rr[dlo:dlo + 64, :N], AF.Exp, reads=("R",), writes=("R",), scale=-1.0)
                    tt(big[dlo:dlo + 64, attnT_i + h // 2, :N], ps[accb][dlo:dlo + 64, :N], Rr[dlo:dlo + 64, :N],
                       ALU.mult, reads=(("ps", accb), "R"), writes=(("big", attnT_i + h // 2),))
                return norm

            for p in prep_pieces(0):
                p()
            pending = None
            for h in range(H):
                nxt = prep_pieces(h + 1) if h + 1 < H else []
                pending = head_loop(h, nxt, pending)
            pending()

            chk("attn")
            if nxt_l is not None:
                load_wuq_wukv(nxt_l)
            bc_rhs = RhsFn(lambda kc: big[:, kc, :N], N)
            bc_keys = lambda kc: ("big", kc)
            at_rhs = RhsFn(lambda kc: big[:, attnT_i + kc, :N], N)
            at_keys = lambda kc: ("big", attnT_i + kc)
            for m in range(KC):
                b_ya = dense_group(l, "w_conv_out", 128 * m, 128, bc_rhs, bc_keys)
                b_ga = dense_group(l, "w_in", O6 + 128 * m, 128, xn_rhs, xn_keys)
                act(sa_t[:, :N], ps[b_ga][:, :N], AF.Sigmoid, reads=(("ps", b_ga),), writes=("sa",))
                tt(sa_t[:, :N], ps[b_ya][:, :N], sa_t[:, :N], ALU.mult, reads=(("ps", b_ya), "sa"), writes=("sa",))
                b_yb = dense_group(l, "w_attn_out", 128 * m, 128, at_rhs, at_keys)
                b_gb = dense_group(l, "w_in", O7 + 128 * m, 128, xn_rhs, xn_keys)
                act(sb_t[:, :N], ps[b_gb][:, :N], AF.Sigmoid, reads=(("ps", b_gb),), writes=("sb",))
                tt(sb_t[:, :N], ps[b_yb][:, :N], sb_t[:, :N], ALU.mult, reads=(("ps", b_yb), "sb"), writes=("sb",))
                tt(big[:, zT_i + m, :N], sa_t[:, :N], sb_t[:, :N], ALU.add, reads=("sa", "sb"),
                   writes=(("big", zT_i + m),))

            chk("z")
            def postnorm_residual(groups_fn, gcol):
                def ss_mm(m):
                    b = m % 2
                    R.mm1(("ps", SS_BANK), (lambda e, b=b, m=m: e.matmul(ps[SS_BANK][:, :N], ones[:, :],
                                                                        sq_t[:, b, :N], start=(m == 0),
                                                                        stop=(m == KC - 1))),
                          reads=(("sq", b), "ones"), first=(m == 0))
                for m in range(KC):
                    bm = groups_fn(m)
                    if m > 0:
                        ss_mm(m - 1)
                    b = m % 2
                    dcopy(scr[:, m, :N], ps[bm][:, :N], reads=(("ps", bm),), writes=(("scr", m),))
                    act(sq_t[:, b, :N], scr[:, m, :N], AF.Square, reads=(("scr", m),), writes=(("sq", b),))
                ss_mm(KC - 1)
                chk("pn_a")
                norm_finish(SS_BANK, D, N)
                chk("pn_b")
                for m in range(KC):
                    stt(scr[:, m, :N], scr[:, m, :N], sp_col(l, gcol + m), rstd_t[:, :N], ALU.mult, ALU.mult,
                        reads=(("scr", m), "rstd", "smallp"), writes=(("scr", m),))
                    tt(xresT[:, m, :N], xresT[:, m, :N], scr[:, m, :N], ALU.add, reads=(("xres", m), ("scr", m)),
                       writes=(("xres", m),))

            z_rhs = RhsFn(lambda kc: big[:, zT_i + kc, :N], N)
            z_keys = lambda kc: ("big", zT_i + kc)
            postnorm_residual(lambda m: dense_group(l, "w_merge", 128 * m, 128, z_rhs, z_keys), C_GPOST)
            if debug and l == 0 and sg is segs[0]:
                def dump(nm, t_, c0, n_, keyf, eng="pool"):
                    for k in range(n_):
                        R.dma(eng, lambda e, k=k: e.dma_start(out=dbg[nm][:, k * SEG:(k + 1) * SEG], in_=t_[:, c0 + k, :]),
                              "S_dbg", reads=(keyf(c0 + k),))
                dump("d_xn", xnT, 0, KC, lambda k: ("xn", k))
                dump("d_bc", big, 0, 8, lambda k: ("big", k))
                dump("d_qn", qnT, 0, 3, lambda k: ("qn", k))
                dump("d_attn", big, 8, 8, lambda k: ("big", k))
                dump("d_z", big, 16, 8, lambda k: ("big", k))
                dump("d_xres", xresT, 0, KC, lambda k: ("xres", k), "sp")
            chk("merge")
            if last_layer:
                prefetch_next(sg)
            prenorm(l, C_FPRE, N)
            for f in range(FC):
                b_g = dense_group(l, "w_gate_up", 128 * f, 128, xn_rhs, xn_keys)
                b_u = dense_group(l, "w_gate_up", DFF + 128 * f, 128, xn_rhs, xn_keys)
                sgb = f % 2
                act(sg_t[:, sgb, :N], ps[b_g][:, :N], AF.Silu, reads=(("ps", b_g),), writes=(("sg", sgb),))
                tt(big[:, f, :N], ps[b_u][:, :N], sg_t[:, sgb, :N], ALU.mult, reads=(("ps", b_u), ("sg", sgb)),
                   writes=(("big", f),))
            chk("ffn")
            if nxt_l is not None:
                prep_wuqAB()
            h_rhs = RhsFn(lambda kc: big[:, kc, :N], N)
            h_keys = lambda kc: ("big", kc)
            postnorm_residual(lambda m: dense_group(l, "w_down", 128 * m, 128, h_rhs, h_keys,
                                                    pieces=[(0, 8), (8, 8), (16, 6)]), C_FPOST)

            chk("down")
            if debug and l == 0 and sg is segs[0]:
                for k in range(KC):
                    R.dma("sp", lambda e, k=k: e.dma_start(out=dbg["d_xres2"][:, k * SEG:(k + 1) * SEG], in_=xresT[:, k, :]),
                          "S_dbg", reads=(("xres", k),))
            if last_layer:
                for blk in range(nblk):
                    nb = min(128, N - blk * 128)
                    for half in range(2):
                        b = bank("mm")
                        for j in range(4):
                            kc = half * 4 + j
                            R.mmgroup(("ps", b) if j == 0 else ("psx", b), [
                                (tr(ps[b][0:nb, j * 128:(j + 1) * 128], xresT[:, kc, blk * 128:blk * 128 + nb],
                                    ident[:, :]), (("xres", kc), "ident"))])
                        R.res[("ps", b)] = [("E_pe", R.cnt["E_pe"]), {}]
                        acopy(xtok[0:nb, blk % 2, half * 512:(half + 1) * 512], ps[b][0:nb, :], reads=(("ps", b),),
                              writes=(("xtok", blk % 2),))
                    if kind == "p":
                        r0 = pos0 + blk * 128
                        dst = y_p[seq, r0:r0 + nb, :]
                    else:
                        dst = y_s[blk * 128:blk * 128 + nb, :]
                    out_tickets.append(R.dma("sp", lambda e, dst=dst, nb=nb, ob=blk % 2: e.dma_start(
                        out=dst, in_=xtok[0:nb, ob, :]), f"S_xtok{blk % 2}", reads=(("xtok", blk % 2),)))

        try:
            sls = [(sg, l) for sg in segs for l in range(n_layers)]
            segs[0]["first_sl"] = True
            for i, (sg, l) in enumerate(sls):
                nxt_l = sls[i + 1][1] if i + 1 < len(sls) else None
                segment_layer(sg, l, first_layer=(l == 0), last_layer=(l == n_layers - 1), nxt_l=nxt_l)
            assert wpos[0] == len(ws.sched), (wpos[0], len(ws.sched))
        except _Stop:
            pass
        for sk in list(R.cnt.keys()):
            if not sk.startswith("E_"):
                R.streams["sp"].append(("w", sk, R.cnt[sk]))

        fin = {}
        for (sk, v) in out_tickets:
            fin[sk] = max(fin.get(sk, 0), v)
        if "S_dbg" in R.cnt:
            fin["S_dbg"] = R.cnt["S_dbg"]
        for sk, v in fin.items():
            R.streams["sp"].append(("w", sk, v))

        semh = {}
        for sk in sorted(R.cnt.keys()):
            semh[sk] = es.enter_context(nc.semaphore(sk))
        with nc.Block() as block:
            @block.tensor
            def _(e):
                _replay(e, R.streams["pe"], semh)

            @block.scalar
            def _(e):
                _replay(e, R.streams["act"], semh)

            @block.vector
            def _(e):
                _replay(e, R.streams["dve"], semh)

            @block.gpsimd
            def _(e):
                _replay(e, R.streams["pool"], semh)

            @block.sync
            def _(e):
                _replay(e, R.streams["sp"], semh)
    stats = {k: len(v) for k, v in R.streams.items()}
    return nc, stats


def _host_constants():
    half = QK_ROPE // 2
    inv = ROPE_THETA ** (-np.arange(half, dtype=np.float32) / np.float32(half))
    pos = np.arange(NKEYMAX, dtype=np.float32)
    ang = pos[None, :] * inv[:, None].astype(np.float32)
    cos = np.cos(ang).astype(np.float32)
    sin = np.sin(ang).astype(np.float32)
    idx = np.arange(128) % half
    rope = np.stack([cos[idx], sin[idx]], axis=0).astype(np.float32)
    ident = np.eye(128, dtype=np.float32)
    return rope, ident


def _feat_major(v, nchunk):
    return np.ascontiguousarray(v.reshape(nchunk, 128).T)


def make_in_maps(inputs):
    f = lambda k: np.ascontiguousarray(np.asarray(inputs[k], dtype=np.float32))
    x_prompt, x_sample = f("x_prompt"), f("x_sample")
    state_conv, cache_ckv, cache_krope = f("state_conv"), f("cache_ckv"), f("cache_krope")
    rope, ident = _host_constants()
    smallp = np.zeros((128, L, NSP), np.float32)
    for l in range(L):
        smallp[:, l, C_GPRE:C_GPRE + 8] = _feat_major(f("norm_attn_pre")[l], 8)
        smallp[:, l, C_GPOST:C_GPOST + 8] = _feat_major(f("norm_attn_post")[l], 8)
        smallp[:, l, C_FPRE:C_FPRE + 8] = _feat_major(f("norm_ffn_pre")[l], 8)
        smallp[:, l, C_FPOST:C_FPOST + 8] = _feat_major(f("norm_ffn_post")[l], 8)
        smallp[:, l, C_GQ:C_GQ + 3] = _feat_major(f("norm_q")[l], 3)
        smallp[:, l, C_GKV:C_GKV + 2] = _feat_major(f("norm_kv")[l], 2)
        cw = f("conv_w")[l]
        for j in range(3):
            smallp[:, l, C_CW + j:C_CW + 24:3] = _feat_major(cw[j], 8)
    shared = dict(smallp=smallp.reshape(128, L * NSP), rope=rope, ident=ident,
                  w_in=f("w_in"), w_uq=f("w_uq"), w_ukv=f("w_ukv"), w_conv_out=f("w_conv_out"),
                  w_attn_out=f("w_attn_out"), w_merge=f("w_merge"), w_gate_up=f("w_gate_up"), w_down=f("w_down"))
    maps = []
    for c in range(NCORES):
        h0 = np.zeros((128, L, KC, 2), np.float32)
        for l in range(L):
            for j in range(2):
                h0[:, l, :, j] = _feat_major(state_conv[l, c, j], 8)
        m = dict(shared)
        m.update(x_p=np.ascontiguousarray(x_prompt[2 * c:2 * c + 2]), x_s=np.ascontiguousarray(x_sample[c]),
                 c_ckv=np.ascontiguousarray(cache_ckv[:, c]), c_kr=np.ascontiguousarray(cache_krope[:, c]),
                 hist0=h0.reshape(128, L * KC * 2))
        maps.append(m)
    return maps


_PROG = None


def kernel(**inputs):
    global _PROG
    if _PROG is None:
        _PROG = build_program()[0]
    in_maps = make_in_maps(inputs)
    res = run_bass_kernel_spmd(_PROG, in_maps, core_ids=list(range(NCORES)))
    rs = res.results
    B = 2 * NCORES
    y_prompt = np.concatenate([r["y_p"] for r in rs], axis=0)
    y_sample = np.stack([r["y_s"] for r in rs], axis=0)
    nconv_p = np.concatenate([r["nconv_p"] for r in rs], axis=1)
    nckv_p = np.concatenate([r["nckv_p"] for r in rs], axis=1)
    nkr_p = np.concatenate([r["nkr_p"] for r in rs], axis=1)
    nconv_s = np.stack([r["nconv_s"] for r in rs], axis=1)
    nckv_s = np.stack([r["nckv_s"] for r in rs], axis=1)
    nkr_s = np.stack([r["nkr_s"] for r in rs], axis=1)
    outs = (y_prompt, y_sample, nconv_p, nckv_p, nkr_p, nconv_s, nckv_s, nkr_s)
    return tuple(np.ascontiguousarray(o, dtype=np.float32) for o in outs)
```

```python
import numpy as np
from contextlib import ExitStack
import concourse.bass as bass
import concourse.mybir as mybir
from concourse.bass_utils import run_bass_kernel_spmd

F32 = mybir.dt.float32
BF16 = mybir.dt.bfloat16
ALU = mybir.AluOpType
AF = mybir.ActivationFunctionType

NCORES = 8
D = 1024
KC = 8
L = 2
SEQ = 2048
SEG = 512
DEC_SEQ = 32
PAST = 2048
NKEYMAX = PAST + DEC_SEQ
H = 16
QK_NOPE = 64
QK_ROPE = 32
QH = QK_NOPE + QK_ROPE
V_HEAD = 64
Q_LORA = 384
KV_LORA = 256
DFF = 2816
FC = DFF // 128
D_IN = 3 * D + Q_LORA + KV_LORA + QK_ROPE + 2 * D
O1, O2, O3 = D, 2 * D, 3 * D
O4 = O3 + Q_LORA
O5 = O4 + KV_LORA
O6 = O5 + QK_ROPE
O7 = O6 + D
EPS = 1e-6
ATTN_SCALE = float(QH ** -0.5)
ROPE_THETA = 10000.0
NSLOT = 9
NSP = 61
C_GPRE, C_GPOST, C_FPRE, C_FPOST, C_GQ, C_GKV, C_CW = 0, 8, 16, 24, 32, 35, 37


class Rec:
    ENG = ("pe", "act", "dve", "pool", "sp")

    def __init__(self):
        self.streams = {e: [] for e in self.ENG}
        self.cnt = {}
        self.waited = {e: {} for e in self.ENG}
        self.res = {}

    def _deps(self, eng, reads, writes):
        deps = {}

        def add(sk, v):
            if eng == "pe" and sk == "E_pe":
                return
            if deps.get(sk, 0) < v:
                deps[sk] = v

        for r in reads:
            st = self.res.get(r)
            if st and st[0] is not None:
                add(*st[0])
        for w in writes:
            st = self.res.get(w)
            if st:
                if st[0] is not None:
                    add(*st[0])
                for sk, v in st[1].items():
                    add(sk, v)
        wd = self.waited[eng]
        for sk, v in deps.items():
            if wd.get(sk, 0) < v:
                self.streams[eng].append(("w", sk, v))
                wd[sk] = v

    def _commit(self, t, reads, writes):
        for r in reads:
            st = self.res.setdefault(r, [None, {}])
            if st[1].get(t[0], 0) < t[1]:
                st[1][t[0]] = t[1]
        for w in writes:
            self.res[w] = [t, {}]

    def op(self, eng, fn, reads=(), writes=()):
        self._deps(eng, reads, writes)
        sk = "E_" + eng
        self.cnt[sk] = self.cnt.get(sk, 0) + 1
        t = (sk, self.cnt[sk])
        self.streams[eng].append(("i", fn, sk, 1))
        self._commit(t, reads, writes)
        return t

    def dma(self, eng, fn, semkey, reads=(), writes=()):
        self._deps(eng, reads, writes)
        self.cnt[semkey] = self.cnt.get(semkey, 0) + 16
        t = (semkey, self.cnt[semkey])
        self.streams[eng].append(("i", fn, semkey, 16))
        self._commit(t, reads, writes)
        return t

    def mmgroup(self, bank_key, mms):
        sk = "E_pe"
        final = (sk, self.cnt.get(sk, 0) + 1)
        n = len(mms)
        for i, (fn, reads) in enumerate(mms):
            self._deps("pe", reads, (bank_key,) if i == 0 else ())
            last = i == n - 1
            self.streams["pe"].append(
                ("i", (lambda e, fn=fn, st=(i == 0), sp=last: fn(e, st, sp)), sk if last else None, 1))
            for r in reads:
                st_ = self.res.setdefault(r, [None, {}])
                if st_[1].get(sk, 0) < final[1]:
                    st_[1][sk] = final[1]
        self.cnt[sk] = final[1]
        self.res[bank_key] = [final, {}]
        return final

    def mm1(self, bank_key, fn, reads, first):
        sk = "E_pe"
        self._deps("pe", reads, (bank_key,) if first else ())
        self.cnt[sk] = self.cnt.get(sk, 0) + 1
        t = (sk, self.cnt[sk])
        self.streams["pe"].append(("i", fn, sk, 1))
        for r in reads:
            st_ = self.res.setdefault(r, [None, {}])
            st_[1][sk] = t[1]
        old = self.res.get(bank_key)
        self.res[bank_key] = [t, {} if (first or not old) else old[1]]
        return t


def _replay(e, stream, semh):
    for it in stream:
        if it[0] == "w":
            e.wait_ge(semh[it[1]], it[2])
        else:
            inst = it[1](e)
            if it[2] is not None:
                inst.then_inc(semh[it[2]], it[3])


class _Stop(Exception):
    pass


def build_program(n_prompt_seq=2, n_quarters=4, with_sample=True, n_layers=L, debug=False, stop=None):
    nc = bass.Bass("TRN2", target_bir_lowering=False)
    R = Rec()

    def din(name, shape):
        return nc.dram_tensor(name, list(shape), F32, kind="ExternalInput").ap()

    def dout(name, shape):
        return nc.dram_tensor(name, list(shape), F32, kind="ExternalOutput").ap()

    x_p = din("x_p", (2, SEQ, D))
    x_s = din("x_s", (DEC_SEQ, D))
    c_ckv = din("c_ckv", (L, PAST, KV_LORA))
    c_kr = din("c_kr", (L, PAST, QK_ROPE))
    hist0 = din("hist0", (128, L * KC * 2))
    smallp_d = din("smallp", (128, L * NSP))
    rope_d = din("rope", (2, 128, NKEYMAX))
    ident_d = din("ident", (128, 128))
    w_in = din("w_in", (L, D, D_IN))
    w_uq = din("w_uq", (L, Q_LORA, H * QH))
    w_ukv = din("w_ukv", (L, KV_LORA, H * 128))
    w_conv_out = din("w_conv_out", (L, D, D))
    w_attn_out = din("w_attn_out", (L, D, D))
    w_merge = din("w_merge", (L, D, D))
    w_gate_up = din("w_gate_up", (L, D, 2 * DFF))
    w_down = din("w_down", (L, DFF, D))

    y_p = dout("y_p", (2, SEQ, D))
    y_s = dout("y_s", (DEC_SEQ, D))
    nconv_p = dout("nconv_p", (L, 2, 2, D))
    nckv_p = dout("nckv_p", (L, 2, SEQ, KV_LORA))
    nkr_p = dout("nkr_p", (L, 2, SEQ, QK_ROPE))
    nconv_s = dout("nconv_s", (L, 2, D))
    nckv_s = dout("nckv_s", (L, DEC_SEQ, KV_LORA))
    nkr_s = dout("nkr_s", (L, DEC_SEQ, QK_ROPE))
    dbg = {}
    if debug:
        for nm, shp in (("d_xn", (128, KC * SEG)), ("d_bc", (128, KC * SEG)), ("d_qn", (128, 3 * SEG)),
                        ("d_attn", (128, KC * SEG)), ("d_z", (128, KC * SEG)), ("d_xres", (128, KC * SEG)),
                        ("d_xres2", (128, KC * SEG)), ("d_qt", (128, SEG)), ("d_kt", (128, SEG))):
            dbg[nm] = dout(nm, shp)

    wv = {
        "w_in": w_in.rearrange("l (kc p) m -> l p kc m", p=128),
        "w_uq": w_uq.rearrange("l (kc p) m -> l p kc m", p=128),
        "w_ukv": w_ukv.rearrange("l (kc p) m -> l p kc m", p=128),
        "w_conv_out": w_conv_out.rearrange("l (kc p) m -> l p kc m", p=128),
        "w_attn_out": w_attn_out.rearrange("l (kc p) m -> l p kc m", p=128),
        "w_merge": w_merge.rearrange("l (kc p) m -> l p kc m", p=128),
        "w_gate_up": w_gate_up.rearrange("l (kc p) m -> l p kc m", p=128),
        "w_down": w_down.rearrange("l (kc p) m -> l p kc m", p=128),
    }

    with ExitStack() as es:
        def sb(name, shape, dt):
            return es.enter_context(nc.sbuf_tensor(name, list(shape), dt))

        xresT = sb("xresT", (128, KC, SEG), F32)
        xnT = sb("xnT", (128, KC, SEG), BF16)
        big = sb("big", (128, 24, SEG), BF16)
        qnT = sb("qnT", (128, 3, SEG), BF16)
        cT = sb("cT", (128, L, 2, NKEYMAX), BF16)
        krT = sb("krT", (128, L, NKEYMAX), BF16)
        scr = sb("scr", (128, KC, SEG), F32)
        wbuf = sb("wbuf", (128, NSLOT, KC, 128), BF16)
        wuq = sb("wuq", (128, 3, H * QH), BF16)
        wuqB = sb("wuqB", (128, 3, H, 32), BF16)
        wuqA = sb("wuqA", (128, 3, H, 32), BF16)
        wukv = sb("wukv", (128, 2, H * 128), BF16)
        wkrB = sb("wkrB", (128, KC, 32), BF16)
        KT = sb("KT", (128, 2, NKEYMAX), BF16)
        QT = sb("QT", (128, 2, SEG), BF16)
        VB = sb("VB", (128, 2, 17, 128), BF16)
        PT = sb("PT", (128, 3, SEG), BF16)
        Rr = sb("Rr", (128, SEG), F32)
        u_t = sb("u_t", (128, SEG + 2), F32)
        cv_t = sb("cv_t", (128, SEG), F32)
        cg_t = sb("cg_t", (128, SEG), F32)
        sa_t = sb("sa_t", (128, SEG), F32)
        sb_t = sb("sb_t", (128, SEG), F32)
        sg_t = sb("sg_t", (128, 2, SEG), F32)
        sq_t = sb("sq_t", (128, 2, SEG), BF16)
        rt_t = sb("rt_t", (128, SEG), F32)
        rstd_t = sb("rstd_t", (128, SEG), F32)
        kro_t = sb("kro_t", (128, SEG), F32)
        t1_t = sb("t1_t", (128, SEG), F32)
        t2_t = sb("t2_t", (128, SEG), F32)
        cos_t = sb("cos_t", (128, SEG), F32)
        sin_t = sb("sin_t", (128, SEG), F32)
        xtok = sb("xtok", (128, 2, D), F32)
        xin = sb("xin", (128, 2, D), F32)
        ost = sb("ost", (128, 2, 288), F32)
        hist = sb("hist", (128, L, KC, 2), F32)
        smallp = sb("smallp_sb", (128, L, NSP), F32)
        ident = sb("ident_sb", (128, 128), F32)
        ones = sb("ones_sb", (128, 128), BF16)
        ps = [es.enter_context(nc.psum_tensor(f"ps{i}", [128, 512], F32)) for i in range(8)]

        bcT = lambda m: big[:, m, :]
        attnT_i, zT_i = 8, 16
        qlat_i, ckvs_i, cnew_i = 0, 3, 5

        rr = {"mm": 0, "aux": 0, "acc": 0}
        MM_BANKS, AUX_BANKS, ACC_BANKS, SS_BANK = (0, 1, 2), (4, 7), (5, 6), 3

        def bank(cls):
            lst = {"mm": MM_BANKS, "aux": AUX_BANKS, "acc": ACC_BANKS}[cls]
            b = lst[rr[cls] % len(lst)]
            rr[cls] += 1
            return b

        def act(out, in_, func, reads, writes, scale=None, bias=None):
            kw = {}
            if scale is not None:
                kw["scale"] = scale
            if bias is not None:
                kw["bias"] = bias
            return R.op("act", lambda e: e.activation(out=out, in_=in_, func=func, **kw), reads, writes)

        def tt(out, in0, in1, op, reads, writes):
            return R.op("dve", lambda e: e.tensor_tensor(out=out, in0=in0, in1=in1, op=op), reads, writes)

        def stt(out, in0, scalar, in1, op0, op1, reads, writes):
            return R.op("dve", lambda e: e.scalar_tensor_tensor(out=out, in0=in0, scalar=scalar, in1=in1,
                                                               op0=op0, op1=op1), reads, writes)

        def dcopy(out, in_, reads, writes):
            return R.op("dve", lambda e: e.tensor_copy(out=out, in_=in_), reads, writes)

        def acopy(out, in_, reads, writes, scale=None):
            if scale is None:
                return R.op("act", lambda e: e.copy(out=out, in_=in_), reads, writes)
            return R.op("act", lambda e: e.mul(out=out, in_=in_, mul=scale), reads, writes)

        def mm(out, lhsT, rhs):
            return lambda e, st, sp: e.matmul(out, lhsT, rhs, start=st, stop=sp)

        def tr(out, in_, idn):
            return lambda e, st, sp: e.transpose(out, in_, idn)

        out_tickets = []

        def chk(name):
            if stop == name:
                raise _Stop()

        class WS:
            def __init__(self):
                self.sched = []
                self.emitted = 0

            def advance(self, upto):
                upto = min(upto, len(self.sched) - 1)
                while self.emitted <= upto:
                    i = self.emitted
                    (l, name, kc0, nkc, c0, ncols) = self.sched[i]
                    slot = i % NSLOT
                    src = wv[name][l, :, kc0:kc0 + nkc, c0:c0 + ncols]
                    dst = wbuf[:, slot, 0:nkc, 0:ncols]
                    R.dma("pool", lambda e, dst=dst, src=src: e.dma_start(out=dst, in_=src), f"W{slot}",
                          reads=(), writes=(("w", slot),))
                    self.emitted += 1

            def use(self, i, desc):
                assert self.sched[i] == desc, (i, self.sched[i], desc)
                assert i < self.emitted, (i, self.emitted)
                return i % NSLOT

        ws = WS()
        wpos = [0]

        def wtile(desc):
            i = wpos[0]
            wpos[0] += 1
            slot = ws.use(i, desc)
            return slot, ("w", slot)

        def wdone():
            ws.advance(wpos[0] - 1 + NSLOT)
            if aux_q:
                aux_q.pop(0)()

        segs = []
        for s in range(n_prompt_seq):
            for q in range(n_quarters):
                segs.append(dict(kind="p", seq=s, q=q, N=SEG, pos0=q * SEG, key0=q * SEG))
        if with_sample:
            segs.append(dict(kind="s", seq=0, q=4, N=DEC_SEQ, pos0=PAST, key0=PAST))

        def sl_sched(l):
            out = []
            for m in range(KC):
                out.append((l, "w_in", 0, KC, O1 + 128 * m, 128))
                out.append((l, "w_in", 0, KC, O2 + 128 * m, 128))
                out.append((l, "w_in", 0, KC, 128 * m, 128))
            for i in range(3):
                out.append((l, "w_in", 0, KC, O3 + 128 * i, 128))
            for i in range(2):
                out.append((l, "w_in", 0, KC, O4 + 128 * i, 128))
            out.append((l, "w_in", 0, KC, O5, 32))
            for m in range(KC):
                out.append((l, "w_conv_out", 0, KC, 128 * m, 128))
                out.append((l, "w_in", 0, KC, O6 + 128 * m, 128))
                out.append((l, "w_attn_out", 0, KC, 128 * m, 128))
                out.append((l, "w_in", 0, KC, O7 + 128 * m, 128))
            for m in range(KC):
                out.append((l, "w_merge", 0, KC, 128 * m, 128))
            for f in range(FC):
                out.append((l, "w_gate_up", 0, KC, 128 * f, 128))
                out.append((l, "w_gate_up", 0, KC, DFF + 128 * f, 128))
            for m in range(KC):
                out.append((l, "w_down", 0, 8, 128 * m, 128))
                out.append((l, "w_down", 8, 8, 128 * m, 128))
                out.append((l, "w_down", 16, 6, 128 * m, 128))
            return out

        for sg in segs:
            for l in range(n_layers):
                ws.sched.extend(sl_sched(l))

        R.dma("sp", lambda e: e.dma_start(out=smallp[:, :, :].rearrange("p l c -> p (l c)"), in_=smallp_d[:, :]),
              "S_small", writes=("smallp",))
        R.dma("sp", lambda e: e.dma_start(out=ident[:, :], in_=ident_d[:, :]), "S_ident", writes=("ident",))
        R.op("dve", lambda e: e.memset(ones[:, :], 1.0), writes=("ones",))
        R.op("dve", lambda e: e.memset(VB[:, 0, :, 64:128], 1.0), writes=(("V", 0),))
        R.op("dve", lambda e: e.memset(VB[:, 1, :, 0:64], 1.0), writes=(("V", 1),))
        R.op("dve", lambda e: e.memset(KT[:, :, :], 0.0), writes=(("KT", 0), ("KT", 1)))
        R.op("dve", lambda e: e.memset(QT[:, :, :], 0.0), writes=(("QT", 0), ("QT", 1)))
        ws.advance(NSLOT - 1)

        def sp_col(l, c0, n=1):
            return smallp[:, l, c0:c0 + n]

        aux_q = []

        def load_wuq_wukv(l, flush=False):
            for kc in range(3):
                aux_q.append(lambda kc=kc: R.dma("pool", lambda e: e.dma_start(
                    out=wuq[:, kc, :], in_=wv["w_uq"][l, :, kc, :]), "S_wuq", writes=("wuq",)))
            for kc in range(2):
                for hf in range(2):
                    aux_q.append(lambda kc=kc, hf=hf: R.dma("pool", lambda e: e.dma_start(
                        out=wukv[:, kc, hf * 1024:(hf + 1) * 1024],
                        in_=wv["w_ukv"][l, :, kc, hf * 1024:(hf + 1) * 1024]), "S_wukv", writes=("wukv",)))
            if flush:
                while aux_q:
                    aux_q.pop(0)()

        def prep_wuqAB():
            wv4 = wuq[:, :, :].rearrange("p k (h d) -> p k h d", d=QH)
            acopy(wuqB[:, :, :, 0:16], wv4[:, :, :, 80:96], reads=("wuq",), writes=("wuqB",), scale=-1.0)
            acopy(wuqB[:, :, :, 16:32], wv4[:, :, :, 64:80], reads=("wuq",), writes=("wuqB",))
            acopy(wuqA[:, :, :, :], wv4[:, :, :, 64:96], reads=("wuq",), writes=("wuqA",))

        def norm_finish(ssb, nfeat, N):
            act(rt_t[:, :N], ps[ssb][:, :N], AF.Ln, reads=(("ps", ssb),), writes=("rt",), scale=1.0 / nfeat,
                bias=EPS)
            act(rstd_t[:, :N], rt_t[:, :N], AF.Exp, reads=("rt",), writes=("rstd",), scale=-0.5)

        def prenorm(l, gcol, N):
            ssb = SS_BANK
            for kc in range(KC):
                b = kc % 2
                act(sq_t[:, b, :N], xresT[:, kc, :N], AF.Square, reads=(("xres", kc),), writes=(("sq", b),))
                R.mm1(("ps", ssb), (lambda e, b=b, kc=kc: e.matmul(ps[ssb][:, :N], ones[:, :], sq_t[:, b, :N],
                                                                 start=(kc == 0), stop=(kc == KC - 1))),
                      reads=(("sq", b), "ones"), first=(kc == 0))
            norm_finish(ssb, D, N)
            for kc in range(KC):
                stt(xnT[:, kc, :N], xresT[:, kc, :N], sp_col(l, gcol + kc), rstd_t[:, :N], ALU.mult, ALU.mult,
                    reads=(("xres", kc), "rstd", "smallp"), writes=(("xn", kc),))

        def dense_group(l, name, c0, ncols, rhs_fn, rhs_keys, nkc_total=KC, pieces=None):
            b = bank("mm")
            mms = []
            pieces = pieces or [(0, nkc_total)]
            for (k0, nk) in pieces:
                slot, wkey = wtile((l, name, k0, nk, c0, ncols))
                for k in range(nk):
                    kc = k0 + k
                    mms.append((mm(ps[b][0:ncols, :rhs_fn.N], wbuf[:, slot, k, 0:ncols], rhs_fn(kc)),
                                (wkey, rhs_keys(kc))))
            R.mmgroup(("ps", b), mms)
            wdone()
            return b

        class RhsFn:
            def __init__(self, fn, N):
                self.fn = fn
                self.N = N

            def __call__(self, kc):
                return self.fn(kc)

        def issue_input(sg, blk):
            if blk in sg.setdefault("in_issued", set()):
                return
            sg["in_issued"].add(blk)
            N_, pos0_ = sg["N"], sg["pos0"]
            nb = min(128, N_ - blk * 128)
            if sg["kind"] == "p":
                src = x_p[sg["seq"], pos0_ + blk * 128: pos0_ + blk * 128 + nb, :]
            else:
                src = x_s[blk * 128: blk * 128 + nb, :]
            xb = blk % 2
            R.dma("sp", lambda e: e.dma_start(out=xin[0:nb, xb, :], in_=src), f"S_xin{xb}", writes=(("xin", xb),))

        def issue_rope(sg):
            if sg.get("rope_issued"):
                return
            sg["rope_issued"] = True
            N_, pos0_ = sg["N"], sg["pos0"]
            R.dma("sp", lambda e: e.dma_start(out=cos_t[:, :N_], in_=rope_d[0, :, pos0_:pos0_ + N_]), "S_cos",
                  writes=("cos",))
            R.dma("sp", lambda e: e.dma_start(out=sin_t[:, :N_], in_=rope_d[1, :, pos0_:pos0_ + N_]), "S_sin",
                  writes=("sin",))

        def prefetch_next(sg):
            i = segs.index(sg)
            if i + 1 < len(segs):
                nx = segs[i + 1]
                for blk in range(min(2, (nx["N"] + 127) // 128)):
                    issue_input(nx, blk)
                issue_rope(nx)

        def segment_layer(sg, l, first_layer, last_layer, nxt_l=None):
            N = sg["N"]
            kind = sg["kind"]
            key0 = sg["key0"]
            pos0 = sg["pos0"]
            q = sg["q"]
            seq = sg["seq"]
            nblk = (N + 127) // 128
            xn_rhs = RhsFn(lambda kc: xnT[:, kc, :N], N)
            xn_keys = lambda kc: ("xn", kc)

            if first_layer:
                for blk in range(nblk):
                    nb = min(128, N - blk * 128)
                    issue_input(sg, blk)
                    xb = blk % 2
                    for half in range(2):
                        b = bank("mm")
                        for j in range(4):
                            kc = half * 4 + j
                            R.mmgroup(("ps", b) if j == 0 else ("psx", b), [
                                (tr(ps[b][:, j * 128: j * 128 + nb], xin[0:nb, xb, kc * 128:(kc + 1) * 128],
                                    ident[0:nb, 0:nb]), (("xin", xb), "ident"))])
                        R.res[("ps", b)] = [("E_pe", R.cnt["E_pe"]), {}]
                        src_ps = ps[b][:, :].rearrange("p (j t) -> p j t", t=128)[:, :, 0:nb]
                        acopy(xresT[:, half * 4:half * 4 + 4, blk * 128: blk * 128 + nb], src_ps,
                              reads=(("ps", b),), writes=tuple(("xres", half * 4 + j) for j in range(4)))
                issue_rope(sg)

            chk("input")
            if kind == "p" and q == 0:
                R.op("dve", lambda e: e.memset(hist[:, l, :, :], 0.0), writes=(("hist", l),))
            if kind == "s":
                R.dma("sp", lambda e: e.dma_start(
                    out=hist[:, l, :, :].rearrange("p k j -> p (k j)"), in_=hist0[:, l * 16:(l + 1) * 16]),
                    "S_hist", writes=(("hist", l),))
                for blk in range(PAST // 128):
                    ob = blk % 2
                    R.dma("sp", lambda e, blk=blk, ob=ob: e.dma_start(
                        out=ost[:, ob, 0:256], in_=c_ckv[l, blk * 128:(blk + 1) * 128, :]), f"S_ost{ob}",
                        writes=(("ost", ob),))
                    R.dma("sp", lambda e, blk=blk, ob=ob: e.dma_start(
                        out=ost[:, ob, 256:288], in_=c_kr[l, blk * 128:(blk + 1) * 128, :]), f"S_ost{ob}",
                        writes=())
                    R.res[("ost", ob)] = [(f"S_ost{ob}", R.cnt[f"S_ost{ob}"]), {}]
                    b = bank("aux")
                    R.mmgroup(("ps", b), [(tr(ps[b][:, 0:128], ost[:, ob, 0:128], ident[:, :]), (("ost", ob), "ident"))])
                    R.mmgroup(("psx", b), [(tr(ps[b][:, 128:256], ost[:, ob, 128:256], ident[:, :]), (("ost", ob), "ident"))])
                    R.mmgroup(("psx", b), [(tr(ps[b][0:32, 256:384], ost[:, ob, 256:288], ident[:, :]), (("ost", ob), "ident"))])
                    R.res[("ps", b)] = [("E_pe", R.cnt["E_pe"]), {}]
                    jt = blk // 4
                    acopy(cT[:, l, :, blk * 128:(blk + 1) * 128],
                          ps[b][:, 0:256].rearrange("p (k t) -> p k t", t=128),
                          reads=(("ps", b),), writes=(("cT", l, jt),))
                    acopy(krT[64:96, l, blk * 128:(blk + 1) * 128], ps[b][0:32, 256:384],
                          reads=(("ps", b),), writes=(("krT", l, jt),))

            chk("hist")
            if sg.get("first_sl") and l == 0:
                load_wuq_wukv(l, flush=True)
                prep_wuqAB()

            chk("wuq")
            prenorm(l, C_GPRE, N)

            chk("prenorm")
            for m in range(KC):
                b_cg = dense_group(l, "w_in", O1 + 128 * m, 128, xn_rhs, xn_keys)
                b_xi = dense_group(l, "w_in", O2 + 128 * m, 128, xn_rhs, xn_keys)
                acopy(cg_t[:, :N], ps[b_cg][:, :N], reads=(("ps", b_cg),), writes=("cg",))
                dcopy(u_t[:, 0:2], hist[:, l, m, :], reads=(("hist", l),), writes=("u",))
                tt(u_t[:, 2:2 + N], ps[b_xi][:, :N], cg_t[:, :N], ALU.mult, reads=(("ps", b_xi), "cg", "u"),
                   writes=("u",))
                cws = [sp_col(l, C_CW + 3 * m + j) for j in range(3)]
                cw = lambda j, cws=cws: cws[j]
                R.op("dve", lambda e, c2=cws[2]: e.tensor_scalar(out=cv_t[:, :N], in0=u_t[:, 2:2 + N], scalar1=c2,
                                                                 scalar2=None, op0=ALU.mult),
                     reads=("u", "smallp"), writes=("cv",))
                stt(cv_t[:, :N], u_t[:, 1:1 + N], cw(1), cv_t[:, :N], ALU.mult, ALU.add, reads=("u", "cv", "smallp"),
                    writes=("cv",))
                stt(cv_t[:, :N], u_t[:, 0:N], cw(0), cv_t[:, :N], ALU.mult, ALU.add, reads=("u", "cv", "smallp"),
                    writes=("cv",))
                dcopy(hist[:, l, m, :], u_t[:, N:N + 2], reads=("u",), writes=(("hist", l),))
                b_bg = dense_group(l, "w_in", 128 * m, 128, xn_rhs, xn_keys)
                tt(big[:, m, :N], ps[b_bg][:, :N], cv_t[:, :N], ALU.mult, reads=(("ps", b_bg), "cv"),
                   writes=(("big", m),))
            if (kind == "p" and q == n_quarters - 1) or kind == "s":
                if kind == "p":
                    dst = nconv_p[l, seq, :, :].rearrange("j (k p) -> p k j", p=128)
                else:
                    dst = nconv_s[l, :, :].rearrange("j (k p) -> p k j", p=128)
                for kc in range(KC):
                    out_tickets.append(R.dma("sp", lambda e, dst=dst, kc=kc: e.dma_start(
                        out=dst[:, kc, :], in_=hist[:, l, kc, :], allow_slow_non_contiguous=True), f"S_histout{l}",
                        reads=(("hist", l),)))

            chk("conv")
            for i in range(3):
                bq = dense_group(l, "w_in", O3 + 128 * i, 128, xn_rhs, xn_keys)
                acopy(scr[:, qlat_i + i, :N], ps[bq][:, :N], reads=(("ps", bq),), writes=(("scr", qlat_i + i),))
            for i in range(2):
                bq = dense_group(l, "w_in", O4 + 128 * i, 128, xn_rhs, xn_keys)
                acopy(scr[:, ckvs_i + i, :N], ps[bq][:, :N], reads=(("ps", bq),), writes=(("scr", ckvs_i + i),))
            jt_new = q
            slot, wkey = wtile((l, "w_in", 0, KC, O5, 32))
            acopy(wkrB[:, :, 0:16], wbuf[:, slot, :, 16:32], reads=(wkey,), writes=("wkrB",), scale=-1.0)
            acopy(wkrB[:, :, 16:32], wbuf[:, slot, :, 0:16], reads=(wkey,), writes=("wkrB",))
            bA = bank("aux")
            R.mmgroup(("ps", bA), [(mm(ps[bA][0:32, :N], wbuf[:, slot, kc, 0:32], xnT[:, kc, :N]),
                                    (wkey, ("xn", kc))) for kc in range(KC)])
            wdone()
            bB = bank("aux")
            R.mmgroup(("ps", bB), [(mm(ps[bB][0:32, :N], wkrB[:, kc, :], xnT[:, kc, :N]), ("wkrB", ("xn", kc)))
                                   for kc in range(KC)])
            tt(t1_t[0:32, :N], ps[bA][0:32, :N], cos_t[0:32, :N], ALU.mult, reads=(("ps", bA), "cos"), writes=("t1",))
            tt(t2_t[0:32, :N], ps[bB][0:32, :N], sin_t[0:32, :N], ALU.mult, reads=(("ps", bB), "sin"), writes=("t2",))
            tt(kro_t[0:32, :N], t1_t[0:32, :N], t2_t[0:32, :N], ALU.add, reads=("t1", "t2"), writes=("kro",))
            acopy(krT[64:96, l, key0:key0 + N], kro_t[0:32, :N], reads=("kro",), writes=(("krT", l, jt_new),))
            for i in range(3):
                b = i % 2
                act(sq_t[:, b, :N], scr[:, qlat_i + i, :N], AF.Square, reads=(("scr", qlat_i + i),),
                    writes=(("sq", b),))
                R.mm1(("ps", SS_BANK), (lambda e, b=b, i=i: e.matmul(ps[SS_BANK][:, :N], ones[:, :], sq_t[:, b, :N],
                                                                    start=(i == 0), stop=(i == 2))),
                      reads=(("sq", b), "ones"), first=(i == 0))
            norm_finish(SS_BANK, Q_LORA, N)
            for i in range(3):
                stt(qnT[:, i, :N], scr[:, qlat_i + i, :N], sp_col(l, C_GQ + i), rstd_t[:, :N], ALU.mult, ALU.mult,
                    reads=(("scr", qlat_i + i), "rstd", "smallp"), writes=(("qn", i),))
            for i in range(2):
                b = i % 2
                act(sq_t[:, b, :N], scr[:, ckvs_i + i, :N], AF.Square, reads=(("scr", ckvs_i + i),),
                    writes=(("sq", b),))
                R.mm1(("ps", SS_BANK), (lambda e, b=b, i=i: e.matmul(ps[SS_BANK][:, :N], ones[:, :], sq_t[:, b, :N],
                                                                    start=(i == 0), stop=(i == 1))),
                      reads=(("sq", b), "ones"), first=(i == 0))
            norm_finish(SS_BANK, KV_LORA, N)
            for i in range(2):
                stt(scr[:, cnew_i + i, :N], scr[:, ckvs_i + i, :N], sp_col(l, C_GKV + i), rstd_t[:, :N], ALU.mult,
                    ALU.mult, reads=(("scr", ckvs_i + i), "rstd", "smallp"), writes=(("scr", cnew_i + i),))
                acopy(cT[:, l, i, key0:key0 + N], scr[:, cnew_i + i, :N], reads=(("scr", cnew_i + i),),
                      writes=(("cT", l, jt_new),))
            chk("lowrank")
            for blk in range(nblk):
                nb = min(128, N - blk * 128)
                ob = blk % 2
                b = bank("aux")
                R.mmgroup(("ps", b), [(tr(ps[b][0:nb, 0:128], scr[:, cnew_i, blk * 128:blk * 128 + nb], ident[:, :]),
                                       (("scr", cnew_i), "ident"))])
                R.mmgroup(("psx", b), [(tr(ps[b][0:nb, 128:256], scr[:, cnew_i + 1, blk * 128:blk * 128 + nb],
                                          ident[:, :]), (("scr", cnew_i + 1), "ident"))])
                R.mmgroup(("psx", b), [(tr(ps[b][0:nb, 256:288], kro_t[0:32, blk * 128:blk * 128 + nb],
                                          ident[0:32, 0:32]), ("kro", "ident"))])
                R.res[("ps", b)] = [("E_pe", R.cnt["E_pe"]), {}]
                dcopy(ost[0:nb, ob, :], ps[b][0:nb, 0:288], reads=(("ps", b),), writes=(("ost", ob),))
                if kind == "p":
                    r0 = pos0 + blk * 128
                    d1 = nckv_p[l, seq, r0:r0 + nb, :]
                    d2 = nkr_p[l, seq, r0:r0 + nb, :]
                else:
                    d1 = nckv_s[l, 0:nb, :]
                    d2 = nkr_s[l, 0:nb, :]
                out_tickets.append(R.dma("sp", lambda e, d1=d1, ob=ob, nb=nb: e.dma_start(
                    out=d1, in_=ost[0:nb, ob, 0:256]), f"S_ost{ob}", reads=(("ost", ob),)))
                out_tickets.append(R.dma("sp", lambda e, d2=d2, ob=ob, nb=nb: e.dma_start(
                    out=d2, in_=ost[0:nb, ob, 256:288]), f"S_ost{ob}", reads=(("ost", ob),)))

            chk("ctxout")
            nkeys = key0 + N
            ktiles = [(j * 512, min(512, nkeys - j * 512)) for j in range((nkeys + 511) // 512)]
            kblocks = [(j * 128, min(128, nkeys - j * 128)) for j in range((nkeys + 127) // 128)]
            for hb in range(2):
                for jt, (k0, nk) in enumerate(ktiles):
                    dcopy(KT[64:96, hb, k0:k0 + nk], krT[64:96, l, k0:k0 + nk], reads=(("krT", l, jt),),
                          writes=(("KT", hb),))
            b4 = SS_BANK

            def prep_pieces(h):
                hb = h % 2
                eo = h % 2
                pieces = []
                ev = dcopy
                ev_kt = acopy if kind == "s" else dcopy
                for jt, (k0, nk) in enumerate(ktiles):
                    def p_kt(jt=jt, k0=k0, nk=nk):
                        b = bank("aux")
                        R.mmgroup(("ps", b), [(mm(ps[b][0:64, :nk], wukv[:, kc, h * 128:h * 128 + 64],
                                                  cT[:, l, kc, k0:k0 + nk]), ("wukv", ("cT", l, jt)))
                                              for kc in range(2)])
                        ev_kt(KT[0:64, hb, k0:k0 + nk], ps[b][0:64, :nk], reads=(("ps", b),), writes=(("KT", hb),))
                    pieces.append(p_kt)
                for g0 in range(0, len(kblocks), 8):
                    def p_v(g0=g0):
                        grp = kblocks[g0:g0 + 8]
                        b = bank("aux")
                        for j, (k0, nk) in enumerate(grp):
                            R.mmgroup(("ps", b) if j == 0 else ("psx", b), [
                                (mm(ps[b][0:nk, j * 64:(j + 1) * 64], cT[:, l, kc, k0:k0 + nk],
                                    wukv[:, kc, h * 128 + 64:h * 128 + 128]), ("wukv", ("cT", l, k0 // 512)))
                                for kc in range(2)])
                        R.res[("ps", b)] = [("E_pe", R.cnt["E_pe"]), {}]
                        vc0 = 0 if eo == 0 else 64
                        full = [g for g in grp if g[1] == 128]
                        if full:
                            nf = len(full)
                            ev(VB[:, eo, g0:g0 + nf, vc0:vc0 + 64],
                                  ps[b][:, 0:nf * 64].rearrange("p (j v) -> p j v", v=64),
                                  reads=(("ps", b),), writes=(("V", eo),))
                        if len(full) < len(grp):
                            j = len(full)
                            nk = grp[j][1]
                            ev(VB[0:nk, eo, g0 + j, vc0:vc0 + 64], ps[b][0:nk, j * 64:(j + 1) * 64],
                                  reads=(("ps", b),), writes=(("V", eo),))
                    pieces.append(p_v)

                def p_rot4():
                    g = h // 4
                    bA4 = bank("aux")
                    R.mmgroup(("ps", bA4), [(mm(ps[bA4][:, :N], wuqA[:, kc, 4 * g:4 * g + 4, :], qnT[:, kc, :N]),
                                             ("wuqA", ("qn", kc))) for kc in range(3)])
                    R.mmgroup(("ps", b4), [(mm(ps[b4][:, :N], wuqB[:, kc, 4 * g:4 * g + 4, :], qnT[:, kc, :N]),
                                            ("wuqB", ("qn", kc))) for kc in range(3)])
                    tt(t1_t[:, :N], ps[bA4][:, :N], cos_t[:, :N], ALU.mult, reads=(("ps", bA4), "cos"), writes=("t1",))
                    tt(t2_t[:, :N], ps[b4][:, :N], sin_t[:, :N], ALU.mult, reads=(("ps", b4), "sin"), writes=("t2",))
                    tt(kro_t[:, :N], t1_t[:, :N], t2_t[:, :N], ALU.add, reads=("t1", "t2"), writes=("kro",))

                def p_q():
                    bA = bank("aux")
                    R.mmgroup(("ps", bA), [(mm(ps[bA][0:64, :N], wuq[:, kc, h * QH:h * QH + 64], qnT[:, kc, :N]),
                                            ("wuq", ("qn", kc))) for kc in range(3)])
                    ev(QT[0:64, hb, :N], ps[bA][0:64, :N], reads=(("ps", bA),), writes=(("QT", hb),))
                    j4 = h % 4
                    R.op("pool", lambda e: e.tensor_copy(out=QT[64:96, hb, :N], in_=kro_t[32 * j4:32 * j4 + 32, :N]),
                         reads=("kro",), writes=(("QT", hb),))
                    if debug and h == 0 and l == 0 and sg is segs[0]:
                        R.dma("pool", lambda e: e.dma_start(out=dbg["d_qt"][:, :], in_=QT[:, 0, :]), "S_dbg",
                              reads=(("QT", 0),))
                        R.dma("pool", lambda e: e.dma_start(out=dbg["d_kt"][:, :], in_=KT[:, 0, 0:SEG]), "S_dbg",
                              reads=(("KT", 0),))
                if h % 4 == 0:
                    pieces.insert(0, p_rot4)
                    pieces.insert(1, p_q)
                else:
                    pieces.insert(0, p_q)
                return pieces

            def head_loop(h, pieces, deferred):
                hb = h % 2
                eo = h % 2
                accb = bank("acc")
                nkb = len(kblocks)
                info = []
                for kb, (k0, nk) in enumerate(kblocks):
                    if kind == "p" and k0 >= key0:
                        bd = (k0 - key0) // 128
                        info.append((k0, nk, bd, 128 * bd))
                    else:
                        info.append((k0, nk, None, 0))
                sbanks = {}

                def issue_s(kb):
                    k0, nk, bd, qlo = info[kb]
                    sbk = bank("mm")
                    R.mmgroup(("ps", sbk), [(mm(ps[sbk][0:nk, qlo:N], KT[0:QH, hb, k0:k0 + nk], QT[0:QH, hb, qlo:N]),
                                             (("KT", hb), ("QT", hb)))])
                    sbanks[kb] = sbk

                for kb in range(min(2, nkb)):
                    issue_s(kb)
                for kb in range(nkb):
                    k0, nk, bd, qlo = info[kb]
                    sbk = sbanks[kb]
                    pb = kb % 3
                    act(PT[0:nk, pb, qlo:N], ps[sbk][0:nk, qlo:N], AF.Exp, reads=(("ps", sbk),),
                        writes=(("PT", pb),), scale=ATTN_SCALE)
                    def pv(c0, c1, r1, first, last, kb=kb, pb=pb):
                        o_ap, l_ap, r_ap = ps[accb][:, c0:c1], VB[0:r1, eo, kb, :], PT[0:r1, pb, c0:c1]
                        R.mm1(("ps", accb), (lambda e: e.matmul(o_ap, l_ap, r_ap, start=first, stop=last)),
                              reads=(("V", eo), ("PT", pb)), first=first)
                    if kb + 2 < nkb:
                        issue_s(kb + 2)
                    if kb == 1 and deferred is not None:
                        deferred()
                        deferred = None
                    if pieces:
                        pieces.pop(0)()
                    if bd is None:
                        pv(qlo, N, nk, kb == 0, kb == nkb - 1)
                    else:
                        pv(qlo + 64, N, nk, kb == 0, False)
                        pv(qlo, qlo + 64, 64, False, kb == nkb - 1)
                while pieces:
                    pieces.pop(0)()
                if deferred is not None:
                    deferred()
                dlo, slo = (0, 64) if eo == 0 else (64, 0)

                def norm():
                    act(Rr[dlo:dlo + 64, :N], ps[accb][slo:slo + 64, :N], AF.Ln, reads=(("ps", accb),), writes=("R",))
                    act(Rr[dlo:dlo + 64, :N], Rr[dlo:dlo + 64, :N], AF.Exp, reads=("R",), writes=("R",), scale=-1.0)
                    tt(big[dlo:dlo + 64, attnT_i + h // 2, :N], ps[accb][dlo:dlo + 64, :N], Rr[dlo:dlo + 64, :N],
                       ALU.mult, reads=(("ps", accb), "R"), writes=(("big", attnT_i + h // 2),))
                return norm

            for p in prep_pieces(0):
                p()
            pending = None
            for h in range(H):
                nxt = prep_pieces(h + 1) if h + 1 < H else []
                pending = head_loop(h, nxt, pending)
            pending()

            chk("attn")
            if nxt_l is not None:
                load_wuq_wukv(nxt_l)
            bc_rhs = RhsFn(lambda kc: big[:, kc, :N], N)
            bc_keys = lambda kc: ("big", kc)
            at_rhs = RhsFn(lambda kc: big[:, attnT_i + kc, :N], N)
            at_keys = lambda kc: ("big", attnT_i + kc)
            for m in range(KC):
                b_ya = dense_group(l, "w_conv_out", 128 * m, 128, bc_rhs, bc_keys)
                b_ga = dense_group(l, "w_in", O6 + 128 * m, 128, xn_rhs, xn_keys)
                act(sa_t[:, :N], ps[b_ga][:, :N], AF.Sigmoid, reads=(("ps", b_ga),), writes=("sa",))
                tt(sa_t[:, :N], ps[b_ya][:, :N], sa_t[:, :N], ALU.mult, reads=(("ps", b_ya), "sa"), writes=("sa",))
                b_yb = dense_group(l, "w_attn_out", 128 * m, 128, at_rhs, at_keys)
                b_gb = dense_group(l, "w_in", O7 + 128 * m, 128, xn_rhs, xn_keys)
                act(sb_t[:, :N], ps[b_gb][:, :N], AF.Sigmoid, reads=(("ps", b_gb),), writes=("sb",))
                tt(sb_t[:, :N], ps[b_yb][:, :N], sb_t[:, :N], ALU.mult, reads=(("ps", b_yb), "sb"), writes=("sb",))
                tt(big[:, zT_i + m, :N], sa_t[:, :N], sb_t[:, :N], ALU.add, reads=("sa", "sb"),
                   writes=(("big", zT_i + m),))

            chk("z")
            def postnorm_residual(groups_fn, gcol):
                def ss_mm(m):
                    b = m % 2
                    R.mm1(("ps", SS_BANK), (lambda e, b=b, m=m: e.matmul(ps[SS_BANK][:, :N], ones[:, :],
                                                                        sq_t[:, b, :N], start=(m == 0),
                                                                        stop=(m == KC - 1))),
                          reads=(("sq", b), "ones"), first=(m == 0))
                for m in range(KC):
                    bm = groups_fn(m)
                    if m > 0:
                        ss_mm(m - 1)
                    b = m % 2
                    dcopy(scr[:, m, :N], ps[bm][:, :N], reads=(("ps", bm),), writes=(("scr", m),))
                    act(sq_t[:, b, :N], scr[:, m, :N], AF.Square, reads=(("scr", m),), writes=(("sq", b),))
                ss_mm(KC - 1)
                chk("pn_a")
                norm_finish(SS_BANK, D, N)
                chk("pn_b")
                for m in range(KC):
                    stt(scr[:, m, :N], scr[:, m, :N], sp_col(l, gcol + m), rstd_t[:, :N], ALU.mult, ALU.mult,
                        reads=(("scr", m), "rstd", "smallp"), writes=(("scr", m),))
                    tt(xresT[:, m, :N], xresT[:, m, :N], scr[:, m, :N], ALU.add, reads=(("xres", m), ("scr", m)),
                       writes=(("xres", m),))

            z_rhs = RhsFn(lambda kc: big[:, zT_i + kc, :N], N)
            z_keys = lambda kc: ("big", zT_i + kc)
            postnorm_residual(lambda m: dense_group(l, "w_merge", 128 * m, 128, z_rhs, z_keys), C_GPOST)
            if debug and l == 0 and sg is segs[0]:
                def dump(nm, t_, c0, n_, keyf, eng="pool"):
                    for k in range(n_):
                        R.dma(eng, lambda e, k=k: e.dma_start(out=dbg[nm][:, k * SEG:(k + 1) * SEG], in_=t_[:, c0 + k, :]),
                              "S_dbg", reads=(keyf(c0 + k),))
                dump("d_xn", xnT, 0, KC, lambda k: ("xn", k))
                dump("d_bc", big, 0, 8, lambda k: ("big", k))
                dump("d_qn", qnT, 0, 3, lambda k: ("qn", k))
                dump("d_attn", big, 8, 8, lambda k: ("big", k))
                dump("d_z", big, 16, 8, lambda k: ("big", k))
                dump("d_xres", xresT, 0, KC, lambda k: ("xres", k), "sp")
            chk("merge")
            if last_layer:
                prefetch_next(sg)
            prenorm(l, C_FPRE, N)
            for f in range(FC):
                b_g = dense_group(l, "w_gate_up", 128 * f, 128, xn_rhs, xn_keys)
                b_u = dense_group(l, "w_gate_up", DFF + 128 * f, 128, xn_rhs, xn_keys)
                sgb = f % 2
                act(sg_t[:, sgb, :N], ps[b_g][:, :N], AF.Silu, reads=(("ps", b_g),), writes=(("sg", sgb),))
                tt(big[:, f, :N], ps[b_u][:, :N], sg_t[:, sgb, :N], ALU.mult, reads=(("ps", b_u), ("sg", sgb)),
                   writes=(("big", f),))
            chk("ffn")
            if nxt_l is not None:
                prep_wuqAB()
            h_rhs = RhsFn(lambda kc: big[:, kc, :N], N)
            h_keys = lambda kc: ("big", kc)
            postnorm_residual(lambda m: dense_group(l, "w_down", 128 * m, 128, h_rhs, h_keys,
                                                    pieces=[(0, 8), (8, 8), (16, 6)]), C_FPOST)

            chk("down")
            if debug and l == 0 and sg is segs[0]:
                for k in range(KC):
                    R.dma("sp", lambda e, k=k: e.dma_start(out=dbg["d_xres2"][:, k * SEG:(k + 1) * SEG], in_=xresT[:, k, :]),
                          "S_dbg", reads=(("xres", k),))
            if last_layer:
                for blk in range(nblk):
                    nb = min(128, N - blk * 128)
                    for half in range(2):
                        b = bank("mm")
                        for j in range(4):
                            kc = half * 4 + j
                            R.mmgroup(("ps", b) if j == 0 else ("psx", b), [
                                (tr(ps[b][0:nb, j * 128:(j + 1) * 128], xresT[:, kc, blk * 128:blk * 128 + nb],
                                    ident[:, :]), (("xres", kc), "ident"))])
                        R.res[("ps", b)] = [("E_pe", R.cnt["E_pe"]), {}]
                        acopy(xtok[0:nb, blk % 2, half * 512:(half + 1) * 512], ps[b][0:nb, :], reads=(("ps", b),),
                              writes=(("xtok", blk % 2),))
                    if kind == "p":
                        r0 = pos0 + blk * 128
                        dst = y_p[seq, r0:r0 + nb, :]
                    else:
                        dst = y_s[blk * 128:blk * 128 + nb, :]
                    out_tickets.append(R.dma("sp", lambda e, dst=dst, nb=nb, ob=blk % 2: e.dma_start(
                        out=dst, in_=xtok[0:nb, ob, :]), f"S_xtok{blk % 2}", reads=(("xtok", blk % 2),)))

        try:
            sls = [(sg, l) for sg in segs for l in range(n_layers)]
            segs[0]["first_sl"] = True
            for i, (sg, l) in enumerate(sls):
                nxt_l = sls[i + 1][1] if i + 1 < len(sls) else None
                segment_layer(sg, l, first_layer=(l == 0), last_layer=(l == n_layers - 1), nxt_l=nxt_l)
            assert wpos[0] == len(ws.sched), (wpos[0], len(ws.sched))
        except _Stop:
            pass
        for sk in list(R.cnt.keys()):
            if not sk.startswith("E_"):
                R.streams["sp"].append(("w", sk, R.cnt[sk]))

        fin = {}
        for (sk, v) in out_tickets:
            fin[sk] = max(fin.get(sk, 0), v)
        if "S_dbg" in R.cnt:
            fin["S_dbg"] = R.cnt["S_dbg"]
        for sk, v in fin.items():
            R.streams["sp"].append(("w", sk, v))

        semh = {}
        for sk in sorted(R.cnt.keys()):
            semh[sk] = es.enter_context(nc.semaphore(sk))
        with nc.Block() as block:
            @block.tensor
            def _(e):
                _replay(e, R.streams["pe"], semh)

            @block.scalar
            def _(e):
                _replay(e, R.streams["act"], semh)

            @block.vector
            def _(e):
                _replay(e, R.streams["dve"], semh)

            @block.gpsimd
            def _(e):
                _replay(e, R.streams["pool"], semh)

            @block.sync
            def _(e):
                _replay(e, R.streams["sp"], semh)
    stats = {k: len(v) for k, v in R.streams.items()}
    return nc, stats


def _host_constants():
    half = QK_ROPE // 2
    inv = ROPE_THETA ** (-np.arange(half, dtype=np.float32) / np.float32(half))
    pos = np.arange(NKEYMAX, dtype=np.float32)
    ang = pos[None, :] * inv[:, None].astype(np.float32)
    cos = np.cos(ang).astype(np.float32)
    sin = np.sin(ang).astype(np.float32)
    idx = np.arange(128) % half
    rope = np.stack([cos[idx], sin[idx]], axis=0).astype(np.float32)
    ident = np.eye(128, dtype=np.float32)
    return rope, ident


def _feat_major(v, nchunk):
    return np.ascontiguousarray(v.reshape(nchunk, 128).T)


def make_in_maps(inputs):
    f = lambda k: np.ascontiguousarray(np.asarray(inputs[k], dtype=np.float32))
    x_prompt, x_sample = f("x_prompt"), f("x_sample")
    state_conv, cache_ckv, cache_krope = f("state_conv"), f("cache_ckv"), f("cache_krope")
    rope, ident = _host_constants()
    smallp = np.zeros((128, L, NSP), np.float32)
    for l in range(L):
        smallp[:, l, C_GPRE:C_GPRE + 8] = _feat_major(f("norm_attn_pre")[l], 8)
        smallp[:, l, C_GPOST:C_GPOST + 8] = _feat_major(f("norm_attn_post")[l], 8)
        smallp[:, l, C_FPRE:C_FPRE + 8] = _feat_major(f("norm_ffn_pre")[l], 8)
        smallp[:, l, C_FPOST:C_FPOST + 8] = _feat_major(f("norm_ffn_post")[l], 8)
        smallp[:, l, C_GQ:C_GQ + 3] = _feat_major(f("norm_q")[l], 3)
        smallp[:, l, C_GKV:C_GKV + 2] = _feat_major(f("norm_kv")[l], 2)
        cw = f("conv_w")[l]
        for j in range(3):
            smallp[:, l, C_CW + j:C_CW + 24:3] = _feat_major(cw[j], 8)
    shared = dict(smallp=smallp.reshape(128, L * NSP), rope=rope, ident=ident,
                  w_in=f("w_in"), w_uq=f("w_uq"), w_ukv=f("w_ukv"), w_conv_out=f("w_conv_out"),
                  w_attn_out=f("w_attn_out"), w_merge=f("w_merge"), w_gate_up=f("w_gate_up"), w_down=f("w_down"))
    maps = []
    for c in range(NCORES):
        h0 = np.zeros((128, L, KC, 2), np.float32)
        for l in range(L):
            for j in range(2):
                h0[:, l, :, j] = _feat_major(state_conv[l, c, j], 8)
        m = dict(shared)
        m.update(x_p=np.ascontiguousarray(x_prompt[2 * c:2 * c + 2]), x_s=np.ascontiguousarray(x_sample[c]),
                 c_ckv=np.ascontiguousarray(cache_ckv[:, c]), c_kr=np.ascontiguousarray(cache_krope[:, c]),
                 hist0=h0.reshape(128, L * KC * 2))
        maps.append(m)
    return maps


_PROG = None


def kernel(**inputs):
    global _PROG
    if _PROG is None:
        _PROG = build_program()[0]
    in_maps = make_in_maps(inputs)
    res = run_bass_kernel_spmd(_PROG, in_maps, core_ids=list(range(NCORES)))
    rs = res.results
    B = 2 * NCORES
    y_prompt = np.concatenate([r["y_p"] for r in rs], axis=0)
    y_sample = np.stack([r["y_s"] for r in rs], axis=0)
    nconv_p = np.concatenate([r["nconv_p"] for r in rs], axis=1)
    nckv_p = np.concatenate([r["nckv_p"] for r in rs], axis=1)
    nkr_p = np.concatenate([r["nkr_p"] for r in rs], axis=1)
    nconv_s = np.stack([r["nconv_s"] for r in rs], axis=1)
    nckv_s = np.stack([r["nckv_s"] for r in rs], axis=1)
    nkr_s = np.stack([r["nkr_s"] for r in rs], axis=1)
    outs = (y_prompt, y_sample, nconv_p, nckv_p, nkr_p, nconv_s, nckv_s, nkr_s)
    return tuple(np.ascontiguousarray(o, dtype=np.float32) for o in outs)
```

```python
import numpy as np
from contextlib import ExitStack
import concourse.bass as bass
import concourse.mybir as mybir
from concourse.bass_utils import run_bass_kernel_spmd

F32 = mybir.dt.float32
BF16 = mybir.dt.bfloat16
ALU = mybir.AluOpType
AF = mybir.ActivationFunctionType

NCORES = 8
D = 1024
KC = 8
L = 2
SEQ = 2048
SEG = 512
DEC_SEQ = 32
PAST = 2048
NKEYMAX = PAST + DEC_SEQ
H = 16
QK_NOPE = 64
QK_ROPE = 32
QH = QK_NOPE + QK_ROPE
V_HEAD = 64
Q_LORA = 384
KV_LORA = 256
DFF = 2816
FC = DFF // 128
D_IN = 3 * D + Q_LORA + KV_LORA + QK_ROPE + 2 * D
O1, O2, O3 = D, 2 * D, 3 * D
O4 = O3 + Q_LORA
O5 = O4 + KV_LORA
O6 = O5 + QK_ROPE
O7 = O6 + D
EPS = 1e-6
ATTN_SCALE = float(QH ** -0.5)
ROPE_THETA = 10000.0
NSLOT = 9
NSP = 61
C_GPRE, C_GPOST, C_FPRE, C_FPOST, C_GQ, C_GKV, C_CW = 0, 8, 16, 24, 32, 35, 37


class Rec:
    ENG = ("pe", "act", "dve", "pool", "sp")

    def __init__(self):
        self.streams = {e: [] for e in self.ENG}
        self.cnt = {}
        self.waited = {e: {} for e in self.ENG}
        self.res = {}

    def _deps(self, eng, reads, writes):
        deps = {}

        def add(sk, v):
            if eng == "pe" and sk == "E_pe":
                return
            if deps.get(sk, 0) < v:
                deps[sk] = v

        for r in reads:
            st = self.res.get(r)
            if st and st[0] is not None:
                add(*st[0])
        for w in writes:
            st = self.res.get(w)
            if st:
                if st[0] is not None:
                    add(*st[0])
                for sk, v in st[1].items():
                    add(sk, v)
        wd = self.waited[eng]
        for sk, v in deps.items():
            if wd.get(sk, 0) < v:
                self.streams[eng].append(("w", sk, v))
                wd[sk] = v

    def _commit(self, t, reads, writes):
        for r in reads:
            st = self.res.setdefault(r, [None, {}])
            if st[1].get(t[0], 0) < t[1]:
                st[1][t[0]] = t[1]
        for w in writes:
            self.res[w] = [t, {}]

    def op(self, eng, fn, reads=(), writes=()):
        self._deps(eng, reads, writes)
        sk = "E_" + eng
        self.cnt[sk] = self.cnt.get(sk, 0) + 1
        t = (sk, self.cnt[sk])
        self.streams[eng].append(("i", fn, sk, 1))
        self._commit(t, reads, writes)
        return t

    def dma(self, eng, fn, semkey, reads=(), writes=()):
        self._deps(eng, reads, writes)
        self.cnt[semkey] = self.cnt.get(semkey, 0) + 16
        t = (semkey, self.cnt[semkey])
        self.streams[eng].append(("i", fn, semkey, 16))
        self._commit(t, reads, writes)
        return t

    def mmgroup(self, bank_key, mms):
        sk = "E_pe"
        final = (sk, self.cnt.get(sk, 0) + 1)
        n = len(mms)
        for i, (fn, reads) in enumerate(mms):
            self._deps("pe", reads, (bank_key,) if i == 0 else ())
            last = i == n - 1
            self.streams["pe"].append(
                ("i", (lambda e, fn=fn, st=(i == 0), sp=last: fn(e, st, sp)), sk if last else None, 1))
            for r in reads:
                st_ = self.res.setdefault(r, [None, {}])
                if st_[1].get(sk, 0) < final[1]:
                    st_[1][sk] = final[1]
        self.cnt[sk] = final[1]
        self.res[bank_key] = [final, {}]
        return final

    def mm1(self, bank_key, fn, reads, first):
        sk = "E_pe"
        self._deps("pe", reads, (bank_key,) if first else ())
        self.cnt[sk] = self.cnt.get(sk, 0) + 1
        t = (sk, self.cnt[sk])
        self.streams["pe"].append(("i", fn, sk, 1))
        for r in reads:
            st_ = self.res.setdefault(r, [None, {}])
            st_[1][sk] = t[1]
        old = self.res.get(bank_key)
        self.res[bank_key] = [t, {} if (first or not old) else old[1]]
        return t


def _replay(e, stream, semh):
    for it in stream:
        if it[0] == "w":
            e.wait_ge(semh[it[1]], it[2])
        else:
            inst = it[1](e)
            if it[2] is not None:
                inst.then_inc(semh[it[2]], it[3])


class _Stop(Exception):
    pass


def build_program(n_prompt_seq=2, n_quarters=4, with_sample=True, n_layers=L, debug=False, stop=None):
    nc = bass.Bass("TRN2", target_bir_lowering=False)
    R = Rec()

    def din(name, shape):
        return nc.dram_tensor(name, list(shape), F32, kind="ExternalInput").ap()

    def dout(name, shape):
        return nc.dram_tensor(name, list(shape), F32, kind="ExternalOutput").ap()

    x_p = din("x_p", (2, SEQ, D))
    x_s = din("x_s", (DEC_SEQ, D))
    c_ckv = din("c_ckv", (L, PAST, KV_LORA))
    c_kr = din("c_kr", (L, PAST, QK_ROPE))
    hist0 = din("hist0", (128, L * KC * 2))
    smallp_d = din("smallp", (128, L * NSP))
    rope_d = din("rope", (2, 128, NKEYMAX))
    ident_d = din("ident", (128, 128))
    w_in = din("w_in", (L, D, D_IN))
    w_uq = din("w_uq", (L, Q_LORA, H * QH))
    w_ukv = din("w_ukv", (L, KV_LORA, H * 128))
    w_conv_out = din("w_conv_out", (L, D, D))
    w_attn_out = din("w_attn_out", (L, D, D))
    w_merge = din("w_merge", (L, D, D))
    w_gate_up = din("w_gate_up", (L, D, 2 * DFF))
    w_down = din("w_down", (L, DFF, D))

    y_p = dout("y_p", (2, SEQ, D))
    y_s = dout("y_s", (DEC_SEQ, D))
    nconv_p = dout("nconv_p", (L, 2, 2, D))
    nckv_p = dout("nckv_p", (L, 2, SEQ, KV_LORA))
    nkr_p = dout("nkr_p", (L, 2, SEQ, QK_ROPE))
    nconv_s = dout("nconv_s", (L, 2, D))
    nckv_s = dout("nckv_s", (L, DEC_SEQ, KV_LORA))
    nkr_s = dout("nkr_s", (L, DEC_SEQ, QK_ROPE))
    NT_SL = 138
    wscr = nc.dram_tensor("wscr", [L, NT_SL, 128, KC * 128], BF16, kind="Internal").ap()
    dbg = {}
    if debug:
        for nm, shp in (("d_xn", (128, KC * SEG)), ("d_bc", (128, KC * SEG)), ("d_qn", (128, 3 * SEG)),
                        ("d_attn", (128, KC * SEG)), ("d_z", (128, KC * SEG)), ("d_xres", (128, KC * SEG)),
                        ("d_xres2", (128, KC * SEG)), ("d_qt", (128, SEG)), ("d_kt", (128, SEG))):
            dbg[nm] = dout(nm, shp)

    wv = {
        "w_in": w_in.rearrange("l (kc p) m -> l p kc m", p=128),
        "w_uq": w_uq.rearrange("l (kc p) m -> l p kc m", p=128),
        "w_ukv": w_ukv.rearrange("l (kc p) m -> l p kc m", p=128),
        "w_conv_out": w_conv_out.rearrange("l (kc p) m -> l p kc m", p=128),
        "w_attn_out": w_attn_out.rearrange("l (kc p) m -> l p kc m", p=128),
        "w_merge": w_merge.rearrange("l (kc p) m -> l p kc m", p=128),
        "w_gate_up": w_gate_up.rearrange("l (kc p) m -> l p kc m", p=128),
        "w_down": w_down.rearrange("l (kc p) m -> l p kc m", p=128),
    }

    with ExitStack() as es:
        def sb(name, shape, dt):
            return es.enter_context(nc.sbuf_tensor(name, list(shape), dt))

        xresT = sb("xresT", (128, KC, SEG), F32)
        xnT = sb("xnT", (128, KC, SEG), BF16)
        big = sb("big", (128, 24, SEG), BF16)
        qnT = sb("qnT", (128, 3, SEG), BF16)
        cT = sb("cT", (128, L, 2, NKEYMAX), BF16)
        krT = sb("krT", (128, L, NKEYMAX), BF16)
        scr = sb("scr", (128, KC, SEG), F32)
        wbuf = sb("wbuf", (128, NSLOT, KC, 128), BF16)
        wuq = sb("wuq", (128, 3, H * QH), BF16)
        wuqB = sb("wuqB", (128, 3, H, 32), BF16)
        wuqA = sb("wuqA", (128, 3, H, 32), BF16)
        wukv = sb("wukv", (128, 2, H * 128), BF16)
        wkrB = sb("wkrB", (128, KC, 32), BF16)
        KT = sb("KT", (128, 2, NKEYMAX), BF16)
        QT = sb("QT", (128, 2, SEG), BF16)
        VB = sb("VB", (128, 2, 17, 128), BF16)
        PT = sb("PT", (128, 3, SEG), BF16)
        Rr = sb("Rr", (128, SEG), F32)
        u_t = sb("u_t", (128, SEG + 2), F32)
        cv_t = sb("cv_t", (128, SEG), F32)
        cg_t = sb("cg_t", (128, SEG), F32)
        sa_t = sb("sa_t", (128, SEG), F32)
        sb_t = sb("sb_t", (128, SEG), F32)
        sg_t = sb("sg_t", (128, 2, SEG), F32)
        sq_t = sb("sq_t", (128, 2, SEG), BF16)
        rt_t = sb("rt_t", (128, SEG), F32)
        rstd_t = sb("rstd_t", (128, SEG), F32)
        kro_t = sb("kro_t", (128, SEG), F32)
        t1_t = sb("t1_t", (128, SEG), F32)
        t2_t = sb("t2_t", (128, SEG), F32)
        cos_t = sb("cos_t", (128, SEG), F32)
        sin_t = sb("sin_t", (128, SEG), F32)
        xtok = sb("xtok", (128, 2, D), F32)
        xin = sb("xin", (128, 2, D), F32)
        ost = sb("ost", (128, 2, 288), F32)
        hist = sb("hist", (128, L, KC, 2), F32)
        smallp = sb("smallp_sb", (128, L, NSP), F32)
        ident = sb("ident_sb", (128, 128), F32)
        ones = sb("ones_sb", (128, 128), BF16)
        ps = [es.enter_context(nc.psum_tensor(f"ps{i}", [128, 512], F32)) for i in range(8)]

        bcT = lambda m: big[:, m, :]
        attnT_i, zT_i = 8, 16
        qlat_i, ckvs_i, cnew_i = 0, 3, 5

        rr = {"mm": 0, "aux": 0, "acc": 0}
        MM_BANKS, AUX_BANKS, ACC_BANKS, SS_BANK = (0, 1, 2), (4, 7), (5, 6), 3

        def bank(cls):
            lst = {"mm": MM_BANKS, "aux": AUX_BANKS, "acc": ACC_BANKS}[cls]
            b = lst[rr[cls] % len(lst)]
            rr[cls] += 1
            return b

        def act(out, in_, func, reads, writes, scale=None, bias=None):
            kw = {}
            if scale is not None:
                kw["scale"] = scale
            if bias is not None:
                kw["bias"] = bias
            return R.op("act", lambda e: e.activation(out=out, in_=in_, func=func, **kw), reads, writes)

        def tt(out, in0, in1, op, reads, writes):
            return R.op("dve", lambda e: e.tensor_tensor(out=out, in0=in0, in1=in1, op=op), reads, writes)

        def stt(out, in0, scalar, in1, op0, op1, reads, writes):
            return R.op("dve", lambda e: e.scalar_tensor_tensor(out=out, in0=in0, scalar=scalar, in1=in1,
                                                               op0=op0, op1=op1), reads, writes)

        def dcopy(out, in_, reads, writes):
            return R.op("dve", lambda e: e.tensor_copy(out=out, in_=in_), reads, writes)

        def acopy(out, in_, reads, writes, scale=None):
            if scale is None:
                return R.op("act", lambda e: e.copy(out=out, in_=in_), reads, writes)
            return R.op("act", lambda e: e.mul(out=out, in_=in_, mul=scale), reads, writes)

        def mm(out, lhsT, rhs):
            return lambda e, st, sp: e.matmul(out, lhsT, rhs, start=st, stop=sp)

        def tr(out, in_, idn):
            return lambda e, st, sp: e.transpose(out, in_, idn)

        out_tickets = []

        def chk(name):
            if stop == name:
                raise _Stop()

        class WS:
            def __init__(self):
                self.sched = []
                self.emitted = 0

            def advance(self, upto):
                upto = min(upto, len(self.sched) - 1)
                while self.emitted <= upto:
                    i = self.emitted
                    (l, name, kc0, nkc, c0, ncols) = self.sched[i]
                    slot = i % NSLOT
                    sl_idx, t = divmod(i, NT_SL)
                    slot_flat = wbuf[:, slot, :, :].rearrange("p k c -> p (k c)")
                    if sl_idx < n_layers:
                        assert sl_idx == l
                        src = wv[name][l, :, kc0:kc0 + nkc, c0:c0 + ncols]
                        dst = wbuf[:, slot, 0:nkc, 0:ncols]
                        R.dma("pool", lambda e, dst=dst, src=src: e.dma_start(out=dst, in_=src), f"W{slot}",
                              reads=(), writes=(("w", slot),))
                        R.dma("sp", lambda e, l=l, t=t, slot_flat=slot_flat: e.dma_start(out=wscr[l, t, :, :],
                                                                                        in_=slot_flat),
                              f"S_wst{l}", reads=(("w", slot),))
                    else:
                        fin = ("S_wst%d" % l, R.cnt["S_wst%d" % l])
                        assert fin[1] == 16 * NT_SL, fin
                        R.res[("wscr", l)] = [fin, {}]
                        R.dma("pool", lambda e, l=l, t=t, slot_flat=slot_flat: e.dma_start(out=slot_flat,
                                                                                          in_=wscr[l, t, :, :]),
                              f"W{slot}", reads=(("wscr", l),), writes=(("w", slot),))
                    self.emitted += 1

            def use(self, i, desc):
                assert self.sched[i] == desc, (i, self.sched[i], desc)
                assert i < self.emitted, (i, self.emitted)
                return i % NSLOT

        ws = WS()
        wpos = [0]

        def wtile(desc):
            i = wpos[0]
            wpos[0] += 1
            slot = ws.use(i, desc)
            return slot, ("w", slot)

        def wdone():
            ws.advance(wpos[0] - 1 + NSLOT)

        segs = []
        for s in range(n_prompt_seq):
            for q in range(n_quarters):
                segs.append(dict(kind="p", seq=s, q=q, N=SEG, pos0=q * SEG, key0=q * SEG))
        if with_sample:
            segs.append(dict(kind="s", seq=0, q=4, N=DEC_SEQ, pos0=PAST, key0=PAST))

        def sl_sched(l):
            out = []
            for m in range(KC):
                out.append((l, "w_in", 0, KC, O1 + 128 * m, 128))
                out.append((l, "w_in", 0, KC, O2 + 128 * m, 128))
                out.append((l, "w_in", 0, KC, 128 * m, 128))
            for i in range(3):
                out.append((l, "w_in", 0, KC, O3 + 128 * i, 128))
            for i in range(2):
                out.append((l, "w_in", 0, KC, O4 + 128 * i, 128))
            out.append((l, "w_in", 0, KC, O5, 32))
            for m in range(KC):
                out.append((l, "w_conv_out", 0, KC, 128 * m, 128))
                out.append((l, "w_in", 0, KC, O6 + 128 * m, 128))
                out.append((l, "w_attn_out", 0, KC, 128 * m, 128))
                out.append((l, "w_in", 0, KC, O7 + 128 * m, 128))
            for m in range(KC):
                out.append((l, "w_merge", 0, KC, 128 * m, 128))
            for f in range(FC):
                out.append((l, "w_gate_up", 0, KC, 128 * f, 128))
                out.append((l, "w_gate_up", 0, KC, DFF + 128 * f, 128))
            for m in range(KC):
                out.append((l, "w_down", 0, 8, 128 * m, 128))
                out.append((l, "w_down", 8, 8, 128 * m, 128))
                out.append((l, "w_down", 16, 6, 128 * m, 128))
            return out

        for sg in segs:
            for l in range(n_layers):
                ws.sched.extend(sl_sched(l))
        assert len(sl_sched(0)) == NT_SL

        R.dma("sp", lambda e: e.dma_start(out=smallp[:, :, :].rearrange("p l c -> p (l c)"), in_=smallp_d[:, :]),
              "S_small", writes=("smallp",))
        R.dma("sp", lambda e: e.dma_start(out=ident[:, :], in_=ident_d[:, :]), "S_ident", writes=("ident",))
        R.op("dve", lambda e: e.memset(ones[:, :], 1.0), writes=("ones",))
        R.op("dve", lambda e: e.memset(VB[:, 0, :, 64:128], 1.0), writes=(("V", 0),))
        R.op("dve", lambda e: e.memset(VB[:, 1, :, 0:64], 1.0), writes=(("V", 1),))
        R.op("dve", lambda e: e.memset(KT[:, :, :], 0.0), writes=(("KT", 0), ("KT", 1)))
        R.op("dve", lambda e: e.memset(QT[:, :, :], 0.0), writes=(("QT", 0), ("QT", 1)))
        ws.advance(NSLOT - 1)

        def sp_col(l, c0, n=1):
            return smallp[:, l, c0:c0 + n]

        def load_wuq_wukv(l):
            for kc in range(3):
                R.dma("pool", lambda e, kc=kc: e.dma_start(out=wuq[:, kc, :], in_=wv["w_uq"][l, :, kc, :]),
                      "S_wuq", writes=("wuq",) if kc == 0 else ())
            R.res["wuq"] = [("S_wuq", R.cnt["S_wuq"]), {}]
            first = True
            for kc in range(2):
                for hf in range(2):
                    R.dma("pool", lambda e, kc=kc, hf=hf: e.dma_start(
                        out=wukv[:, kc, hf * 1024:(hf + 1) * 1024], in_=wv["w_ukv"][l, :, kc, hf * 1024:(hf + 1) * 1024]),
                        "S_wukv", writes=("wukv",) if first else ())
                    first = False
            R.res["wukv"] = [("S_wukv", R.cnt["S_wukv"]), {}]

        def prep_wuqAB():
            wv4 = wuq[:, :, :].rearrange("p k (h d) -> p k h d", d=QH)
            acopy(wuqB[:, :, :, 0:16], wv4[:, :, :, 80:96], reads=("wuq",), writes=("wuqB",), scale=-1.0)
            acopy(wuqB[:, :, :, 16:32], wv4[:, :, :, 64:80], reads=("wuq",), writes=("wuqB",))
            acopy(wuqA[:, :, :, :], wv4[:, :, :, 64:96], reads=("wuq",), writes=("wuqA",))

        def norm_finish(ssb, nfeat, N):
            act(rt_t[:, :N], ps[ssb][:, :N], AF.Ln, reads=(("ps", ssb),), writes=("rt",), scale=1.0 / nfeat,
                bias=EPS)
            act(rstd_t[:, :N], rt_t[:, :N], AF.Exp, reads=("rt",), writes=("rstd",), scale=-0.5)

        def prenorm(l, gcol, N):
            ssb = SS_BANK
            for kc in range(KC):
                b = kc % 2
                act(sq_t[:, b, :N], xresT[:, kc, :N], AF.Square, reads=(("xres", kc),), writes=(("sq", b),))
                R.mm1(("ps", ssb), (lambda e, b=b, kc=kc: e.matmul(ps[ssb][:, :N], ones[:, :], sq_t[:, b, :N],
                                                                 start=(kc == 0), stop=(kc == KC - 1))),
                      reads=(("sq", b), "ones"), first=(kc == 0))
            norm_finish(ssb, D, N)
            for kc in range(KC):
                stt(xnT[:, kc, :N], xresT[:, kc, :N], sp_col(l, gcol + kc), rstd_t[:, :N], ALU.mult, ALU.mult,
                    reads=(("xres", kc), "rstd", "smallp"), writes=(("xn", kc),))

        def dense_group(l, name, c0, ncols, rhs_fn, rhs_keys, nkc_total=KC, pieces=None):
            b = bank("mm")
            mms = []
            pieces = pieces or [(0, nkc_total)]
            for (k0, nk) in pieces:
                slot, wkey = wtile((l, name, k0, nk, c0, ncols))
                for k in range(nk):
                    kc = k0 + k
                    mms.append((mm(ps[b][0:ncols, :rhs_fn.N], wbuf[:, slot, k, 0:ncols], rhs_fn(kc)),
                                (wkey, rhs_keys(kc))))
            R.mmgroup(("ps", b), mms)
            wdone()
            return b

        class RhsFn:
            def __init__(self, fn, N):
                self.fn = fn
                self.N = N

            def __call__(self, kc):
                return self.fn(kc)

        def issue_input(sg, blk):
            if blk in sg.setdefault("in_issued", set()):
                return
            sg["in_issued"].add(blk)
            N_, pos0_ = sg["N"], sg["pos0"]
            nb = min(128, N_ - blk * 128)
            if sg["kind"] == "p":
                src = x_p[sg["seq"], pos0_ + blk * 128: pos0_ + blk * 128 + nb, :]
            else:
                src = x_s[blk * 128: blk * 128 + nb, :]
            xb = blk % 2
            R.dma("sp", lambda e: e.dma_start(out=xin[0:nb, xb, :], in_=src), f"S_xin{xb}", writes=(("xin", xb),))

        def issue_rope(sg):
            if sg.get("rope_issued"):
                return
            sg["rope_issued"] = True
            N_, pos0_ = sg["N"], sg["pos0"]
            R.dma("sp", lambda e: e.dma_start(out=cos_t[:, :N_], in_=rope_d[0, :, pos0_:pos0_ + N_]), "S_cos",
                  writes=("cos",))
            R.dma("sp", lambda e: e.dma_start(out=sin_t[:, :N_], in_=rope_d[1, :, pos0_:pos0_ + N_]), "S_sin",
                  writes=("sin",))

        def prefetch_next(sg):
            i = segs.index(sg)
            if i + 1 < len(segs):
                nx = segs[i + 1]
                for blk in range(min(2, (nx["N"] + 127) // 128)):
                    issue_input(nx, blk)
                issue_rope(nx)

        def segment_layer(sg, l, first_layer, last_layer, nxt_l=None):
            N = sg["N"]
            kind = sg["kind"]
            key0 = sg["key0"]
            pos0 = sg["pos0"]
            q = sg["q"]
            seq = sg["seq"]
            nblk = (N + 127) // 128
            xn_rhs = RhsFn(lambda kc: xnT[:, kc, :N], N)
            xn_keys = lambda kc: ("xn", kc)

            if first_layer:
                for blk in range(nblk):
                    nb = min(128, N - blk * 128)
                    issue_input(sg, blk)
                    xb = blk % 2
                    for half in range(2):
                        b = bank("mm")
                        for j in range(4):
                            kc = half * 4 + j
                            R.mmgroup(("ps", b) if j == 0 else ("psx", b), [
                                (tr(ps[b][:, j * 128: j * 128 + nb], xin[0:nb, xb, kc * 128:(kc + 1) * 128],
                                    ident[0:nb, 0:nb]), (("xin", xb), "ident"))])
                        R.res[("ps", b)] = [("E_pe", R.cnt["E_pe"]), {}]
                        src_ps = ps[b][:, :].rearrange("p (j t) -> p j t", t=128)[:, :, 0:nb]
                        acopy(xresT[:, half * 4:half * 4 + 4, blk * 128: blk * 128 + nb], src_ps,
                              reads=(("ps", b),), writes=tuple(("xres", half * 4 + j) for j in range(4)))
                issue_rope(sg)

            chk("input")
            if kind == "p" and q == 0:
                R.op("dve", lambda e: e.memset(hist[:, l, :, :], 0.0), writes=(("hist", l),))
            if kind == "s":
                R.dma("sp", lambda e: e.dma_start(
                    out=hist[:, l, :, :].rearrange("p k j -> p (k j)"), in_=hist0[:, l * 16:(l + 1) * 16]),
                    "S_hist", writes=(("hist", l),))
                for blk in range(PAST // 128):
                    ob = blk % 2
                    R.dma("sp", lambda e, blk=blk, ob=ob: e.dma_start(
                        out=ost[:, ob, 0:256], in_=c_ckv[l, blk * 128:(blk + 1) * 128, :]), f"S_ost{ob}",
                        writes=(("ost", ob),))
                    R.dma("sp", lambda e, blk=blk, ob=ob: e.dma_start(
                        out=ost[:, ob, 256:288], in_=c_kr[l, blk * 128:(blk + 1) * 128, :]), f"S_ost{ob}",
                        writes=())
                    R.res[("ost", ob)] = [(f"S_ost{ob}", R.cnt[f"S_ost{ob}"]), {}]
                    b = bank("aux")
                    R.mmgroup(("ps", b), [(tr(ps[b][:, 0:128], ost[:, ob, 0:128], ident[:, :]), (("ost", ob), "ident"))])
                    R.mmgroup(("psx", b), [(tr(ps[b][:, 128:256], ost[:, ob, 128:256], ident[:, :]), (("ost", ob), "ident"))])
                    R.mmgroup(("psx", b), [(tr(ps[b][0:32, 256:384], ost[:, ob, 256:288], ident[:, :]), (("ost", ob), "ident"))])
                    R.res[("ps", b)] = [("E_pe", R.cnt["E_pe"]), {}]
                    jt = blk // 4
                    acopy(cT[:, l, :, blk * 128:(blk + 1) * 128],
                          ps[b][:, 0:256].rearrange("p (k t) -> p k t", t=128),
                          reads=(("ps", b),), writes=(("cT", l, jt),))
                    acopy(krT[64:96, l, blk * 128:(blk + 1) * 128], ps[b][0:32, 256:384],
                          reads=(("ps", b),), writes=(("krT", l, jt),))

            chk("hist")
            if sg.get("first_sl") and l == 0:
                load_wuq_wukv(l)
                prep_wuqAB()

            chk("wuq")
            prenorm(l, C_GPRE, N)

            chk("prenorm")
            for m in range(KC):
                b_cg = dense_group(l, "w_in", O1 + 128 * m, 128, xn_rhs, xn_keys)
                b_xi = dense_group(l, "w_in", O2 + 128 * m, 128, xn_rhs, xn_keys)
                acopy(cg_t[:, :N], ps[b_cg][:, :N], reads=(("ps", b_cg),), writes=("cg",))
                dcopy(u_t[:, 0:2], hist[:, l, m, :], reads=(("hist", l),), writes=("u",))
                tt(u_t[:, 2:2 + N], ps[b_xi][:, :N], cg_t[:, :N], ALU.mult, reads=(("ps", b_xi), "cg", "u"),
                   writes=("u",))
                cws = [sp_col(l, C_CW + 3 * m + j) for j in range(3)]
                cw = lambda j, cws=cws: cws[j]
                R.op("dve", lambda e, c2=cws[2]: e.tensor_scalar(out=cv_t[:, :N], in0=u_t[:, 2:2 + N], scalar1=c2,
                                                                 scalar2=None, op0=ALU.mult),
                     reads=("u", "smallp"), writes=("cv",))
                stt(cv_t[:, :N], u_t[:, 1:1 + N], cw(1), cv_t[:, :N], ALU.mult, ALU.add, reads=("u", "cv", "smallp"),
                    writes=("cv",))
                stt(cv_t[:, :N], u_t[:, 0:N], cw(0), cv_t[:, :N], ALU.mult, ALU.add, reads=("u", "cv", "smallp"),
                    writes=("cv",))
                dcopy(hist[:, l, m, :], u_t[:, N:N + 2], reads=("u",), writes=(("hist", l),))
                b_bg = dense_group(l, "w_in", 128 * m, 128, xn_rhs, xn_keys)
                tt(big[:, m, :N], ps[b_bg][:, :N], cv_t[:, :N], ALU.mult, reads=(("ps", b_bg), "cv"),
                   writes=(("big", m),))
            if (kind == "p" and q == n_quarters - 1) or kind == "s":
                if kind == "p":
                    dst = nconv_p[l, seq, :, :].rearrange("j (k p) -> p k j", p=128)
                else:
                    dst = nconv_s[l, :, :].rearrange("j (k p) -> p k j", p=128)
                for kc in range(KC):
                    out_tickets.append(R.dma("sp", lambda e, dst=dst, kc=kc: e.dma_start(
                        out=dst[:, kc, :], in_=hist[:, l, kc, :], allow_slow_non_contiguous=True), f"S_histout{l}",
                        reads=(("hist", l),)))

            chk("conv")
            for i in range(3):
                bq = dense_group(l, "w_in", O3 + 128 * i, 128, xn_rhs, xn_keys)
                acopy(scr[:, qlat_i + i, :N], ps[bq][:, :N], reads=(("ps", bq),), writes=(("scr", qlat_i + i),))
            for i in range(2):
                bq = dense_group(l, "w_in", O4 + 128 * i, 128, xn_rhs, xn_keys)
                acopy(scr[:, ckvs_i + i, :N], ps[bq][:, :N], reads=(("ps", bq),), writes=(("scr", ckvs_i + i),))
            jt_new = q
            slot, wkey = wtile((l, "w_in", 0, KC, O5, 32))
            acopy(wkrB[:, :, 0:16], wbuf[:, slot, :, 16:32], reads=(wkey,), writes=("wkrB",), scale=-1.0)
            acopy(wkrB[:, :, 16:32], wbuf[:, slot, :, 0:16], reads=(wkey,), writes=("wkrB",))
            bA = bank("aux")
            R.mmgroup(("ps", bA), [(mm(ps[bA][0:32, :N], wbuf[:, slot, kc, 0:32], xnT[:, kc, :N]),
                                    (wkey, ("xn", kc))) for kc in range(KC)])
            wdone()
            bB = bank("aux")
            R.mmgroup(("ps", bB), [(mm(ps[bB][0:32, :N], wkrB[:, kc, :], xnT[:, kc, :N]), ("wkrB", ("xn", kc)))
                                   for kc in range(KC)])
            tt(t1_t[0:32, :N], ps[bA][0:32, :N], cos_t[0:32, :N], ALU.mult, reads=(("ps", bA), "cos"), writes=("t1",))
            tt(t2_t[0:32, :N], ps[bB][0:32, :N], sin_t[0:32, :N], ALU.mult, reads=(("ps", bB), "sin"), writes=("t2",))
            tt(kro_t[0:32, :N], t1_t[0:32, :N], t2_t[0:32, :N], ALU.add, reads=("t1", "t2"), writes=("kro",))
            acopy(krT[64:96, l, key0:key0 + N], kro_t[0:32, :N], reads=("kro",), writes=(("krT", l, jt_new),))
            for i in range(3):
                b = i % 2
                act(sq_t[:, b, :N], scr[:, qlat_i + i, :N], AF.Square, reads=(("scr", qlat_i + i),),
                    writes=(("sq", b),))
                R.mm1(("ps", SS_BANK), (lambda e, b=b, i=i: e.matmul(ps[SS_BANK][:, :N], ones[:, :], sq_t[:, b, :N],
                                                                    start=(i == 0), stop=(i == 2))),
                      reads=(("sq", b), "ones"), first=(i == 0))
            norm_finish(SS_BANK, Q_LORA, N)
            for i in range(3):
                stt(qnT[:, i, :N], scr[:, qlat_i + i, :N], sp_col(l, C_GQ + i), rstd_t[:, :N], ALU.mult, ALU.mult,
                    reads=(("scr", qlat_i + i), "rstd", "smallp"), writes=(("qn", i),))
            for i in range(2):
                b = i % 2
                act(sq_t[:, b, :N], scr[:, ckvs_i + i, :N], AF.Square, reads=(("scr", ckvs_i + i),),
                    writes=(("sq", b),))
                R.mm1(("ps", SS_BANK), (lambda e, b=b, i=i: e.matmul(ps[SS_BANK][:, :N], ones[:, :], sq_t[:, b, :N],
                                                                    start=(i == 0), stop=(i == 1))),
                      reads=(("sq", b), "ones"), first=(i == 0))
            norm_finish(SS_BANK, KV_LORA, N)
            for i in range(2):
                stt(scr[:, cnew_i + i, :N], scr[:, ckvs_i + i, :N], sp_col(l, C_GKV + i), rstd_t[:, :N], ALU.mult,
                    ALU.mult, reads=(("scr", ckvs_i + i), "rstd", "smallp"), writes=(("scr", cnew_i + i),))
                acopy(cT[:, l, i, key0:key0 + N], scr[:, cnew_i + i, :N], reads=(("scr", cnew_i + i),),
                      writes=(("cT", l, jt_new),))
            chk("lowrank")
            for blk in range(nblk):
                nb = min(128, N - blk * 128)
                ob = blk % 2
                b = bank("aux")
                R.mmgroup(("ps", b), [(tr(ps[b][0:nb, 0:128], scr[:, cnew_i, blk * 128:blk * 128 + nb], ident[:, :]),
                                       (("scr", cnew_i), "ident"))])
                R.mmgroup(("psx", b), [(tr(ps[b][0:nb, 128:256], scr[:, cnew_i + 1, blk * 128:blk * 128 + nb],
                                          ident[:, :]), (("scr", cnew_i + 1), "ident"))])
                R.mmgroup(("psx", b), [(tr(ps[b][0:nb, 256:288], kro_t[0:32, blk * 128:blk * 128 + nb],
                                          ident[0:32, 0:32]), ("kro", "ident"))])
                R.res[("ps", b)] = [("E_pe", R.cnt["E_pe"]), {}]
                dcopy(ost[0:nb, ob, :], ps[b][0:nb, 0:288], reads=(("ps", b),), writes=(("ost", ob),))
                if kind == "p":
                    r0 = pos0 + blk * 128
                    d1 = nckv_p[l, seq, r0:r0 + nb, :]
                    d2 = nkr_p[l, seq, r0:r0 + nb, :]
                else:
                    d1 = nckv_s[l, 0:nb, :]
                    d2 = nkr_s[l, 0:nb, :]
                out_tickets.append(R.dma("sp", lambda e, d1=d1, ob=ob, nb=nb: e.dma_start(
                    out=d1, in_=ost[0:nb, ob, 0:256]), f"S_ost{ob}", reads=(("ost", ob),)))
                out_tickets.append(R.dma("sp", lambda e, d2=d2, ob=ob, nb=nb: e.dma_start(
                    out=d2, in_=ost[0:nb, ob, 256:288]), f"S_ost{ob}", reads=(("ost", ob),)))

            chk("ctxout")
            nkeys = key0 + N
            ktiles = [(j * 512, min(512, nkeys - j * 512)) for j in range((nkeys + 511) // 512)]
            kblocks = [(j * 128, min(128, nkeys - j * 128)) for j in range((nkeys + 127) // 128)]
            for hb in range(2):
                for jt, (k0, nk) in enumerate(ktiles):
                    dcopy(KT[64:96, hb, k0:k0 + nk], krT[64:96, l, k0:k0 + nk], reads=(("krT", l, jt),),
                          writes=(("KT", hb),))
            b4 = SS_BANK

            def prep_pieces(h):
                hb = h % 2
                eo = h % 2
                pieces = []
                ev = dcopy
                ev_kt = acopy if kind == "s" else dcopy
                for jt, (k0, nk) in enumerate(ktiles):
                    def p_kt(jt=jt, k0=k0, nk=nk):
                        b = bank("aux")
                        R.mmgroup(("ps", b), [(mm(ps[b][0:64, :nk], wukv[:, kc, h * 128:h * 128 + 64],
                                                  cT[:, l, kc, k0:k0 + nk]), ("wukv", ("cT", l, jt)))
                                              for kc in range(2)])
                        ev_kt(KT[0:64, hb, k0:k0 + nk], ps[b][0:64, :nk], reads=(("ps", b),), writes=(("KT", hb),))
                    pieces.append(p_kt)
                for g0 in range(0, len(kblocks), 8):
                    def p_v(g0=g0):
                        grp = kblocks[g0:g0 + 8]
                        b = bank("aux")
                        for j, (k0, nk) in enumerate(grp):
                            R.mmgroup(("ps", b) if j == 0 else ("psx", b), [
                                (mm(ps[b][0:nk, j * 64:(j + 1) * 64], cT[:, l, kc, k0:k0 + nk],
                                    wukv[:, kc, h * 128 + 64:h * 128 + 128]), ("wukv", ("cT", l, k0 // 512)))
                                for kc in range(2)])
                        R.res[("ps", b)] = [("E_pe", R.cnt["E_pe"]), {}]
                        vc0 = 0 if eo == 0 else 64
                        full = [g for g in grp if g[1] == 128]
                        if full:
                            nf = len(full)
                            ev(VB[:, eo, g0:g0 + nf, vc0:vc0 + 64],
                                  ps[b][:, 0:nf * 64].rearrange("p (j v) -> p j v", v=64),
                                  reads=(("ps", b),), writes=(("V", eo),))
                        if len(full) < len(grp):
                            j = len(full)
                            nk = grp[j][1]
                            ev(VB[0:nk, eo, g0 + j, vc0:vc0 + 64], ps[b][0:nk, j * 64:(j + 1) * 64],
                                  reads=(("ps", b),), writes=(("V", eo),))
                    pieces.append(p_v)

                def p_rot4():
                    g = h // 4
                    bA4 = bank("aux")
                    R.mmgroup(("ps", bA4), [(mm(ps[bA4][:, :N], wuqA[:, kc, 4 * g:4 * g + 4, :], qnT[:, kc, :N]),
                                             ("wuqA", ("qn", kc))) for kc in range(3)])
                    R.mmgroup(("ps", b4), [(mm(ps[b4][:, :N], wuqB[:, kc, 4 * g:4 * g + 4, :], qnT[:, kc, :N]),
                                            ("wuqB", ("qn", kc))) for kc in range(3)])
                    tt(t1_t[:, :N], ps[bA4][:, :N], cos_t[:, :N], ALU.mult, reads=(("ps", bA4), "cos"), writes=("t1",))
                    tt(t2_t[:, :N], ps[b4][:, :N], sin_t[:, :N], ALU.mult, reads=(("ps", b4), "sin"), writes=("t2",))
                    tt(kro_t[:, :N], t1_t[:, :N], t2_t[:, :N], ALU.add, reads=("t1", "t2"), writes=("kro",))

                def p_q():
                    bA = bank("aux")
                    R.mmgroup(("ps", bA), [(mm(ps[bA][0:64, :N], wuq[:, kc, h * QH:h * QH + 64], qnT[:, kc, :N]),
                                            ("wuq", ("qn", kc))) for kc in range(3)])
                    ev(QT[0:64, hb, :N], ps[bA][0:64, :N], reads=(("ps", bA),), writes=(("QT", hb),))
                    j4 = h % 4
                    R.op("pool", lambda e: e.tensor_copy(out=QT[64:96, hb, :N], in_=kro_t[32 * j4:32 * j4 + 32, :N]),
                         reads=("kro",), writes=(("QT", hb),))
                    if debug and h == 0 and l == 0 and sg is segs[0]:
                        R.dma("pool", lambda e: e.dma_start(out=dbg["d_qt"][:, :], in_=QT[:, 0, :]), "S_dbg",
                              reads=(("QT", 0),))
                        R.dma("pool", lambda e: e.dma_start(out=dbg["d_kt"][:, :], in_=KT[:, 0, 0:SEG]), "S_dbg",
                              reads=(("KT", 0),))
                if h % 4 == 0:
                    pieces.insert(0, p_rot4)
                    pieces.insert(1, p_q)
                else:
                    pieces.insert(0, p_q)
                return pieces

            def head_loop(h, pieces, deferred):
                hb = h % 2
                eo = h % 2
                accb = bank("acc")
                nkb = len(kblocks)
                info = []
                for kb, (k0, nk) in enumerate(kblocks):
                    if kind == "p" and k0 >= key0:
                        bd = (k0 - key0) // 128
                        info.append((k0, nk, bd, 128 * bd))
                    else:
                        info.append((k0, nk, None, 0))
                sbanks = {}

                def issue_s(kb):
                    k0, nk, bd, qlo = info[kb]
                    sbk = bank("mm")
                    R.mmgroup(("ps", sbk), [(mm(ps[sbk][0:nk, qlo:N], KT[0:QH, hb, k0:k0 + nk], QT[0:QH, hb, qlo:N]),
                                             (("KT", hb), ("QT", hb)))])
                    sbanks[kb] = sbk

                for kb in range(min(2, nkb)):
                    issue_s(kb)
                for kb in range(nkb):
                    k0, nk, bd, qlo = info[kb]
                    sbk = sbanks[kb]
                    pb = kb % 3
                    act(PT[0:nk, pb, qlo:N], ps[sbk][0:nk, qlo:N], AF.Exp, reads=(("ps", sbk),),
                        writes=(("PT", pb),), scale=ATTN_SCALE)
                    def pv(c0, c1, r1, first, last, kb=kb, pb=pb):
                        o_ap, l_ap, r_ap = ps[accb][:, c0:c1], VB[0:r1, eo, kb, :], PT[0:r1, pb, c0:c1]
                        R.mm1(("ps", accb), (lambda e: e.matmul(o_ap, l_ap, r_ap, start=first, stop=last)),
                              reads=(("V", eo), ("PT", pb)), first=first)
                    if kb + 2 < nkb:
                        issue_s(kb + 2)
                    if kb == 1 and deferred is not None:
                        deferred()
                        deferred = None
                    if pieces:
                        pieces.pop(0)()
                    if bd is None:
                        pv(qlo, N, nk, kb == 0, kb == nkb - 1)
                    else:
                        pv(qlo + 64, N, nk, kb == 0, False)
                        pv(qlo, qlo + 64, 64, False, kb == nkb - 1)
                while pieces:
                    pieces.pop(0)()
                if deferred is not None:
                    deferred()
                dlo, slo = (0, 64) if eo == 0 else (64, 0)

                def norm():
                    act(Rr[dlo:dlo + 64, :N], ps[accb][slo:slo + 64, :N], AF.Ln, reads=(("ps", accb),), writes=("R",))
                    act(Rr[dlo:dlo + 64, :N], Rr[dlo:dlo + 64, :N], AF.Exp, reads=("R",), writes=("R",), scale=-1.0)
                    tt(big[dlo:dlo + 64, attnT_i + h // 2, :N], ps[accb][dlo:dlo + 64, :N], Rr[dlo:dlo + 64, :N],
                       ALU.mult, reads=(("ps", accb), "R"), writes=(("big", attnT_i + h // 2),))
                return norm

            for p in prep_pieces(0):
                p()
            pending = None
            for h in range(H):
                nxt = prep_pieces(h + 1) if h + 1 < H else []
                pending = head_loop(h, nxt, pending)
            pending()

            chk("attn")
            if nxt_l is not None:
                load_wuq_wukv(nxt_l)
            bc_rhs = RhsFn(lambda kc: big[:, kc, :N], N)
            bc_keys = lambda kc: ("big", kc)
            at_rhs = RhsFn(lambda kc: big[:, attnT_i + kc, :N], N)
            at_keys = lambda kc: ("big", attnT_i + kc)
            for m in range(KC):
                b_ya = dense_group(l, "w_conv_out", 128 * m, 128, bc_rhs, bc_keys)
                b_ga = dense_group(l, "w_in", O6 + 128 * m, 128, xn_rhs, xn_keys)
                act(sa_t[:, :N], ps[b_ga][:, :N], AF.Sigmoid, reads=(("ps", b_ga),), writes=("sa",))
                tt(sa_t[:, :N], ps[b_ya][:, :N], sa_t[:, :N], ALU.mult, reads=(("ps", b_ya), "sa"), writes=("sa",))
                b_yb = dense_group(l, "w_attn_out", 128 * m, 128, at_rhs, at_keys)
                b_gb = dense_group(l, "w_in", O7 + 128 * m, 128, xn_rhs, xn_keys)
                act(sb_t[:, :N], ps[b_gb][:, :N], AF.Sigmoid, reads=(("ps", b_gb),), writes=("sb",))
                tt(sb_t[:, :N], ps[b_yb][:, :N], sb_t[:, :N], ALU.mult, reads=(("ps", b_yb), "sb"), writes=("sb",))
                tt(big[:, zT_i + m, :N], sa_t[:, :N], sb_t[:, :N], ALU.add, reads=("sa", "sb"),
                   writes=(("big", zT_i + m),))

            chk("z")
            def postnorm_residual(groups_fn, gcol):
                def ss_mm(m):
                    b = m % 2
                    R.mm1(("ps", SS_BANK), (lambda e, b=b, m=m: e.matmul(ps[SS_BANK][:, :N], ones[:, :],
                                                                        sq_t[:, b, :N], start=(m == 0),
                                                                        stop=(m == KC - 1))),
                          reads=(("sq", b), "ones"), first=(m == 0))
                for m in range(KC):
                    bm = groups_fn(m)
                    if m > 0:
                        ss_mm(m - 1)
                    b = m % 2
                    dcopy(scr[:, m, :N], ps[bm][:, :N], reads=(("ps", bm),), writes=(("scr", m),))
                    act(sq_t[:, b, :N], scr[:, m, :N], AF.Square, reads=(("scr", m),), writes=(("sq", b),))
                ss_mm(KC - 1)
                chk("pn_a")
                norm_finish(SS_BANK, D, N)
                chk("pn_b")
                for m in range(KC):
                    stt(scr[:, m, :N], scr[:, m, :N], sp_col(l, gcol + m), rstd_t[:, :N], ALU.mult, ALU.mult,
                        reads=(("scr", m), "rstd", "smallp"), writes=(("scr", m),))
                    tt(xresT[:, m, :N], xresT[:, m, :N], scr[:, m, :N], ALU.add, reads=(("xres", m), ("scr", m)),
                       writes=(("xres", m),))

            z_rhs = RhsFn(lambda kc: big[:, zT_i + kc, :N], N)
            z_keys = lambda kc: ("big", zT_i + kc)
            postnorm_residual(lambda m: dense_group(l, "w_merge", 128 * m, 128, z_rhs, z_keys), C_GPOST)
            if debug and l == 0 and sg is segs[0]:
                def dump(nm, t_, c0, n_, keyf, eng="pool"):
                    for k in range(n_):
                        R.dma(eng, lambda e, k=k: e.dma_start(out=dbg[nm][:, k * SEG:(k + 1) * SEG], in_=t_[:, c0 + k, :]),
                              "S_dbg", reads=(keyf(c0 + k),))
                dump("d_xn", xnT, 0, KC, lambda k: ("xn", k))
                dump("d_bc", big, 0, 8, lambda k: ("big", k))
                dump("d_qn", qnT, 0, 3, lambda k: ("qn", k))
                dump("d_attn", big, 8, 8, lambda k: ("big", k))
                dump("d_z", big, 16, 8, lambda k: ("big", k))
                dump("d_xres", xresT, 0, KC, lambda k: ("xres", k), "sp")
            chk("merge")
            if last_layer:
                prefetch_next(sg)
            prenorm(l, C_FPRE, N)
            for f in range(FC):
                b_g = dense_group(l, "w_gate_up", 128 * f, 128, xn_rhs, xn_keys)
                b_u = dense_group(l, "w_gate_up", DFF + 128 * f, 128, xn_rhs, xn_keys)
                sgb = f % 2
                act(sg_t[:, sgb, :N], ps[b_g][:, :N], AF.Silu, reads=(("ps", b_g),), writes=(("sg", sgb),))
                tt(big[:, f, :N], ps[b_u][:, :N], sg_t[:, sgb, :N], ALU.mult, reads=(("ps", b_u), ("sg", sgb)),
                   writes=(("big", f),))
            chk("ffn")
            if nxt_l is not None:
                prep_wuqAB()
            h_rhs = RhsFn(lambda kc: big[:, kc, :N], N)
            h_keys = lambda kc: ("big", kc)
            postnorm_residual(lambda m: dense_group(l, "w_down", 128 * m, 128, h_rhs, h_keys,
                                                    pieces=[(0, 8), (8, 8), (16, 6)]), C_FPOST)

            chk("down")
            if debug and l == 0 and sg is segs[0]:
                for k in range(KC):
                    R.dma("sp", lambda e, k=k: e.dma_start(out=dbg["d_xres2"][:, k * SEG:(k + 1) * SEG], in_=xresT[:, k, :]),
                          "S_dbg", reads=(("xres", k),))
            if last_layer:
                for blk in range(nblk):
                    nb = min(128, N - blk * 128)
                    for half in range(2):
                        b = bank("mm")
                        for j in range(4):
                            kc = half * 4 + j
                            R.mmgroup(("ps", b) if j == 0 else ("psx", b), [
                                (tr(ps[b][0:nb, j * 128:(j + 1) * 128], xresT[:, kc, blk * 128:blk * 128 + nb],
                                    ident[:, :]), (("xres", kc), "ident"))])
                        R.res[("ps", b)] = [("E_pe", R.cnt["E_pe"]), {}]
                        acopy(xtok[0:nb, blk % 2, half * 512:(half + 1) * 512], ps[b][0:nb, :], reads=(("ps", b),),
                              writes=(("xtok", blk % 2),))
                    if kind == "p":
                        r0 = pos0 + blk * 128
                        dst = y_p[seq, r0:r0 + nb, :]
                    else:
                        dst = y_s[blk * 128:blk * 128 + nb, :]
                    out_tickets.append(R.dma("sp", lambda e, dst=dst, nb=nb, ob=blk % 2: e.dma_start(
                        out=dst, in_=xtok[0:nb, ob, :]), f"S_xtok{blk % 2}", reads=(("xtok", blk % 2),)))

        try:
            sls = [(sg, l) for sg in segs for l in range(n_layers)]
            segs[0]["first_sl"] = True
            for i, (sg, l) in enumerate(sls):
                nxt_l = sls[i + 1][1] if i + 1 < len(sls) else None
                segment_layer(sg, l, first_layer=(l == 0), last_layer=(l == n_layers - 1), nxt_l=nxt_l)
            assert wpos[0] == len(ws.sched), (wpos[0], len(ws.sched))
        except _Stop:
            pass
        for sk in list(R.cnt.keys()):
            if not sk.startswith("E_"):
                R.streams["sp"].append(("w", sk, R.cnt[sk]))

        fin = {}
        for (sk, v) in out_tickets:
            fin[sk] = max(fin.get(sk, 0), v)
        if "S_dbg" in R.cnt:
            fin["S_dbg"] = R.cnt["S_dbg"]
        for sk, v in fin.items():
            R.streams["sp"].append(("w", sk, v))

        semh = {}
        for sk in sorted(R.cnt.keys()):
            semh[sk] = es.enter_context(nc.semaphore(sk))
        with nc.Block() as block:
            @block.tensor
            def _(e):
                _replay(e, R.streams["pe"], semh)

            @block.scalar
            def _(e):
                _replay(e, R.streams["act"], semh)

            @block.vector
            def _(e):
                _replay(e, R.streams["dve"], semh)

            @block.gpsimd
            def _(e):
                _replay(e, R.streams["pool"], semh)

            @block.sync
            def _(e):
                _replay(e, R.streams["sp"], semh)
    stats = {k: len(v) for k, v in R.streams.items()}
    return nc, stats


def _host_constants():
    half = QK_ROPE // 2
    inv = ROPE_THETA ** (-np.arange(half, dtype=np.float32) / np.float32(half))
    pos = np.arange(NKEYMAX, dtype=np.float32)
    ang = pos[None, :] * inv[:, None].astype(np.float32)
    cos = np.cos(ang).astype(np.float32)
    sin = np.sin(ang).astype(np.float32)
    idx = np.arange(128) % half
    rope = np.stack([cos[idx], sin[idx]], axis=0).astype(np.float32)
    ident = np.eye(128, dtype=np.float32)
    return rope, ident


def _feat_major(v, nchunk):
    return np.ascontiguousarray(v.reshape(nchunk, 128).T)


def make_in_maps(inputs):
    f = lambda k: np.ascontiguousarray(np.asarray(inputs[k], dtype=np.float32))
    x_prompt, x_sample = f("x_prompt"), f("x_sample")
    state_conv, cache_ckv, cache_krope = f("state_conv"), f("cache_ckv"), f("cache_krope")
    rope, ident = _host_constants()
    smallp = np.zeros((128, L, NSP), np.float32)
    for l in range(L):
        smallp[:, l, C_GPRE:C_GPRE + 8] = _feat_major(f("norm_attn_pre")[l], 8)
        smallp[:, l, C_GPOST:C_GPOST + 8] = _feat_major(f("norm_attn_post")[l], 8)
        smallp[:, l, C_FPRE:C_FPRE + 8] = _feat_major(f("norm_ffn_pre")[l], 8)
        smallp[:, l, C_FPOST:C_FPOST + 8] = _feat_major(f("norm_ffn_post")[l], 8)
        smallp[:, l, C_GQ:C_GQ + 3] = _feat_major(f("norm_q")[l], 3)
        smallp[:, l, C_GKV:C_GKV + 2] = _feat_major(f("norm_kv")[l], 2)
        cw = f("conv_w")[l]
        for j in range(3):
            smallp[:, l, C_CW + j:C_CW + 24:3] = _feat_major(cw[j], 8)
    shared = dict(smallp=smallp.reshape(128, L * NSP), rope=rope, ident=ident,
                  w_in=f("w_in"), w_uq=f("w_uq"), w_ukv=f("w_ukv"), w_conv_out=f("w_conv_out"),
                  w_attn_out=f("w_attn_out"), w_merge=f("w_merge"), w_gate_up=f("w_gate_up"), w_down=f("w_down"))
    maps = []
    for c in range(NCORES):
        h0 = np.zeros((128, L, KC, 2), np.float32)
        for l in range(L):
            for j in range(2):
                h0[:, l, :, j] = _feat_major(state_conv[l, c, j], 8)
        m = dict(shared)
        m.update(x_p=np.ascontiguousarray(x_prompt[2 * c:2 * c + 2]), x_s=np.ascontiguousarray(x_sample[c]),
                 c_ckv=np.ascontiguousarray(cache_ckv[:, c]), c_kr=np.ascontiguousarray(cache_krope[:, c]),
                 hist0=h0.reshape(128, L * KC * 2))
        maps.append(m)
    return maps


_PROG = None


def kernel(**inputs):
    global _PROG
    if _PROG is None:
        _PROG = build_program()[0]
    in_maps = make_in_maps(inputs)
    res = run_bass_kernel_spmd(_PROG, in_maps, core_ids=list(range(NCORES)))
    rs = res.results
    B = 2 * NCORES
    y_prompt = np.concatenate([r["y_p"] for r in rs], axis=0)
    y_sample = np.stack([r["y_s"] for r in rs], axis=0)
    nconv_p = np.concatenate([r["nconv_p"] for r in rs], axis=1)
    nckv_p = np.concatenate([r["nckv_p"] for r in rs], axis=1)
    nkr_p = np.concatenate([r["nkr_p"] for r in rs], axis=1)
    nconv_s = np.stack([r["nconv_s"] for r in rs], axis=1)
    nckv_s = np.stack([r["nckv_s"] for r in rs], axis=1)
    nkr_s = np.stack([r["nkr_s"] for r in rs], axis=1)
    outs = (y_prompt, y_sample, nconv_p, nckv_p, nkr_p, nconv_s, nckv_s, nkr_s)
    return tuple(np.ascontiguousarray(o, dtype=np.float32) for o in outs)
```
